# Optimizing a Trainium2 kernel written in Bass

```python
import jax, jax.numpy as jnp
from jax import lax
import numpy as np

D_MODEL = 1024
BATCH = 8
SEQ = 2048
DEPTH = 4
DEC_BATCH = 128
DEC_SEQ = 1
PAST_LEN = 16384
PAGE_SIZE = 128

N_MIXERS = 2
EXPAND = 2
D_INNER = EXPAND * D_MODEL
A_EXPAND_RATIO = 128
A_HEADS = D_MODEL // A_EXPAND_RATIO
A_DK = A_EXPAND_RATIO
A_DV = D_INNER // A_HEADS
A_CHUNK = 64
B_CHUNK = 128
B_GROUPS = 8
B_DG = D_INNER // B_GROUPS
PLE_DIM = 256
N_A = (DEPTH + 1) // 2
N_B = DEPTH // 2
EPS = 1e-6

kernel_name = "hgrn2_chunkmlp_hybrid_step"


def rmsnorm(x, g):
    xf = x.astype(jnp.float32)
    y = xf * lax.rsqrt(jnp.mean(xf * xf, axis=-1, keepdims=True) + EPS)
    return (y * g.astype(jnp.float32)).astype(x.dtype)


def layernorm(x, g, b):
    xf = x.astype(jnp.float32)
    mu = jnp.mean(xf, axis=-1, keepdims=True)
    xc = xf - mu
    y = xc * lax.rsqrt(jnp.mean(xc * xc, axis=-1, keepdims=True) + EPS)
    return (y * g.astype(jnp.float32) + b.astype(jnp.float32)).astype(x.dtype)


def hgrn2_chunkwise(q, k, v, logf, s0):
    B, L = q.shape[0], q.shape[1]
    c = min(A_CHUNK, L)
    n = -(-L // c)
    pad = n * c - L

    def prep(a):
        a = jnp.pad(a.astype(jnp.float32), ((0, 0), (0, pad), (0, 0), (0, 0)))
        return a.reshape(B, n, c, a.shape[2], a.shape[3]).swapaxes(0, 1)

    qc, kc, vc, gc = prep(q), prep(k), prep(v), prep(logf)
    causal = jnp.tril(jnp.ones((c, c), dtype=bool))[None, :, :, None, None]

    def step(S, inp):
        qb, kb, vb, gb = inp
        b = jnp.cumsum(gb, axis=1)
        o_inter = jnp.einsum('bthk,bhkv->bthv', qb * jnp.exp(b), S)
        diff = jnp.where(causal, b[:, :, None] - b[:, None, :], -jnp.inf)
        att = jnp.einsum('bthk,bshk,btshk->btsh', qb, kb, jnp.exp(diff))
        o = o_inter + jnp.einsum('btsh,bshv->bthv', att, vb)
        b_last = b[:, -1]
        k_dec = kb * jnp.exp(b_last[:, None] - b)
        S = jnp.exp(b_last)[..., None] * S + jnp.einsum('bshk,bshv->bhkv', k_dec, vb)
        return S, o

    S, o = lax.scan(step, s0.astype(jnp.float32), (qc, kc, vc, gc))
    o = o.swapaxes(0, 1).reshape(B, n * c, q.shape[2], v.shape[3])[:, :L]
    return o, S


def hgrn2_mixer(h, s0, w_in, lb, g_norm, w_out):
    B, L, _ = h.shape
    proj = h @ w_in
    q, f, i, z = jnp.split(proj, [D_MODEL, 2 * D_MODEL, 2 * D_MODEL + D_INNER], axis=-1)
    q = jax.nn.silu(q).reshape(B, L, A_HEADS, A_DK)
    lbh = lb.astype(jnp.float32).reshape(A_HEADS, A_DK)
    fpre = f.astype(jnp.float32).reshape(B, L, A_HEADS, A_DK)
    logf = jnp.logaddexp(jnp.log(lbh), jnp.log1p(-lbh) + jax.nn.log_sigmoid(fpre))
    k = -jnp.expm1(logf)
    v = i.reshape(B, L, A_HEADS, A_DV)
    o, S = hgrn2_chunkwise(q, k, v, logf, s0)
    o = rmsnorm(o, g_norm).reshape(B, L, D_INNER).astype(h.dtype)
    return (o * jax.nn.silu(z)) @ w_out, S


def chunk_mlp_mixer(h, w_in, ln_g, ln_b, w_sp, b_sp, w_out):
    B, L, _ = h.shape
    proj = h @ w_in
    u, v, z = jnp.split(proj, 3, axis=-1)
    u = jax.nn.gelu(u)
    v = layernorm(jax.nn.gelu(v), ln_g, ln_b)
    c = B_CHUNK
    n = -(-L // c)
    pad = n * c - L
    vc = jnp.pad(v, ((0, 0), (0, pad), (0, 0))).reshape(B, n, c, B_GROUPS, B_DG)
    w = jnp.where(jnp.tril(jnp.ones((c, c), dtype=bool))[None], w_sp, 0.0)
    s = jnp.einsum('gts,bnsgd->bntgd', w, vc) + b_sp.T[None, None, :, :, None]
    s = s.reshape(B, n * c, D_INNER)[:, :L]
    y = (u * s * jax.nn.silu(z)) @ w_out
    v_tail = v[:, ((L - 1) // c) * c:]
    return y, v_tail


def trunk(x, p, s_hgrn, lbs, norm_mix, w_in_a, gnorm_a, w_out_a, w_in_b, ln_v_g, ln_v_b,
          w_spatial, b_spatial, w_out_b, norm_ple, w_ple_gate, w_ple_proj, norm_final):
    h = x
    new_s, new_v = [], []
    for i in range(DEPTH):
        j = i // N_MIXERS
        hn = rmsnorm(h, norm_mix[i])
        if i % N_MIXERS == 0:
            y, S = hgrn2_mixer(hn, s_hgrn[j], w_in_a[j], lbs[j], gnorm_a[j], w_out_a[j])
            new_s.append(S)
        else:
            y, vt = chunk_mlp_mixer(hn, w_in_b[j], ln_v_g[j], ln_v_b[j], w_spatial[j], b_spatial[j], w_out_b[j])
            new_v.append(vt)
        h = h + y
        gate = jax.nn.sigmoid(rmsnorm(h, norm_ple[i]) @ w_ple_gate[i])
        h = h + gate * (p[i] @ w_ple_proj[i])
    return rmsnorm(h, norm_final), jnp.stack(new_s), jnp.stack(new_v)


def setup_inputs(seed: int = 0) -> dict:
    key = jax.random.key(seed)
    ks = jax.random.split(key, 24)
    nrm = jax.random.normal
    f32 = jnp.float32
    return {
        "x_prompt": nrm(ks[0], (BATCH, SEQ, D_MODEL), f32),
        "x_sample": nrm(ks[1], (DEC_BATCH, DEC_SEQ, D_MODEL), f32),
        "state_hgrn": nrm(ks[2], (N_A, DEC_BATCH, A_HEADS, A_DK, A_DV), f32),
        "p_prompt": nrm(ks[3], (DEPTH, BATCH, SEQ, PLE_DIM), f32),
        "p_sample": nrm(ks[4], (DEPTH, DEC_BATCH, DEC_SEQ, PLE_DIM), f32),
        "norm_mix": 1.0 + 0.05 * nrm(ks[5], (DEPTH, D_MODEL), f32),
        "w_in_a": nrm(ks[6], (N_A, D_MODEL, 3 * D_INNER), f32) * D_MODEL ** -0.5,
        "lb_logits": nrm(ks[7], (N_A, A_HEADS * A_DK), f32),
        "gnorm_a": 1.0 + 0.05 * nrm(ks[8], (N_A, A_DV), f32),
        "w_out_a": nrm(ks[9], (N_A, D_INNER, D_MODEL), f32) * D_INNER ** -0.5,
        "w_in_b": nrm(ks[10], (N_B, D_MODEL, 3 * D_INNER), f32) * D_MODEL ** -0.5,
        "ln_v_g": 1.0 + 0.05 * nrm(ks[11], (N_B, D_INNER), f32),
        "ln_v_b": 0.02 * nrm(ks[12], (N_B, D_INNER), f32),
        "w_spatial": nrm(ks[13], (N_B, B_GROUPS, B_CHUNK, B_CHUNK), f32) * B_CHUNK ** -0.5,
        "b_spatial": 1.0 + 0.1 * nrm(ks[14], (N_B, B_GROUPS, B_CHUNK), f32),
        "w_out_b": nrm(ks[15], (N_B, D_INNER, D_MODEL), f32) * D_INNER ** -0.5,
        "norm_ple": 1.0 + 0.05 * nrm(ks[16], (DEPTH, D_MODEL), f32),
        "w_ple_gate": nrm(ks[17], (DEPTH, D_MODEL, D_MODEL), f32) * D_MODEL ** -0.5,
        "w_ple_proj": nrm(ks[18], (DEPTH, PLE_DIM, D_MODEL), f32) * PLE_DIM ** -0.5,
        "norm_final": 1.0 + 0.05 * nrm(ks[19], (D_MODEL,), f32),
    }


def reference(x_prompt, x_sample, state_hgrn, p_prompt, p_sample, norm_mix, w_in_a, lb_logits,
              gnorm_a, w_out_a, w_in_b, ln_v_g, ln_v_b, w_spatial, b_spatial, w_out_b,
              norm_ple, w_ple_gate, w_ple_proj, norm_final):
    lb_cum = jnp.cumsum(jax.nn.softmax(lb_logits.astype(jnp.float32), axis=0), axis=0)
    lbs = lb_cum - lb_cum[0:1]
    s_zero = jnp.zeros((N_A, x_prompt.shape[0], A_HEADS, A_DK, A_DV), jnp.float32)
    y_prompt, state_hgrn_prompt, chunk_v_prompt = trunk(
        x_prompt, p_prompt, s_zero, lbs, norm_mix, w_in_a, gnorm_a, w_out_a, w_in_b, ln_v_g, ln_v_b,
        w_spatial, b_spatial, w_out_b, norm_ple, w_ple_gate, w_ple_proj, norm_final)
    y_sample, state_hgrn_sample, chunk_v_sample = trunk(
        x_sample, p_sample, state_hgrn, lbs, norm_mix, w_in_a, gnorm_a, w_out_a, w_in_b, ln_v_g, ln_v_b,
        w_spatial, b_spatial, w_out_b, norm_ple, w_ple_gate, w_ple_proj, norm_final)
    return (y_prompt, y_sample, state_hgrn_prompt, state_hgrn_sample, chunk_v_prompt, chunk_v_sample)
```

```python
from contextlib import ExitStack
import numpy as np
import concourse.bass as bass
import concourse.mybir as mybir
from concourse.bass_utils import run_bass_kernel_spmd

F32 = mybir.dt.float32
BF16 = mybir.dt.bfloat16
AF = mybir.ActivationFunctionType
ALU = mybir.AluOpType

NCORES = 8
NTOK = 2048
NS = 16
NT = NTOK + NS
EPS = 1e-6
PW = 1040

C_NMIX, C_NPLE, C_NFIN, C_LBL, C_GN, C_LNG, C_LNB = 0, 32, 64, 72, 88, 92, 124
C_ID, C_MT, C_MB, C_SM, C_ON, C_I16 = 156, 284, 412, 540, 1052, 1180
C_TOT = 1436


class Ev:
    __slots__ = ("sem", "val")

    def __init__(self, sem, val):
        self.sem, self.val = sem, val


class Buf:
    __slots__ = ("name", "w", "rs", "dsem", "dcnt", "excl")

    def __init__(self, name, excl=False):
        self.name, self.w, self.rs, self.dsem, self.dcnt, self.excl = name, None, {}, None, 0, excl


class Eng:
    def __init__(self, name, h, sem):
        self.name, self.h, self.sem, self.cnt, self.seen = name, h, sem, 0, {}

    def wait(self, ev):
        if ev is None:
            return
        if self.seen.get(ev.sem.num, 0) < ev.val:
            self.h.wait_ge(ev.sem, ev.val)
            self.seen[ev.sem.num] = ev.val


class K:
    def __init__(self, nc, es):
        self.nc, self.es = nc, es
        self.E = {}
        for name, h in (("pe", nc.tensor), ("act", nc.scalar), ("dve", nc.vector),
                        ("pool", nc.gpsimd), ("sp", nc.sync)):
            self.E[name] = Eng(name, h, es.enter_context(nc.semaphore("s_" + name)))
        self.stores = []
        self.nb = 0

    def buf(self, name=None, excl=False):
        self.nb += 1
        return Buf(name or "b%d" % self.nb, excl)

    def _deps(self, E, reads, writes):
        for b in reads:
            if b.w is not None and not (E.name == "pe" and b.w.sem is E.sem):
                E.wait(b.w)
        for b in writes:
            if b.w is not None and not (E.name == "pe" and b.w.sem is E.sem):
                E.wait(b.w)
            for ev in b.rs.values():
                if not (E.name == "pe" and ev.sem is E.sem):
                    E.wait(ev)

    def op(self, eng, fn, reads=(), writes=()):
        E = self.E[eng]
        ex = [b for b in reads if b.excl]
        if ex:
            reads = [b for b in reads if not b.excl]
            writes = list(writes) + [b for b in ex if b not in writes]
        self._deps(E, reads, writes)
        ins = fn()
        E.cnt += 1
        assert E.cnt < 60000
        ins.then_inc(E.sem, 1)
        ev = Ev(E.sem, E.cnt)
        for b in reads:
            b.rs[E.sem.num] = ev
        for b in writes:
            b.w = ev
            b.rs = {}
        return ev

    def dma(self, q, out, in_, reads=(), writes=()):
        E = self.E[q]
        self._deps(E, reads, writes)
        pb = writes[0] if writes else reads[0]
        if pb.dsem is None:
            self.nb += 1
            pb.dsem = self.es.enter_context(self.nc.semaphore("d%d_%s" % (self.nb, pb.name)))
        pb.dcnt += 16
        assert pb.dcnt < 60000
        E.h.dma_start(out=out, in_=in_).then_inc(pb.dsem, 16)
        ev = Ev(pb.dsem, pb.dcnt)
        for b in reads:
            b.rs[pb.dsem.num] = ev
        for b in writes:
            b.w = ev
            b.rs = {}
        if not writes:
            self.stores.append(ev)
        return ev

    def finish(self):
        E = self.E["sp"]
        for ev in self.stores:
            E.wait(ev)


def build(depth_run=4):
    nc = bass.Bass("TRN2", target_bir_lowering=False)
    D = {}

    def din(name, shape):
        D[name] = nc.dram_tensor(name, shape, F32, kind="ExternalInput").ap()
        return D[name]

    def dout(name, shape):
        D[name] = nc.dram_tensor(name, shape, F32, kind="ExternalOutput").ap()
        return D[name]

    xT = din("xT", [128, 8, NT])
    pTd = din("pT", [4, 128, 2, NT])
    st_in = din("st_in", [2, 8, 128, 16, 256])
    cst = din("cst", [128, C_TOT])
    lnraw = din("lnraw", [2, 2, 2048])
    wspT = din("wspT", [2, 128, 8, 128])
    bspr = din("bspr", [2, 1024])
    w00r = din("w00r", [2, 8])
    WAQF = din("WAQF", [2, 8, 128, 8, 256])
    WAVZ = din("WAVZ", [2, 8, 128, 8, 512])
    WOA = din("WOA", [2, 8, 128, 2, 1024])
    WBV = din("WBV", [2, 4, 128, 8, 512])
    WBUZ = din("WBUZ", [2, 8, 128, 8, 512])
    WOB = din("WOB", [2, 8, 128, 2, 1024])
    WG = din("WG", [4, 2, 128, 8, 512])
    WP = din("WP", [4, 128, 2, 1024])
    yT = dout("yT", [128, 8, NT])
    sp_out = dout("sp_out", [2, 8, 128, 256])
    ss_out = dout("ss_out", [2, 8, 128, 16, 256])
    cvp = dout("cvp", [2, 128, 2048])
    cvs = dout("cvs", [2, 16, 2048])

    es = ExitStack()
    with es:
        k = K(nc, es)

        def sb(name, shape, dt=F32):
            return es.enter_context(nc.sbuf_tensor(name, shape, dt))

        def ps(name, shape, dt=F32):
            return es.enter_context(nc.psum_tensor(name, shape, dt))

        hT = sb("hT", [128, 8, NT])
        hTb = [[k.buf("hT%d_%d" % (kt, b)) for b in range(5)] for kt in range(8)]
        hnT = sb("hnT", [128, 8, PW], BF16)
        hnTb = [k.buf("hnT%d" % b) for b in range(3)]
        csb = sb("csb", [128, C_TOT])
        cbuf = k.buf("csb")
        wring = [sb("wr%d" % i, [128, 4096], BF16) for i in range(3)]
        wrb = [k.buf("wr%d" % i) for i in range(3)]
        wstate = {"n": 0}
        T = [sb("T%d" % i, [128, 512]) for i in range(6)]
        Tb = [k.buf("T%d" % i) for i in range(6)]
        rstd = T[5]; rstdb = Tb[5]
        pTs = sb("pTs", [128, 2, PW], BF16); pTb = k.buf("pTs")
        cbf = sb("cbf", [128, 128], BF16)
        cbfb = k.buf("cbf")
        lbv = sb("lbv", [128, 3, 2, 8]); lbb = k.buf("lbv")
        pA = ps("pA", [128, 512]); pB = ps("pB", [128, 512])
        pD = [ps("pD0", [128, 512]), ps("pD1", [128, 512])]
        pAtt = ps("pAtt", [128, 512]); pO = ps("pO", [128, 512])
        pTr = ps("pTr", [128, 1024], BF16); pM = ps("pM", [128, 512])
        pAb, pBb, pAttb, pOb, pTrb, pMb = (k.buf(n, True) for n in ("pA", "pB", "pAtt", "pO", "pTr", "pM"))
        pDb = [k.buf("pD%d" % i, True) for i in range(2)]
        pAttq = [pAttb] * 4
        pD3b = [pDb[0], pDb[1], pTrb]
        pOq = [pOb] * 2
        pAB = [(pA, pAb), (pB, pBb)]
        abst = {"n": 0}
        rot = {"n": 0}

        def next_ab():
            abst["n"] += 1
            return pAB[abst["n"] % 2]

        ident_f = csb[:, C_ID:C_ID + 128]
        maskT = csb[:, C_MT:C_MT + 128]
        maskB = csb[:, C_MB:C_MB + 128]
        scanm = csb[:, C_SM:C_SM + 512]
        onesN = csb[:, C_ON:C_ON + 128]
        id16 = csb[:, C_I16:C_I16 + 256].rearrange("p (a b) -> p a b", b=16)
        identb = cbf[:, 0:128]

        def cvec(base, i, n):
            return csb[:, base + i * n: base + (i + 1) * n]

        k.dma("sp", csb[:], cst[:, :], writes=[cbuf])
        for bi_, (cb_, n_) in enumerate([(0, 512), (512, 512), (1024, 512), (1536, 512), (2048, 16)]):
            k.dma("sp", hT[:, :, cb_:cb_ + n_], xT[:, :, cb_:cb_ + n_], writes=[hTb[kt][bi_] for kt in range(8)])
        k.op("dve", lambda: nc.vector.tensor_copy(out=identb, in_=ident_f), reads=[cbuf], writes=[cbfb])
        def _lb():
            nc.vector.memset(lbv[:, 0, 0, :], 0.0)
            return nc.vector.tensor_tensor(out=lbv[:, 0, 1, :], in0=csb[:, C_LBL + 8:C_LBL + 16], in1=csb[:, C_LBL:C_LBL + 8], op=ALU.subtract)
        k.op("dve", _lb, reads=[cbuf], writes=[lbb])
        k.op("act", lambda: nc.scalar.activation(out=lbv[:, 0, 1, :], in_=lbv[:, 0, 1, :], func=AF.Sigmoid), reads=[lbb], writes=[lbb])
        def _lb2():
            nc.vector.tensor_scalar(out=lbv[:, 1, :, :], in0=lbv[:, 0, :, :], scalar1=-0.5, scalar2=0.5, op0=ALU.mult, op1=ALU.add)
            return nc.vector.tensor_scalar(out=lbv[:, 2, :, :], in0=lbv[:, 0, :, :], scalar1=0.5, scalar2=0.5, op0=ALU.mult, op1=ALU.add)
        k.op("dve", _lb2, reads=[lbb], writes=[lbb])

        def wload(src_ap, ncols_total, view3=None, slot=None):
            if slot is None:
                i = wstate["n"] % 3
                wstate["n"] += 1
            else:
                i = slot
            dst = wring[i][:, 0:ncols_total]
            if view3:
                dst = dst.rearrange("p (g c) -> p g c", c=view3)
            k.dma("pool", dst, src_ap, writes=[wrb[i]])
            return wring[i], wrb[i]

        def pass_blocks(p):
            bl = [(p * 1024, 512, 2 * p), (p * 1024 + 512, 512, 2 * p + 1)]
            if p == 1:
                bl.append((2048, 16, 4))
            return bl

        def lcol(p, cb):
            return cb - p * 1024

        def hn_idx(p, cb):
            return (cb - p * 1024) // 512

        def rsqrt(out, in_, rbufs, wbuf):
            k.op("act", lambda: nc.scalar.activation(out=out, in_=in_, func=AF.Sqrt, bias=EPS, scale=1.0), reads=rbufs, writes=[wbuf])
            k.op("dve", lambda: nc.vector.reciprocal(out=out, in_=out), reads=[wbuf], writes=[wbuf])

        def norm_pass(p, wbase, wi):
            for (cb, n, bi) in pass_blocks(p):
                for kt in range(8):
                    tb, tbb = T[kt % 2], Tb[kt % 2]
                    k.op("act", lambda kt=kt, tb=tb: nc.scalar.activation(out=tb[:, 0:n], in_=hT[:, kt, cb:cb + n], func=AF.Square),
                         reads=[hTb[kt][bi]], writes=[tbb])
                    k.op("pe", lambda kt=kt, tb=tb: nc.tensor.matmul(pM[:, 0:n], lhsT=onesN, rhs=tb[:, 0:n], start=(kt == 0), stop=(kt == 7)),
                         reads=[tbb, cbuf], writes=[pMb])
                rsqrt(rstd[:, 0:n], pM[:, 0:n], [pMb], rstdb)
                lc = lcol(p, cb)
                hb = hnTb[hn_idx(p, cb)]
                for kt in range(8):
                    k.op("dve", lambda kt=kt: nc.vector.scalar_tensor_tensor(
                        out=hnT[:, kt, lc:lc + n], in0=hT[:, kt, cb:cb + n], scalar=csb[:, wbase + wi * 8 + kt: wbase + wi * 8 + kt + 1],
                        in1=rstd[:, 0:n], op0=ALU.mult, op1=ALU.mult), reads=[hTb[kt][bi], rstdb, cbuf], writes=[hb])

        def outproj(p, wsrc, gt, gtb, slot=None, pre=None):
            if pre is not None:
                wt, wb = pre
            else:
                wt, wb = wload(wsrc.rearrange("g p a c -> p g (a c)"), 4096, view3=2048, slot=slot)
            wv = wt[:, 0:4096].rearrange("p (a c) -> p a c", c=1024)
            for m in range(8):
                for (cb, n, bi) in pass_blocks(p):
                    lc = lcol(p, cb)
                    pp, ppb = next_ab()
                    def mm(pp=pp, lc=lc, n=n, m=m):
                        for a4 in range(4):
                            ins = nc.tensor.matmul(pp[:, 0:n], lhsT=wv[:, a4, m * 128:(m + 1) * 128], rhs=gt[:, a4, lc:lc + n], start=(a4 == 0), stop=(a4 == 3))
                        return ins
                    k.op("pe", mm, reads=[wb, gtb], writes=[ppb])
                    k.op("dve", lambda pp=pp, n=n, m=m, cb=cb: nc.vector.tensor_tensor(out=hT[:, m, cb:cb + n], in0=hT[:, m, cb:cb + n], in1=pp[:, 0:n], op=ALU.add),
                         reads=[ppb], writes=[hTb[m][bi]])

        def ple_pass(p, i):
            norm_pass(p, C_NPLE, i)
            c0 = p * 1024
            ncol = PW if p == 1 else 1024
            k.dma("pool", pTs[:, :, 0:ncol], pTd[i, :, :, c0:c0 + ncol], writes=[pTb])
            wpt, wpb = wload(WP[i].rearrange("p a c -> p (a c)"), 2048)
            wpv = wpt[:, 0:2048].rearrange("p (a c) -> p a c", c=1024)
            wgl = []
            for ch in range(2):
                wgt, wgb = wload(WG[i, ch].rearrange("p a c -> p (a c)"), 4096)
                wgv = wgt[:, 0:4096].rearrange("p (a c) -> p a c", c=512)
                wgl.append((wgv, wgb))
                for mm_ in range(4):
                    m = ch * 4 + mm_
                    for (cb, n, bi) in pass_blocks(p):
                        if n == 16:
                            continue
                        lc = lcol(p, cb)
                        hb = hnTb[hn_idx(p, cb)]
                        rot["n"] += 1
                        r = rot["n"] % 2
                        pg, pgb = (pA, pAb) if r == 0 else (pB, pBb)
                        pq, pqb = pD[r], pDb[r]
                        tg_, tgb_ = (T[2], Tb[2]) if r == 0 else (T[4], Tb[4])
                        tm_, tmb_ = (T[3], Tb[3]) if r == 0 else (T[0], Tb[0])
                        def mg(lc=lc, n=n, mm_=mm_, pg=pg):
                            for kt in range(8):
                                ins = nc.tensor.matmul(pg[:, 0:n], lhsT=wgv[:, kt, mm_ * 128:(mm_ + 1) * 128], rhs=hnT[:, kt, lc:lc + n], start=(kt == 0), stop=(kt == 7))
                            return ins
                        k.op("pe", mg, reads=[wgb, hb], writes=[pgb])
                        def mp(lc=lc, n=n, m=m, pq=pq):
                            nc.tensor.matmul(pq[:, 0:n], lhsT=wpv[:, 0, m * 128:(m + 1) * 128], rhs=pTs[:, 0, lc:lc + n], start=True, stop=False)
                            return nc.tensor.matmul(pq[:, 0:n], lhsT=wpv[:, 1, m * 128:(m + 1) * 128], rhs=pTs[:, 1, lc:lc + n], start=False, stop=True)
                        k.op("pe", mp, reads=[wpb, pTb], writes=[pqb])
                        k.op("act", lambda n=n: nc.scalar.activation(out=tg_[:, 0:n], in_=pg[:, 0:n], func=AF.Tanh, scale=0.5), reads=[pgb], writes=[tgb_])
                        k.op("dve", lambda n=n: nc.vector.scalar_tensor_tensor(out=tm_[:, 0:n], in0=tg_[:, 0:n], scalar=1.0, in1=pq[:, 0:n], op0=ALU.add, op1=ALU.mult), reads=[tgb_, pqb], writes=[tmb_])
                        k.op("dve", lambda n=n, m=m, cb=cb: nc.vector.scalar_tensor_tensor(out=hT[:, m, cb:cb + n], in0=tm_[:, 0:n], scalar=0.5, in1=hT[:, m, cb:cb + n], op0=ALU.mult, op1=ALU.add),
                             reads=[tmb_], writes=[hTb[m][bi]])
            if p == 1:
                hb = hnTb[2]
                def mgs():
                    for m in range(8):
                        wgv_, _ = wgl[m // 4]
                        for kt in range(8):
                            ins = nc.tensor.matmul(pA[:, m * 16:(m + 1) * 16], lhsT=wgv_[:, kt, (m % 4) * 128:(m % 4 + 1) * 128], rhs=hnT[:, kt, 1024:1040], start=(kt == 0), stop=(kt == 7))
                    return ins
                k.op("pe", mgs, reads=[wgl[0][1], wgl[1][1], hb], writes=[pAb])
                def mps():
                    for m in range(8):
                        nc.tensor.matmul(pB[:, m * 16:(m + 1) * 16], lhsT=wpv[:, 0, m * 128:(m + 1) * 128], rhs=pTs[:, 0, 1024:1040], start=True, stop=False)
                        ins = nc.tensor.matmul(pB[:, m * 16:(m + 1) * 16], lhsT=wpv[:, 1, m * 128:(m + 1) * 128], rhs=pTs[:, 1, 1024:1040], start=False, stop=True)
                    return ins
                k.op("pe", mps, reads=[wpb, pTb], writes=[pBb])
                k.op("act", lambda: nc.scalar.activation(out=T[2][:, 0:128], in_=pA[:, 0:128], func=AF.Tanh, scale=0.5), reads=[pAb], writes=[Tb[2]])
                k.op("dve", lambda: nc.vector.scalar_tensor_tensor(out=T[3][:, 0:128], in0=T[2][:, 0:128], scalar=1.0, in1=pB[:, 0:128], op0=ALU.add, op1=ALU.mult), reads=[Tb[2], pBb], writes=[Tb[3]])
                k.op("dve", lambda: nc.vector.scalar_tensor_tensor(out=hT[:, :, 2048:2064], in0=T[3][:, 0:128].rearrange("p (m x) -> p m x", x=16), scalar=0.5, in1=hT[:, :, 2048:2064], op0=ALU.mult, op1=ALU.add),
                     reads=[Tb[3]], writes=[hTb[m][4] for m in range(8)])

        def barrier():
            evs = {}
            for en in ("pe", "act", "dve", "pool", "sp"):
                E = k.E[en]
                if E.cnt > 0:
                    evs[E.sem.num] = Ev(E.sem, E.cnt)
            for ev in k.stores:
                if ev.sem.num not in evs or evs[ev.sem.num].val < ev.val:
                    evs[ev.sem.num] = ev
            for en in ("pe", "act", "dve", "pool", "sp"):
                for ev in evs.values():
                    k.E[en].wait(ev)

        def layer_A(i, j, es2):
            def sb2(name, shape, dt=F32):
                return es2.enter_context(nc.sbuf_tensor(name + "_L%d" % i, shape, dt))
            gTt = [sb2("gT%d" % t, [128, 4, PW], BF16) for t in range(2)]; gTb = [k.buf("gT%d" % t) for t in range(2)]
            TA = [sb2("TA%d" % t, [128, 512]) for t in range(4)]; TAb = [k.buf("TA%d" % t) for t in range(4)]
            qtil = sb2("qtil", [128, 1024], BF16); qtilb = k.buf("qtil")
            qpad = sb2("qpad", [128, 16, 128], BF16); qpadb = k.buf("qpad")
            ktil = sb2("ktil", [128, 1024], BF16); ktilb = k.buf("ktil")
            kdec = sb2("kdec", [128, 1024], BF16); kdecb = k.buf("kdec")
            dch = sb2("dch", [128, 16]); dchb = k.buf("dch")
            vh = sb2("vh", [128, 8, 256], BF16); vhb = [k.buf("vh%d" % t) for t in range(8)]
            szh = sb2("szh", [128, 8, 256], BF16); szhb = [k.buf("szh%d" % t) for t in range(8)]
            vs = sb2("vs", [16, 256], BF16); vsb = k.buf("vs")
            szs = sb2("szs", [16, 256], BF16); szsb = k.buf("szs")
            kT = sb2("kT", [128, 8, 128], BF16); kTb = k.buf("kT")
            attm = [sb2("attm%d" % t, [128, 128], BF16) for t in range(2)]; attmb = [k.buf("attm%d" % t) for t in range(2)]
            Sf = sb2("Sf", [128, 8, 256]); Sfb = [k.buf("Sf%d" % h) for h in range(8)]
            Sx = sb2("Sx", [128, 256]); Sxb = k.buf("Sx")
            NSB = 6
            Sb = sb2("Sb", [128, NSB, 256], BF16); Sbb = [k.buf("Sb%d" % t) for t in range(NSB)]
            junk = sb2("junk", [128, 256], BF16); junkb = k.buf("junk")
            og = sb2("og", [128, 8, 256], BF16); ogs = sb2("ogs", [16, 256], BF16); ogb = k.buf("og"); ogb2 = k.buf("og2")
            ssq = sb2("ssq", [128, 32]); ssqb = k.buf("ssq")
            k.op("dve", lambda: nc.vector.memset(ssq[:, :], 1.0), writes=[ssqb])
            qS2 = sb2("qS", [128, 2, 16]); fS2 = sb2("fS", [128, 2, 16]); kkS2 = sb2("kkS", [128, 2, 16]); smb2 = [k.buf("smp0"), k.buf("smp1")]
            TS = sb2("TS", [128, 4, 16]); TSb = [k.buf("TS%d" % t) for t in range(4)]
            kkST = sb2("kkST", [16, 128]); kkSTb = k.buf("kkST")
            Am3 = sb2("Am3", [16, 3, 128], BF16); Am3b = [k.buf("Am3_%d" % t) for t in range(3)]
            Qp = sb2("Qp", [128, 16, 16], BF16); Qpb = k.buf("Qp")
            qf16 = sb2("qf16", [128, 16]); qf16b = k.buf("qf16")
            qk = sb2("qk", [16, 4]); qkb = k.buf("qk")
            osb = sb2("osb", [16, 256]); osbb = k.buf("osb")
            NSM = 6
            ssm = [sb2("ssm%d" % t, [128, 2, 256]) for t in range(NSM)]; ssmb = [[k.buf("ssm%d_%d" % (t, u)) for u in range(2)] for t in range(NSM)]
            snb = [sb2("snb%d" % t, [128, 256], BF16) for t in range(3)]; snbb = [k.buf("snb%d" % t) for t in range(3)]

            def _z():
                nc.vector.memset(qpad[:], 0.0)
                return nc.vector.memset(Sf[:], 0.0)
            k.op("dve", _z, writes=[qpadb] + Sfb)
            c1_ = lambda h: lbv[:, 1, j, h:h + 1]
            c2_ = lambda h: lbv[:, 2, j, h:h + 1]

            for p in range(2):
                norm_pass(p, C_NMIX, i)
                def make_head(h):
                    W = {}
                    qS, fS, kkS, smb = qS2[:, h % 2, :], fS2[:, h % 2, :], kkS2[:, h % 2, :], smb2[h % 2]
                    gt, gtb = gTt[(h // 2) % 2], gTb[(h // 2) % 2]
                    gk = 2 * (h % 2)
                    blks = pass_blocks(p)

                    def tset(bix):
                        if bix == 2:
                            return TS[:, 0, :], TSb[0], TS[:, 1, :], TSb[1], TS[:, 2, :], TSb[2], TS[:, 3, :], TSb[3]
                        if bix % 2 == 0:
                            return T[2], Tb[2], T[3], Tb[3], T[4], Tb[4], T[5], Tb[5]
                        return TA[0], TAb[0], TA[1], TAb[1], TA[2], TAb[2], TA[3], TAb[3]

                    def qf_s1(bix, part=0):
                        cb, n, bi = blks[bix]
                        lc = lcol(p, cb)
                        hb = hnTb[hn_idx(p, cb)]
                        Tq, Tqb, Tk, Tkb, Tl, Tlb, Te, Teb = tset(bix)
                        pq_, pqb_ = (pTr[:, 0:1024].bitcast(F32), pTrb) if part == 1 else (pA, pAb)
                        def mq(o=0, pp=pq_):
                            for kt in range(8):
                                ins = nc.tensor.matmul(pp[:, 0:n], lhsT=W["wv"][:, kt, o:o + 128], rhs=hnT[:, kt, lc:lc + n], start=(kt == 0), stop=(kt == 7))
                            return ins
                        if part in (0, 1):
                            k.op("pe", mq, reads=[W["wb"], hb], writes=[pqb_])
                            k.op("pe", lambda: mq(128, pB), reads=[W["wb"], hb], writes=[pBb])
                            k.op("act", lambda: nc.scalar.activation(out=Tq[:, 0:n], in_=pq_[:, 0:n], func=AF.Silu), reads=[pqb_], writes=[Tqb])
                            k.op("act", lambda: nc.scalar.activation(out=Tk[:, 0:n], in_=pB[:, 0:n], func=AF.Tanh, scale=0.5), reads=[pBb], writes=[Tkb])
                            k.op("act", lambda: nc.scalar.activation(out=Tl[:, 0:n], in_=Tk[:, 0:n], func=AF.Identity, bias=c2_(h), scale=c1_(h)),
                                 reads=[Tkb, lbb], writes=[Tlb])
                            if n == 512:
                                k.op("pool", lambda: nc.gpsimd.tensor_tensor(out=Te[:, :], in0=Tl[:, :], in1=scanm, op=ALU.mult), reads=[Tlb, cbuf], writes=[Teb])
                        if n == 512:
                            if part in (0, 2):
                                k.op("dve", lambda: nc.vector.tensor_tensor_scan(out=Tk[:, :], data0=Tl[:, :], data1=Te[:, :], initial=0.0, op0=ALU.mult, op1=ALU.max),
                                     reads=[Tlb, Teb], writes=[Tkb])
                                k.op("pool", lambda: nc.gpsimd.tensor_scalar(out=Tl[:, :], in0=Tl[:, :], scalar1=-1.0, scalar2=1.0, op0=ALU.mult, op1=ALU.add), reads=[Tlb], writes=[Tlb])
                            if part in (0, 3):
                                k.op("dve", lambda: nc.vector.reciprocal(out=Te[:, :], in_=Tk[:, :]), reads=[Tkb], writes=[Teb])
                                k.op("pool", lambda: nc.gpsimd.tensor_tensor(out=Tl[:, :], in0=Tl[:, :], in1=Te[:, :], op=ALU.mult), reads=[Tlb, Teb], writes=[Tlb])
                        elif part in (0, 1):
                            def smp_():
                                nc.vector.tensor_copy(out=qS, in_=Tq[:, 0:16])
                                nc.vector.tensor_copy(out=fS, in_=Tl[:, 0:16])
                                return nc.vector.tensor_scalar(out=kkS, in0=Tl[:, 0:16], scalar1=-1.0, scalar2=1.0, op0=ALU.mult, op1=ALU.add)
                            k.op("dve", smp_, reads=[Tqb, Tlb], writes=[smb])

                    def qf_s2(bix):
                        cb, n, bi = blks[bix]
                        if n != 512:
                            return
                        lc = lcol(p, cb)
                        Tq, Tqb, Tk, Tkb, Tl, Tlb, Te, Teb = tset(bix)
                        k.op("pool", lambda: nc.gpsimd.tensor_tensor(out=qtil[:, lc:lc + 512], in0=Tq[:, :], in1=Tk[:, :], op=ALU.mult),
                             reads=[Tqb, Tkb], writes=[qtilb])
                        c0 = lc // 64
                        def qp():
                            src = qtil[:, lc:lc + 512].rearrange("p (t a x) -> p t a x", a=2, x=64)
                            dst = qpad[:, c0:c0 + 8, :].rearrange("p (t a) (b x) -> p t a b x", a=2, b=2)
                            nc.scalar.copy(out=dst[:, :, 0, 0, :], in_=src[:, :, 0, :])
                            return nc.scalar.copy(out=dst[:, :, 1, 1, :], in_=src[:, :, 1, :])
                        k.op("act", qp, reads=[qtilb], writes=[qpadb])
                        k.op("act", lambda: nc.scalar.copy(out=ktil[:, lc:lc + 512], in_=Tl[:, :]), reads=[Tlb], writes=[ktilb])
                        def kd():
                            ebl = Tk[:, :].rearrange("p (c t) -> p c t", t=64)[:, :, 63:64].to_broadcast([128, 8, 64])
                            return nc.vector.tensor_tensor(out=kdec[:, lc:lc + 512].rearrange("p (c t) -> p c t", t=64),
                                                           in0=Tl[:, :].rearrange("p (c t) -> p c t", t=64), in1=ebl, op=ALU.mult)
                        k.op("dve", kd, reads=[Tlb, Tkb], writes=[kdecb])
                        k.op("pool", lambda: nc.gpsimd.tensor_copy(out=dch[:, c0:c0 + 8], in_=Tk[:, :].rearrange("p (c t) -> p c t", t=64)[:, :, 63]),
                             reads=[Tkb], writes=[dchb])

                    ntile = 9 if p == 1 else 8
                    vzbanks = [(pO, pOb), (pM, pMb), (pAtt, pAttb)]

                    def vz(lt):
                        if lt >= ntile or W.get(("vz", lt)):
                            return
                        W[("vz", lt)] = True
                        M = 128 if lt < 8 else 16
                        hb = hnTb[lt // 4]
                        if lt < 3 and not W.get("early"):
                            pp, ppb = vzbanks[lt]
                        else:
                            pp, ppb = pA, pAb
                        def mvz():
                            for kt in range(8):
                                ins = nc.tensor.matmul(pp[0:M, :], lhsT=hnT[:, kt, lt * 128:lt * 128 + M], rhs=W["wv2"][:, kt, :], start=(kt == 0), stop=(kt == 7))
                            return ins
                        k.op("pe", mvz, reads=[W["wb2"], hb], writes=[ppb])
                        if lt < 8:
                            k.op("dve", lambda: nc.vector.tensor_copy(out=vh[:, lt, :], in_=pp[:, 0:256]), reads=[ppb], writes=[vhb[lt]])
                            k.op("act", lambda: nc.scalar.activation(out=szh[:, lt, :], in_=pp[:, 256:512], func=AF.Silu), reads=[ppb], writes=[szhb[lt]])
                        else:
                            k.op("dve", lambda: nc.vector.tensor_copy(out=vs[:, :], in_=pp[0:16, 0:256]), reads=[ppb], writes=[vsb])
                            k.op("act", lambda: nc.scalar.activation(out=szs[:, :], in_=pp[0:16, 256:512], func=AF.Silu), reads=[ppb], writes=[szsb])


                    def preq():
                        wt, W["wb"] = wload(WAQF[j, h].rearrange("p a c -> p (a c)"), 2048, slot=2)
                        W["wv"] = wt[:, 0:2048].rearrange("p (a c) -> p a c", c=256)

                    def pre():
                        wt2, W["wb2"] = wload(WAVZ[j, h].rearrange("p a c -> p (a c)"), 4096, slot=h % 2)
                        W["wv2"] = wt2[:, 0:4096].rearrange("p (a c) -> p a c", c=512)

                    def prew():
                        W["wo"] = wload(WOA[j, h - 1:h + 1].rearrange("g p a c -> p g (a c)"), 4096, view3=2048, slot=h % 2)

                    def stl():
                        if p == 1:
                            for g2 in range(NSM):
                                bi_ = (8 * h + g2) % NSM
                                k.dma("sp", ssm[bi_][:, :, :], st_in[j, h, :, g2 * 2:(g2 + 1) * 2, :], writes=ssmb[bi_])

                    def s1all():
                        for bix in range(len(blks)):
                            qf_s1(bix)

                    def vz_early():
                        W["early"] = True
                        vz(0)
                        vz(1)
                        W["early"] = False

                    def mida():
                        qf_s2(0)
                        qf_s2(1)
                        if p == 1:
                            vz(8)
                        vz(0)
                        vz(1)
                        vz(2)

                    def mid(nxt, nxt2=None):
                        def trs():
                            for lt in range(8):
                                ins = nc.tensor.transpose(out=pTr[:, lt * 128:(lt + 1) * 128], in_=kdec[:, lt * 128:(lt + 1) * 128], identity=identb)
                            return ins
                        k.op("pe", trs, reads=[kdecb, cbfb], writes=[pTrb])
                        k.op("act", lambda: nc.scalar.copy(out=kT[:, :, :], in_=pTr[:, 0:1024].rearrange("p (t c) -> p t c", c=128)), reads=[pTrb], writes=[kTb])
                        k.op("act", lambda: nc.scalar.copy(out=Sb[:, 0, :], in_=Sf[:, h, :]), reads=[Sfb[h]], writes=[Sbb[0]])

                        def stage_a(lt):
                            for a in range(2):
                                c = 2 * lt + a
                                dsl = c % 2
                                pd = pD[dsl][:, 0:256]
                                k.op("pe", lambda lt=lt, a=a, pd=pd: nc.tensor.matmul(pd, lhsT=kT[a * 64:(a + 1) * 64, lt, :], rhs=vh[a * 64:(a + 1) * 64, lt, :], start=True, stop=True),
                                     reads=[kTb, vhb[lt]], writes=[pD3b[dsl]])
                                if c % 2 == 0:
                                    ssrc, ssrcb, sdst, sdstb = Sf[:, h, :], Sfb[h], Sx[:, :], Sxb
                                else:
                                    ssrc, ssrcb, sdst, sdstb = Sx[:, :], Sxb, Sf[:, h, :], Sfb[h]
                                k.op("dve", lambda c=c, pd=pd, ssrc=ssrc, sdst=sdst: nc.vector.scalar_tensor_tensor(out=sdst, in0=ssrc, scalar=dch[:, c:c + 1], in1=pd, op0=ALU.mult, op1=ALU.add),
                                     reads=[pD3b[dsl], dchb, ssrcb], writes=[sdstb])
                                if c < 15:
                                    k.op("act", lambda c=c, sdst=sdst: nc.scalar.copy(out=Sb[:, (c + 1) % NSB, :], in_=sdst), reads=[sdstb], writes=[Sbb[(c + 1) % NSB]])
                            k.op("pe", lambda lt=lt: nc.tensor.matmul(pAtt[:, 0:128], lhsT=ktil[:, lt * 128:(lt + 1) * 128], rhs=qtil[:, lt * 128:(lt + 1) * 128], start=True, stop=True),
                                 reads=[ktilb, qtilb], writes=[pAttb])
                            k.op("dve", lambda lt=lt: nc.vector.tensor_tensor(out=attm[lt % 2][:, :], in0=pAtt[:, 0:128], in1=maskT, op=ALU.mult),
                                 reads=[pAttb, cbuf], writes=[attmb[lt % 2]])

                        def obank(lt):
                            return (pO, pOb) if lt % 2 == 0 else (pM, pMb)

                        def stage_b(lt):
                            sl = lt % 2
                            c = 2 * lt
                            po_t, pob = obank(lt)
                            po = po_t[:, 0:256]
                            def mo():
                                nc.tensor.matmul(po, lhsT=attm[sl][:, :], rhs=vh[:, lt, :], start=True, stop=False)
                                nc.tensor.matmul(po, lhsT=qpad[:, c, :], rhs=Sb[:, c % NSB, :], start=False, stop=False)
                                return nc.tensor.matmul(po, lhsT=qpad[:, c + 1, :], rhs=Sb[:, (c + 1) % NSB, :], start=False, stop=True)
                            k.op("pe", mo, reads=[attmb[sl], vhb[lt], qpadb, Sbb[c % NSB], Sbb[(c + 1) % NSB]], writes=[pob])
                            post_b(po, pob, szh[:, lt, :], szhb[lt], 128, lt)

                        def post_b(po, pob, sz, szb, M, lt):
                            k.op("act", lambda: nc.scalar.activation(out=junk[0:M, :], in_=po[0:M, :], func=AF.Square, scale=1.0 / 16.0, accum_out=ssq[0:M, lt:lt + 1]),
                                 reads=[pob], writes=[junkb, ssqb])
                            ogd = og[0:M, lt, :] if lt < 8 else ogs[0:M, :]
                            k.op("dve", lambda: nc.vector.tensor_tensor(out=ogd, in0=po[0:M, :], in1=sz[0:M, :] if M < 128 else sz, op=ALU.mult),
                                 reads=[pob, szb], writes=[ogb if (lt < 4 or lt == 8) else ogb2])

                        def tail(nt):
                            rsqrt(ssq[:, 16:16 + nt], ssq[:, 0:nt], [ssqb], ssqb)
                            k.op("dve", lambda: nc.vector.tensor_tensor(out=og[:, 0:4, :], in0=og[:, 0:4, :], in1=ssq[:, 16:20].unsqueeze(2).to_broadcast([128, 4, 256]), op=ALU.mult),
                                 reads=[ssqb], writes=[ogb])
                            k.op("pool", lambda: nc.gpsimd.tensor_tensor(out=og[:, 4:8, :], in0=og[:, 4:8, :], in1=ssq[:, 20:24].unsqueeze(2).to_broadcast([128, 4, 256]), op=ALU.mult),
                                 reads=[ssqb], writes=[ogb2])
                            if nt == 9:
                                k.op("dve", lambda: nc.vector.tensor_scalar(out=ogs[:, :], in0=ogs[:, :], scalar1=ssq[0:16, 24:25], scalar2=None, op0=ALU.mult),
                                     reads=[ssqb], writes=[ogb])
                            for q4 in range(2):
                                def tg():
                                    for t4 in range(4):
                                        for half in range(2):
                                            ins = nc.tensor.transpose(out=pTr[:, (t4 * 2 + half) * 128:(t4 * 2 + half + 1) * 128], in_=og[:, q4 * 4 + t4, half * 128:(half + 1) * 128], identity=identb)
                                    return ins
                                k.op("pe", tg, reads=[ogb if q4 == 0 else ogb2, cbfb], writes=[pTrb])
                                def cg():
                                    src = pTr[:, 0:1024].rearrange("p (t a x) -> p t a x", a=2, x=128)
                                    for half in range(2):
                                        ins = nc.scalar.activation(out=gt[:, gk + half, q4 * 512:(q4 + 1) * 512].rearrange("p (t x) -> p t x", x=128), in_=src[:, :, half, :], func=AF.Copy,
                                                                   scale=csb[:, C_GN + j * 2 + half:C_GN + j * 2 + half + 1])
                                    return ins
                                k.op("act", cg, reads=[pTrb, cbuf], writes=[gtb])
                            if nt == 9:
                                def tgs():
                                    nc.tensor.transpose(out=pTr[:, 0:16], in_=ogs[0:16, 0:128], identity=identb[0:16, 0:16])
                                    return nc.tensor.transpose(out=pTr[:, 128:144], in_=ogs[0:16, 128:256], identity=identb[0:16, 0:16])
                                k.op("pe", tgs, reads=[ogb, cbfb], writes=[pTrb])
                                def cgs():
                                    nc.scalar.activation(out=gt[:, gk, 1024:1040], in_=pTr[:, 0:16], func=AF.Copy, scale=csb[:, C_GN + j * 2:C_GN + j * 2 + 1])
                                    return nc.scalar.activation(out=gt[:, gk + 1, 1024:1040], in_=pTr[:, 128:144], func=AF.Copy, scale=csb[:, C_GN + j * 2 + 1:C_GN + j * 2 + 2])
                                k.op("act", cgs, reads=[pTrb, cbuf], writes=[gtb])

                        W["tail"] = tail
                        if p == 1:
                            pOS = pTr[0:16, 0:512].bitcast(F32)
                            k.op("pe", lambda: nc.tensor.transpose(out=pM[0:16, 0:128], in_=kkS, identity=ident_f), reads=[smb, cbuf], writes=[pMb])
                            k.op("dve", lambda: nc.vector.tensor_copy(out=kkST[:, :], in_=pM[0:16, 0:128]), reads=[pMb], writes=[kkSTb])
                            k.op("dve", lambda: nc.vector.tensor_tensor(out=qf16[:, :], in0=qS, in1=fS, op=ALU.mult), reads=[smb], writes=[qf16b])
                            k.op("dve", lambda: nc.vector.tensor_tensor(out=Qp[:, :, :], in0=qf16[:, :].unsqueeze(2).to_broadcast([128, 16, 16]), in1=id16, op=ALU.mult),
                                 reads=[qf16b, cbuf], writes=[Qpb])
                            k.op("dve", lambda: nc.vector.tensor_tensor(out=qf16[:, :], in0=qS, in1=kkS, op=ALU.mult), reads=[smb, Qpb], writes=[qf16b])
                            k.op("pe", lambda: nc.tensor.matmul(pM[0:16, 0:1], lhsT=qf16[:, :], rhs=csb[:, C_ON:C_ON + 1], start=True, stop=True), reads=[qf16b, cbuf], writes=[pMb])
                            k.op("dve", lambda: nc.vector.tensor_scalar(out=qk[:, 0:1], in0=pM[0:16, 0:1], scalar1=1024.0, scalar2=None, op0=ALU.mult), reads=[pMb], writes=[qkb])

                        def mk_am(b):
                            k.op("dve", lambda: nc.vector.tensor_scalar(out=Am3[:, b % 3, :], in0=kkST[:, :], scalar1=csb[0:16, C_ID + b:C_ID + b + 1], scalar2=None, op0=ALU.mult),
                                 reads=[kkSTb, cbuf], writes=[Am3b[b % 3]])

                        def samp(b):
                            g2, bb = b // 2, b % 2
                            sm_, smb_ = ssm[(8 * h + g2) % NSM], ssmb[(8 * h + g2) % NSM]
                            pd = pD[b % 2][:, 0:256]
                            pdb = pDb[b % 2]
                            k.op("act", lambda: nc.scalar.copy(out=snb[b % 3][:, :], in_=sm_[:, bb, :]), reads=[smb_[bb]], writes=[snbb[b % 3]])
                            k.op("pe", lambda: nc.tensor.matmul(pOS, lhsT=Qp[:, b, :], rhs=snb[b % 3][:, :], start=(b == 0), stop=(b == 15)),
                                 reads=[Qpb, snbb[b % 3]], writes=[pTrb])
                            k.op("pe", lambda: nc.tensor.matmul(pd, lhsT=Am3[:, b % 3, :], rhs=vs[:, :], start=True, stop=True), reads=[Am3b[b % 3], vsb], writes=[pdb])
                            if b + 2 < 16:
                                mk_am(b + 2)
                            k.op("dve", lambda: nc.vector.scalar_tensor_tensor(out=sm_[:, bb, :], in0=sm_[:, bb, :], scalar=fS[:, b:b + 1], in1=pd, op0=ALU.mult, op1=ALU.add),
                                 reads=[pdb, smb], writes=[smb_[bb]])
                            if bb == 1:
                                k.dma("sp", ss_out[j, h, :, g2 * 2:(g2 + 1) * 2, :], sm_[:, :, :], reads=smb_)
                                if g2 + NSM < 8:
                                    k.dma("sp", sm_[:, :, :], st_in[j, h, :, (g2 + NSM) * 2:(g2 + NSM + 1) * 2, :], writes=smb_)

                        for step in range(9):
                            if step < 8:
                                stage_a(step)
                            if 0 <= step - 1 < 8:
                                stage_b(step - 1)
                            vz(step + 3)
                            if step == 6 and h % 2 == 1:
                                prew()
                            if step == 7 and nxt2 is not None:
                                nxt2["preq"]()
                            if step == 7 and nxt is not None:
                                nxt["vz_early"]()
                            if step == 0:
                                yield
                            if nxt is not None:
                                if step == 0:
                                    nxt["pre"]()
                                elif step == 1:
                                    nxt["s1"](0, 1)
                                elif step == 2:
                                    nxt["s1"](0, 2)
                                elif step == 3:
                                    nxt["s1"](0, 3)
                                elif step == 4:
                                    nxt["s1"](1, 1)
                                elif step == 5:
                                    nxt["s1"](1, 2)
                                elif step == 6:
                                    nxt["s1"](1, 3)
                                    if p == 1:
                                        nxt["s1"](2)
                        if p == 1:
                            mk_am(0)
                            mk_am(1)
                            for b in range(16):
                                samp(b)
                        if p == 1:
                            k.dma("sp", sp_out[j, h, :, :], Sf[:, h, :], reads=[Sfb[h]])
                            k.op("dve", lambda: nc.vector.scalar_tensor_tensor(out=osb[:, :], in0=vs[:, :], scalar=qk[:, 0:1], in1=pOS, op0=ALU.mult, op1=ALU.add),
                                 reads=[vsb, qkb, pTrb], writes=[osbb])
                            post_b(osb[:, :], osbb, szs, szsb, 16, 8)
                            if nxt is not None:
                                nxt["stl"]()

                    def post():
                        W["tail"](9 if p == 1 else 8)
                        if h % 2 == 1:
                            outproj(p, WOA[j, h - 1:h + 1], gt, gtb, pre=W["wo"])

                    return {"mida": mida, "vz_early": vz_early, "pre": pre, "preq": preq, "stl": stl, "s1": qf_s1, "s1all": s1all, "mid": mid, "post": post}

                heads = [make_head(h) for h in range(8)]
                heads[0]["preq"]()
                heads[0]["pre"]()
                heads[0]["stl"]()
                heads[0]["s1all"]()
                heads[1]["preq"]()
                for h in range(8):
                    heads[h]["mida"]()
                    gen = heads[h]["mid"](heads[h + 1] if h < 7 else None, heads[h + 2] if h < 6 else None)
                    next(gen)
                    if h > 0:
                        heads[h - 1]["post"]()
                    for _ in gen:
                        pass
                heads[7]["post"]()
                ple_pass(p, i)

        def layer_B(i, j, es2):
            def sb2(name, shape, dt=F32):
                return es2.enter_context(nc.sbuf_tensor(name + "_L%d" % i, shape, dt))
            vraw = sb2("vraw", [128, 9, 2048], BF16); vrawb = [k.buf("vraw%d" % t) for t in range(9)]
            vo = [sb2("vo%d" % t, [128, 2048]) for t in range(2)]; vob = [k.buf("vo%d" % t) for t in range(2)]
            st1 = sb2("st1", [128, 9, 4]); st2 = sb2("st2", [128, 9, 4]); stb = [k.buf("st%d" % t) for t in range(9)]
            mr = sb2("mr", [128, 9, 4]); mrb = [k.buf("mr%d" % t) for t in range(9)]
            k.op("dve", lambda: nc.vector.memset(mr[:, :, :], 1.0), writes=mrb)
            gTt = [sb2("gT0", [128, 4, PW], BF16)] * 2; gTb = [k.buf("gT0")] * 2
            Cc = sb2("Cc", [128, 16, 128]); Ccb = k.buf("Cc")
            WmT = sb2("WmT", [128, 8, 128], BF16); WmTb = k.buf("WmT")
            wsf = vo[0][:, 0:1024].rearrange("p (a b) -> p a b", b=128); wsfb = vob[0]
            bsb = vo[0][:, 1024:2048].rearrange("p (a b) -> p a b", b=128); bsbb = vob[0]
            Rr = vo[1][:, 0:1024].rearrange("p (a b) -> p a b", b=128); Rrb = vob[1]
            w00 = sb2("w00", [16, 8]); w00b = k.buf("w00")
            Dg = sb2("Dg", [16, 8, 16], BF16); Dgb = k.buf("Dg")
            junk = sb2("junkB", [128, 512], BF16); junkb = k.buf("junkB")
            k.dma("sp", wsf, wspT[j, :, :, :], writes=[wsfb])
            k.dma("sp", vo[0][:, 1024:2048], bspr[j, :].partition_broadcast(128), writes=[bsbb])
            k.dma("sp", w00[:, :], w00r[j, :].partition_broadcast(16), writes=[w00b])
            def mk():
                return nc.vector.tensor_tensor(out=wsf, in0=wsf, in1=maskB.unsqueeze(1).to_broadcast([128, 8, 128]), op=ALU.mult)
            k.op("dve", mk, reads=[cbuf], writes=[wsfb])
            k.op("dve", lambda: nc.vector.tensor_copy(out=WmT[:, :, :], in_=wsf), reads=[wsfb], writes=[WmTb])
            for half in range(2):
                k.op("pe", lambda half=half: nc.tensor.matmul(pM[:, 0:512], lhsT=onesN, rhs=vo[0][:, half * 512:(half + 1) * 512], start=True, stop=True),
                     reads=[wsfb, cbuf], writes=[pMb])
                k.op("dve", lambda half=half: nc.vector.tensor_scalar(out=vo[1][:, half * 512:(half + 1) * 512], in0=pM[:, 0:512], scalar1=1024.0, scalar2=None, op0=ALU.mult),
                     reads=[pMb], writes=[Rrb])
            def mkC():
                for kt in range(16):
                    ins = nc.vector.scalar_tensor_tensor(out=Cc[:, kt, :], in0=Rr[:, kt // 2, :], scalar=csb[:, C_LNB + j * 16 + kt:C_LNB + j * 16 + kt + 1], in1=bsb[:, kt // 2, :],
                                                         op0=ALU.mult, op1=ALU.add)
                return ins
            k.op("dve", mkC, reads=[Rrb, bsbb, cbuf], writes=[Ccb])
            k.op("dve", lambda: nc.vector.tensor_tensor(out=Dg[:, :, :], in0=w00[:, :].unsqueeze(2).to_broadcast([16, 8, 16]),
                                                       in1=csb[0:16, C_ID:C_ID + 16].unsqueeze(1).to_broadcast([16, 8, 16]),
                                                       op=ALU.mult), reads=[w00b, cbuf], writes=[Dgb])

            for p in range(2):
                norm_pass(p, C_NMIX, i)
                ntile = 9 if p == 1 else 8
                def lnbufs(q4):
                    return (T[0], Tb[0], T[1], Tb[1]) if q4 % 2 == 0 else (T[4], Tb[4], T[5], Tb[5])

                def lnload(q4):
                    tg_, tgb_, tb_, tbb_ = lnbufs(q4)
                    k.dma("sp", tg_[:, :], lnraw[j, 0, q4 * 512:(q4 + 1) * 512].partition_broadcast(128), writes=[tgb_])
                    k.dma("sp", tb_[:, :], lnraw[j, 1, q4 * 512:(q4 + 1) * 512].partition_broadcast(128), writes=[tbb_])
                if p == 1:
                    lnload(0)
                    lnload(1)
                for vc in range(4):
                    wt, wb = wload(WBV[j, vc].rearrange("p a c -> p (a c)"), 4096)
                    wv = wt[:, 0:4096].rearrange("p (a c) -> p a c", c=512)
                    for lt in range(ntile):
                        M = 128 if lt < 8 else 16
                        hb = hnTb[lt // 4]
                        pp, ppb = next_ab()
                        def mv(lt=lt, M=M, pp=pp):
                            for kt in range(8):
                                ins = nc.tensor.matmul(pp[0:M, :], lhsT=hnT[:, kt, lt * 128:lt * 128 + M], rhs=wv[:, kt, :], start=(kt == 0), stop=(kt == 7))
                            return ins
                        k.op("pe", mv, reads=[wb, hb], writes=[ppb])
                        tix = 2 + (lt % 2)
                        k.op("act", lambda M=M, pp=pp, lt=lt, tix=tix: nc.scalar.activation(out=T[tix][0:M, :], in_=pp[0:M, :], func=AF.Gelu_apprx_tanh, accum_out=st1[0:M, lt, vc:vc + 1]),
                             reads=[ppb], writes=[Tb[tix], stb[lt]])
                        k.op("act", lambda M=M, lt=lt, tix=tix: nc.scalar.activation(out=junk[0:M, :], in_=T[tix][0:M, :], func=AF.Square, accum_out=st2[0:M, lt, vc:vc + 1]),
                             reads=[Tb[tix]], writes=[junkb, stb[lt]])
                        k.op("dve", lambda M=M, lt=lt, tix=tix: nc.vector.tensor_copy(out=vraw[0:M, lt, vc * 512:(vc + 1) * 512], in_=T[tix][0:M, :]), reads=[Tb[tix]], writes=[vrawb[lt]])
                        if p == 1 and lt >= 7:
                            oi = lt - 7
                            k.op("dve", lambda M=M, oi=oi, tix=tix: nc.vector.tensor_copy(out=vo[oi][0:M, vc * 512:(vc + 1) * 512], in_=T[tix][0:M, :]), reads=[Tb[tix]], writes=[vob[oi]])
                for lt in range(ntile):
                    M = 128 if lt < 8 else 16
                    def st_a(lt=lt, M=M):
                        nc.vector.tensor_reduce(out=mr[0:M, lt, 0:1], in_=st1[0:M, lt, :], axis=mybir.AxisListType.X, op=ALU.add)
                        return nc.vector.tensor_reduce(out=mr[0:M, lt, 1:2], in_=st2[0:M, lt, :], axis=mybir.AxisListType.X, op=ALU.add)
                    k.op("dve", st_a, reads=[stb[lt]], writes=[mrb[lt]])
                    k.op("dve", lambda lt=lt, M=M: nc.vector.tensor_scalar(out=mr[0:M, lt, 0:2], in0=mr[0:M, lt, 0:2], scalar1=1.0 / 2048.0, scalar2=None, op0=ALU.mult),
                         reads=[mrb[lt]], writes=[mrb[lt]])
                    k.op("dve", lambda lt=lt, M=M: nc.vector.tensor_tensor(out=mr[0:M, lt, 2:3], in0=mr[0:M, lt, 0:1], in1=mr[0:M, lt, 0:1], op=ALU.mult),
                         reads=[mrb[lt]], writes=[mrb[lt]])
                    k.op("dve", lambda lt=lt, M=M: nc.vector.tensor_tensor(out=mr[0:M, lt, 2:3], in0=mr[0:M, lt, 1:2], in1=mr[0:M, lt, 2:3], op=ALU.subtract),
                         reads=[mrb[lt]], writes=[mrb[lt]])
                rsqrt(mr[:, 0:ntile, 3], mr[:, 0:ntile, 2], mrb[0:ntile], mrb[0])
                for lt in range(ntile):
                    M = 128 if lt < 8 else 16
                    k.op("dve", lambda lt=lt, M=M: nc.vector.tensor_scalar(out=vraw[0:M, lt, :], in0=vraw[0:M, lt, :], scalar1=mr[0:M, lt, 0:1], scalar2=mr[0:M, lt, 3:4], op0=ALU.subtract, op1=ALU.mult),
                         reads=[mrb[lt], mrb[0]], writes=[vrawb[lt]])
                    if p == 1 and lt >= 7:
                        oi = lt - 7
                        k.op("dve", lambda lt=lt, M=M, oi=oi: nc.vector.tensor_scalar(out=vo[oi][0:M, :], in0=vo[oi][0:M, :], scalar1=mr[0:M, lt, 0:1], scalar2=mr[0:M, lt, 3:4], op0=ALU.subtract, op1=ALU.mult),
                             reads=[mrb[lt], mrb[0]], writes=[vob[oi]])
                if p == 1:
                    for q4 in range(4):
                        tg_, tgb_, tb_, tbb_ = lnbufs(q4)
                        for oi, M in ((0, 128), (1, 16)):
                            k.op("dve", lambda M=M, oi=oi, q4=q4, tg_=tg_: nc.vector.tensor_tensor(out=vo[oi][0:M, q4 * 512:(q4 + 1) * 512], in0=vo[oi][0:M, q4 * 512:(q4 + 1) * 512], in1=tg_[0:M, :], op=ALU.mult),
                                 reads=[tgb_], writes=[vob[oi]])
                            k.op("dve", lambda M=M, oi=oi, q4=q4, tb_=tb_: nc.vector.tensor_tensor(out=vo[oi][0:M, q4 * 512:(q4 + 1) * 512], in0=vo[oi][0:M, q4 * 512:(q4 + 1) * 512], in1=tb_[0:M, :], op=ALU.add),
                                 reads=[tbb_], writes=[vob[oi]])
                        if q4 + 2 < 4:
                            lnload(q4 + 2)
                    k.dma("sp", cvp[j, :, :], vo[0][:, :], reads=[vob[0]])
                    k.dma("sp", cvs[j, :, :], vo[1][0:16, :], reads=[vob[1]])
                for g in range(8):
                    wt, wb = wload(WBUZ[j, g].rearrange("p a c -> p (a c)"), 4096)
                    wv = wt[:, 0:4096].rearrange("p (a c) -> p a c", c=512)
                    gt, gtb = gTt[(g // 2) % 2], gTb[(g // 2) % 2]
                    gk = 2 * (g % 2)
                    for half in range(2):
                        kt16 = 2 * g + half
                        for (cb, n, bi) in pass_blocks(p):
                            lc = lcol(p, cb)
                            hb = hnTb[hn_idx(p, cb)]
                            rot["n"] += 1
                            r = rot["n"] % 2
                            pu, pub = (pA, pAb) if r == 0 else (pD[0], pDb[0])
                            pz, pzb = (pB, pBb) if r == 0 else (pD[1], pDb[1])
                            tu, tub = (T[4], Tb[4]) if r == 0 else (T[2], Tb[2])
                            tz, tzb = (T[5], Tb[5]) if r == 0 else (T[3], Tb[3])
                            def mu(lc=lc, n=n, o=half * 128, pp=pu):
                                for kt in range(8):
                                    ins = nc.tensor.matmul(pp[:, 0:n], lhsT=wv[:, kt, o:o + 128], rhs=hnT[:, kt, lc:lc + n], start=(kt == 0), stop=(kt == 7))
                                return ins
                            k.op("pe", mu, reads=[wb, hb], writes=[pub])
                            k.op("pe", lambda lc=lc, n=n, half=half: mu(lc, n, 256 + half * 128, pz), reads=[wb, hb], writes=[pzb])
                            k.op("act", lambda n=n: nc.scalar.activation(out=tu[:, 0:n], in_=pu[:, 0:n], func=AF.Gelu_apprx_tanh), reads=[pub], writes=[tub])
                            k.op("act", lambda n=n: nc.scalar.activation(out=tz[:, 0:n], in_=pz[:, 0:n], func=AF.Tanh, scale=0.5), reads=[pzb], writes=[tzb])
                            k.op("dve", lambda n=n: nc.vector.scalar_tensor_tensor(out=tz[:, 0:n], in0=tz[:, 0:n], scalar=1.0, in1=pz[:, 0:n], op0=ALU.add, op1=ALU.mult), reads=[tzb, pzb], writes=[tzb])
                            k.op("dve", lambda n=n, lc=lc, half=half: nc.vector.scalar_tensor_tensor(out=gt[:, gk + half, lc:lc + n], in0=tz[:, 0:n], scalar=0.5, in1=tu[:, 0:n], op0=ALU.mult, op1=ALU.mult),
                                 reads=[tub, tzb], writes=[gtb])
                        for q in range(2):
                            rot["n"] += 1
                            r = rot["n"] % 2
                            psp, pspb = (pAtt, pAttb) if r == 0 else (pO, pOb)
                            tsp, tspb = (T[0], Tb[0]) if r == 0 else (T[1], Tb[1])
                            def msp(q=q, kt16=kt16, psp=psp):
                                for t4 in range(4):
                                    lt = q * 4 + t4
                                    ins = nc.tensor.matmul(psp[:, t4 * 128:(t4 + 1) * 128], lhsT=vraw[:, lt, kt16 * 128:(kt16 + 1) * 128], rhs=WmT[:, g, :], start=True, stop=True)
                                return ins
                            k.op("pe", msp, reads=[vrawb[q * 4 + t] for t in range(4)] + [WmTb], writes=[pspb])
                            lc = q * 512
                            k.op("dve", lambda kt16=kt16, psp=psp, tsp=tsp: nc.vector.scalar_tensor_tensor(out=tsp[:, :].rearrange("p (t x) -> p t x", x=128), in0=psp[:, :].rearrange("p (t x) -> p t x", x=128),
                                                                                       scalar=csb[:, C_LNG + j * 16 + kt16:C_LNG + j * 16 + kt16 + 1],
                                                                                       in1=Cc[:, kt16, :].unsqueeze(1).to_broadcast([128, 4, 128]), op0=ALU.mult, op1=ALU.add),
                                 reads=[pspb, Ccb, cbuf], writes=[tspb])
                            k.op("dve", lambda lc=lc, half=half, tsp=tsp: nc.vector.tensor_tensor(out=gt[:, gk + half, lc:lc + 512], in0=tsp[:, :], in1=gt[:, gk + half, lc:lc + 512], op=ALU.mult),
                                 reads=[tspb], writes=[gtb])
                        if p == 1:
                            k.op("pe", lambda kt16=kt16: nc.tensor.matmul(pM[:, 0:16], lhsT=vraw[0:16, 8, kt16 * 128:(kt16 + 1) * 128], rhs=Dg[:, g, :], start=True, stop=True),
                                 reads=[vrawb[8], Dgb], writes=[pMb])
                            k.op("dve", lambda kt16=kt16: nc.vector.scalar_tensor_tensor(out=T[1][:, 0:16], in0=pM[:, 0:16], scalar=csb[:, C_LNG + j * 16 + kt16:C_LNG + j * 16 + kt16 + 1],
                                                                                       in1=Cc[:, kt16, 0:1].to_broadcast([128, 16]), op0=ALU.mult, op1=ALU.add),
                                 reads=[pMb, Ccb, cbuf], writes=[Tb[1]])
                            k.op("dve", lambda half=half: nc.vector.tensor_tensor(out=gt[:, gk + half, 1024:1040], in0=T[1][:, 0:16], in1=gt[:, gk + half, 1024:1040], op=ALU.mult),
                                 reads=[Tb[1]], writes=[gtb])
                    if g % 2 == 1:
                        outproj(p, WOB[j, g - 1:g + 1], gt, gtb)
                ple_pass(p, i)

        for i in range(depth_run):
            j = i // 2
            with ExitStack() as es2:
                if i % 2 == 0:
                    layer_A(i, j, es2)
                else:
                    layer_B(i, j, es2)
                barrier()

        for (cb, n, bi) in [(0, 512, 0), (512, 512, 1), (1024, 512, 2), (1536, 512, 3), (2048, 16, 4)]:
            for kt in range(8):
                tb, tbb = T[kt % 2], Tb[kt % 2]
                k.op("act", lambda kt=kt, tb=tb, n=n, cb=cb: nc.scalar.activation(out=tb[:, 0:n], in_=hT[:, kt, cb:cb + n], func=AF.Square), reads=[hTb[kt][bi]], writes=[tbb])
                k.op("pe", lambda kt=kt, tb=tb, n=n: nc.tensor.matmul(pM[:, 0:n], lhsT=onesN, rhs=tb[:, 0:n], start=(kt == 0), stop=(kt == 7)), reads=[tbb, cbuf], writes=[pMb])
            rsqrt(rstd[:, 0:n], pM[:, 0:n], [pMb], rstdb)
            for kt in range(8):
                k.op("dve", lambda kt=kt, n=n, cb=cb: nc.vector.scalar_tensor_tensor(out=hT[:, kt, cb:cb + n], in0=hT[:, kt, cb:cb + n], scalar=csb[:, C_NFIN + kt:C_NFIN + kt + 1],
                                                                                  in1=rstd[:, 0:n], op0=ALU.mult, op1=ALU.mult), reads=[rstdb, cbuf], writes=[hTb[kt][bi]])
            k.dma("sp", yT[:, :, cb:cb + n], hT[:, :, cb:cb + n], reads=[hTb[kt][bi] for kt in range(8)])
        k.finish()
    return nc


def _prep(inputs):
    f = lambda a: np.ascontiguousarray(np.asarray(a, dtype=np.float32))
    w_in_a = f(inputs["w_in_a"]).reshape(2, 8, 128, 6144)
    q = w_in_a[..., 0:1024].reshape(2, 8, 128, 8, 128)
    fz = w_in_a[..., 1024:2048].reshape(2, 8, 128, 8, 128)
    v = w_in_a[..., 2048:4096].reshape(2, 8, 128, 8, 256)
    z = w_in_a[..., 4096:6144].reshape(2, 8, 128, 8, 256)
    WAQF = f(np.concatenate([q, fz], axis=-1).transpose(0, 3, 2, 1, 4))
    WAVZ = f(np.concatenate([v, z], axis=-1).transpose(0, 3, 2, 1, 4))
    WOA = f(f(inputs["w_out_a"]).reshape(2, 8, 2, 128, 1024).transpose(0, 1, 3, 2, 4))
    w_in_b = f(inputs["w_in_b"]).reshape(2, 8, 128, 6144)
    WBV = f(w_in_b[..., 2048:4096].reshape(2, 8, 128, 4, 512).transpose(0, 3, 2, 1, 4))
    u = w_in_b[..., 0:2048].reshape(2, 8, 128, 8, 256)
    zb = w_in_b[..., 4096:6144].reshape(2, 8, 128, 8, 256)
    WBUZ = f(np.concatenate([u, zb], axis=-1).transpose(0, 3, 2, 1, 4))
    WOB = f(f(inputs["w_out_b"]).reshape(2, 8, 2, 128, 1024).transpose(0, 1, 3, 2, 4))
    WG = f(f(inputs["w_ple_gate"]).reshape(4, 8, 128, 2, 512).transpose(0, 3, 2, 1, 4))
    WP = f(f(inputs["w_ple_proj"]).reshape(4, 2, 128, 1024).transpose(0, 2, 1, 3))
    cst = np.zeros((128, C_TOT), np.float32)
    cst[:, C_NMIX:C_NMIX + 32] = f(inputs["norm_mix"]).reshape(4, 8, 128).transpose(2, 0, 1).reshape(128, 32)
    cst[:, C_NPLE:C_NPLE + 32] = f(inputs["norm_ple"]).reshape(4, 8, 128).transpose(2, 0, 1).reshape(128, 32)
    cst[:, C_NFIN:C_NFIN + 8] = f(inputs["norm_final"]).reshape(8, 128).T
    cst[:, C_LBL:C_LBL + 16] = f(inputs["lb_logits"]).reshape(2, 8, 128).transpose(2, 0, 1).reshape(128, 16)
    cst[:, C_GN:C_GN + 4] = f(inputs["gnorm_a"]).reshape(2, 2, 128).transpose(2, 0, 1).reshape(128, 4)
    cst[:, C_LNG:C_LNG + 32] = f(inputs["ln_v_g"]).reshape(2, 16, 128).transpose(2, 0, 1).reshape(128, 32)
    cst[:, C_LNB:C_LNB + 32] = f(inputs["ln_v_b"]).reshape(2, 16, 128).transpose(2, 0, 1).reshape(128, 32)
    cst[:, C_ID:C_ID + 128] = np.eye(128, dtype=np.float32)
    s = np.arange(128)[:, None]
    t = np.arange(128)[None, :]
    cst[:, C_MT:C_MT + 128] = ((s <= t) & (s // 64 == t // 64)).astype(np.float32)
    cst[:, C_MB:C_MB + 128] = (s <= t).astype(np.float32)
    sm = np.zeros(512, np.float32)
    sm[::64] = 1.0
    cst[:, C_SM:C_SM + 512] = sm[None, :]
    cst[:, C_ON:C_ON + 128] = 1.0 / 1024.0
    cst[:, C_I16:C_I16 + 256] = np.eye(16, dtype=np.float32).reshape(1, 256)
    lnraw = f(np.stack([f(inputs["ln_v_g"]), f(inputs["ln_v_b"])], axis=1))
    wspT = f(f(inputs["w_spatial"]).transpose(0, 3, 1, 2))
    bspr = f(f(inputs["b_spatial"]).reshape(2, 1024))
    w00r = f(f(inputs["w_spatial"])[:, :, 0, 0])
    shared = dict(cst=cst, lnraw=lnraw, wspT=wspT, bspr=bspr, w00r=w00r, WAQF=WAQF, WAVZ=WAVZ, WOA=WOA, WBV=WBV, WBUZ=WBUZ, WOB=WOB, WG=WG, WP=WP)
    xp = f(inputs["x_prompt"]); xs = f(inputs["x_sample"])
    pp = f(inputs["p_prompt"]); psm = f(inputs["p_sample"])
    st = f(inputs["state_hgrn"])
    maps = []
    for c in range(NCORES):
        xa = np.concatenate([xp[c], xs[16 * c:16 * c + 16, 0]], axis=0)
        xT = f(xa.T.reshape(8, 128, NT).transpose(1, 0, 2))
        pa = np.concatenate([pp[:, c], psm[:, 16 * c:16 * c + 16, 0]], axis=1)
        pT = f(pa.transpose(0, 2, 1).reshape(4, 2, 128, NT).transpose(0, 2, 1, 3))
        sti = f(st[:, 16 * c:16 * c + 16].transpose(0, 2, 3, 1, 4))
        m = dict(shared)
        m.update(xT=xT, pT=pT, st_in=sti)
        maps.append(m)
    return maps


_NC_CACHE = {}


def kernel(**inputs):
    maps = _prep(inputs)
    if "nc" not in _NC_CACHE:
        _NC_CACHE["nc"] = build()
    nc = _NC_CACHE["nc"]
    res = run_bass_kernel_spmd(nc, maps, core_ids=list(range(NCORES)))
    R = res.results
    y_prompt = np.zeros((8, 2048, 1024), np.float32)
    y_sample = np.zeros((128, 1, 1024), np.float32)
    sp = np.zeros((2, 8, 8, 128, 256), np.float32)
    ss = np.zeros((2, 128, 8, 128, 256), np.float32)
    cvp = np.zeros((2, 8, 128, 2048), np.float32)
    cvs = np.zeros((2, 128, 1, 2048), np.float32)
    for c in range(NCORES):
        yT = R[c]["yT"]
        ya = yT.transpose(2, 1, 0).reshape(NT, 1024)
        y_prompt[c] = ya[:2048]
        y_sample[16 * c:16 * c + 16, 0] = ya[2048:]
        sp[:, c] = R[c]["sp_out"]
        ss[:, 16 * c:16 * c + 16] = R[c]["ss_out"].transpose(0, 3, 1, 2, 4)
        cvp[:, c] = R[c]["cvp"]
        cvs[:, 16 * c:16 * c + 16, 0] = R[c]["cvs"]
    return (y_prompt, y_sample, sp, ss, cvp, cvs)
```

```python
from contextlib import ExitStack
import numpy as np
import concourse.bass as bass
import concourse.mybir as mybir
from concourse.bass_utils import run_bass_kernel_spmd

F32 = mybir.dt.float32
BF16 = mybir.dt.bfloat16
AF = mybir.ActivationFunctionType
ALU = mybir.AluOpType

NCORES = 8
NTOK = 2048
NS = 16
NT = NTOK + NS
EPS = 1e-6
PW = 1040

C_NMIX, C_NPLE, C_NFIN, C_LBL, C_GN, C_LNG, C_LNB = 0, 32, 64, 72, 88, 92, 124
C_ID, C_MT, C_MB, C_SM, C_ON, C_I16 = 156, 284, 412, 540, 1052, 1180
C_TOT = 1436


class Ev:
    __slots__ = ("sem", "val")

    def __init__(self, sem, val):
        self.sem, self.val = sem, val


class Buf:
    __slots__ = ("name", "w", "rs", "dsem", "dcnt", "excl")

    def __init__(self, name, excl=False):
        self.name, self.w, self.rs, self.dsem, self.dcnt, self.excl = name, None, {}, None, 0, excl


class Eng:
    def __init__(self, name, h, sem):
        self.name, self.h, self.sem, self.cnt, self.seen = name, h, sem, 0, {}

    def wait(self, ev):
        if ev is None:
            return
        if self.seen.get(ev.sem.num, 0) < ev.val:
            self.h.wait_ge(ev.sem, ev.val)
            self.seen[ev.sem.num] = ev.val


class K:
    def __init__(self, nc, es):
        self.nc, self.es = nc, es
        self.E = {}
        for name, h in (("pe", nc.tensor), ("act", nc.scalar), ("dve", nc.vector),
                        ("pool", nc.gpsimd), ("sp", nc.sync)):
            self.E[name] = Eng(name, h, es.enter_context(nc.semaphore("s_" + name)))
        self.stores = []
        self.nb = 0

    def buf(self, name=None, excl=False):
        self.nb += 1
        return Buf(name or "b%d" % self.nb, excl)

    def _deps(self, E, reads, writes):
        for b in reads:
            if b.w is not None and not (E.name == "pe" and b.w.sem is E.sem):
                E.wait(b.w)
        for b in writes:
            if b.w is not None and not (E.name == "pe" and b.w.sem is E.sem):
                E.wait(b.w)
            for ev in b.rs.values():
                if not (E.name == "pe" and ev.sem is E.sem):
                    E.wait(ev)

    def op(self, eng, fn, reads=(), writes=()):
        E = self.E[eng]
        ex = [b for b in reads if b.excl]
        if ex:
            reads = [b for b in reads if not b.excl]
            writes = list(writes) + [b for b in ex if b not in writes]
        self._deps(E, reads, writes)
        ins = fn()
        E.cnt += 1
        assert E.cnt < 60000
        ins.then_inc(E.sem, 1)
        ev = Ev(E.sem, E.cnt)
        for b in reads:
            b.rs[E.sem.num] = ev
        for b in writes:
            b.w = ev
            b.rs = {}
        return ev

    def dma(self, q, out, in_, reads=(), writes=()):
        E = self.E[q]
        self._deps(E, reads, writes)
        pb = writes[0] if writes else reads[0]
        if pb.dsem is None:
            self.nb += 1
            pb.dsem = self.es.enter_context(self.nc.semaphore("d%d_%s" % (self.nb, pb.name)))
        pb.dcnt += 16
        assert pb.dcnt < 60000
        E.h.dma_start(out=out, in_=in_).then_inc(pb.dsem, 16)
        ev = Ev(pb.dsem, pb.dcnt)
        for b in reads:
            b.rs[pb.dsem.num] = ev
        for b in writes:
            b.w = ev
            b.rs = {}
        if not writes:
            self.stores.append(ev)
        return ev

    def finish(self):
        E = self.E["sp"]
        for ev in self.stores:
            E.wait(ev)


def build(depth_run=4):
    nc = bass.Bass("TRN2", target_bir_lowering=False)
    D = {}

    def din(name, shape):
        D[name] = nc.dram_tensor(name, shape, F32, kind="ExternalInput").ap()
        return D[name]

    def dout(name, shape):
        D[name] = nc.dram_tensor(name, shape, F32, kind="ExternalOutput").ap()
        return D[name]

    xT = din("xT", [128, 8, NT])
    pTd = din("pT", [4, 128, 2, NT])
    st_in = din("st_in", [2, 8, 128, 16, 256])
    cst = din("cst", [128, C_TOT])
    lnraw = din("lnraw", [2, 2, 2048])
    wspT = din("wspT", [2, 128, 8, 128])
    bspr = din("bspr", [2, 1024])
    w00r = din("w00r", [2, 8])
    WAQF = din("WAQF", [2, 8, 128, 8, 256])
    WAVZ = din("WAVZ", [2, 8, 128, 8, 512])
    WOA = din("WOA", [2, 8, 128, 2, 1024])
    WBV = din("WBV", [2, 4, 128, 8, 512])
    WBUZ = din("WBUZ", [2, 8, 128, 8, 512])
    WOB = din("WOB", [2, 8, 128, 2, 1024])
    WG = din("WG", [4, 2, 128, 8, 512])
    WP = din("WP", [4, 128, 2, 1024])
    yT = dout("yT", [128, 8, NT])
    sp_out = dout("sp_out", [2, 8, 128, 256])
    ss_out = dout("ss_out", [2, 8, 128, 16, 256])
    cvp = dout("cvp", [2, 128, 2048])
    cvs = dout("cvs", [2, 16, 2048])

    es = ExitStack()
    with es:
        k = K(nc, es)

        def sb(name, shape, dt=F32):
            return es.enter_context(nc.sbuf_tensor(name, shape, dt))

        def ps(name, shape, dt=F32):
            return es.enter_context(nc.psum_tensor(name, shape, dt))

        hT = sb("hT", [128, 8, NT])
        hTb = [[k.buf("hT%d_%d" % (kt, b)) for b in range(5)] for kt in range(8)]
        hnT = sb("hnT", [128, 8, PW], BF16)
        hnTb = [k.buf("hnT%d" % b) for b in range(3)]
        csb = sb("csb", [128, C_TOT])
        cbuf = k.buf("csb")
        wring = [sb("wr%d" % i, [128, 4096], BF16) for i in range(3)]
        wrb = [k.buf("wr%d" % i) for i in range(3)]
        wstate = {"n": 0}
        T = [sb("T%d" % i, [128, 512]) for i in range(6)]
        Tb = [k.buf("T%d" % i) for i in range(6)]
        rstd = T[5]; rstdb = Tb[5]
        pTs = sb("pTs", [128, 2, PW], BF16); pTb = k.buf("pTs")
        cbf = sb("cbf", [128, 128], BF16)
        cbfb = k.buf("cbf")
        lbv = sb("lbv", [128, 3, 2, 8]); lbb = k.buf("lbv")
        pA = ps("pA", [128, 512]); pB = ps("pB", [128, 512])
        pD = [ps("pD0", [128, 512]), ps("pD1", [128, 512])]
        pAtt = ps("pAtt", [128, 512]); pO = ps("pO", [128, 512])
        pTr = ps("pTr", [128, 1024], BF16); pM = ps("pM", [128, 512])
        pAb, pBb, pAttb, pOb, pTrb, pMb = (k.buf(n, True) for n in ("pA", "pB", "pAtt", "pO", "pTr", "pM"))
        pDb = [k.buf("pD%d" % i, True) for i in range(2)]
        pAttq = [pAttb] * 4
        pD3b = [pDb[0], pDb[1], pTrb]
        pOq = [pOb] * 2
        pAB = [(pA, pAb), (pB, pBb)]
        abst = {"n": 0}
        rot = {"n": 0}

        def next_ab():
            abst["n"] += 1
            return pAB[abst["n"] % 2]

        ident_f = csb[:, C_ID:C_ID + 128]
        maskT = csb[:, C_MT:C_MT + 128]
        maskB = csb[:, C_MB:C_MB + 128]
        scanm = csb[:, C_SM:C_SM + 512]
        onesN = csb[:, C_ON:C_ON + 128]
        id16 = csb[:, C_I16:C_I16 + 256].rearrange("p (a b) -> p a b", b=16)
        identb = cbf[:, 0:128]

        def cvec(base, i, n):
            return csb[:, base + i * n: base + (i + 1) * n]

        k.dma("sp", csb[:], cst[:, :], writes=[cbuf])
        for bi_, (cb_, n_) in enumerate([(0, 512), (512, 512), (1024, 512), (1536, 512), (2048, 16)]):
            k.dma("sp", hT[:, :, cb_:cb_ + n_], xT[:, :, cb_:cb_ + n_], writes=[hTb[kt][bi_] for kt in range(8)])
        k.op("dve", lambda: nc.vector.tensor_copy(out=identb, in_=ident_f), reads=[cbuf], writes=[cbfb])
        def _lb():
            nc.vector.memset(lbv[:, 0, 0, :], 0.0)
            return nc.vector.tensor_tensor(out=lbv[:, 0, 1, :], in0=csb[:, C_LBL + 8:C_LBL + 16], in1=csb[:, C_LBL:C_LBL + 8], op=ALU.subtract)
        k.op("dve", _lb, reads=[cbuf], writes=[lbb])
        k.op("act", lambda: nc.scalar.activation(out=lbv[:, 0, 1, :], in_=lbv[:, 0, 1, :], func=AF.Sigmoid), reads=[lbb], writes=[lbb])
        def _lb2():
            nc.vector.tensor_scalar(out=lbv[:, 1, :, :], in0=lbv[:, 0, :, :], scalar1=-0.5, scalar2=0.5, op0=ALU.mult, op1=ALU.add)
            return nc.vector.tensor_scalar(out=lbv[:, 2, :, :], in0=lbv[:, 0, :, :], scalar1=0.5, scalar2=0.5, op0=ALU.mult, op1=ALU.add)
        k.op("dve", _lb2, reads=[lbb], writes=[lbb])

        def wload(src_ap, ncols_total, view3=None, slot=None):
            if slot is None:
                i = wstate["n"] % 3
                wstate["n"] += 1
            else:
                i = slot
            dst = wring[i][:, 0:ncols_total]
            if view3:
                dst = dst.rearrange("p (g c) -> p g c", c=view3)
            k.dma("pool", dst, src_ap, writes=[wrb[i]])
            return wring[i], wrb[i]

        def pass_blocks(p):
            bl = [(p * 1024, 512, 2 * p), (p * 1024 + 512, 512, 2 * p + 1)]
            if p == 1:
                bl.append((2048, 16, 4))
            return bl

        def lcol(p, cb):
            return cb - p * 1024

        def hn_idx(p, cb):
            return (cb - p * 1024) // 512

        def rsqrt(out, in_, rbufs, wbuf):
            k.op("act", lambda: nc.scalar.activation(out=out, in_=in_, func=AF.Sqrt, bias=EPS, scale=1.0), reads=rbufs, writes=[wbuf])
            k.op("dve", lambda: nc.vector.reciprocal(out=out, in_=out), reads=[wbuf], writes=[wbuf])

        def norm_pass(p, wbase, wi):
            for (cb, n, bi) in pass_blocks(p):
                for kt in range(8):
                    tb, tbb = T[kt % 2], Tb[kt % 2]
                    k.op("act", lambda kt=kt, tb=tb: nc.scalar.activation(out=tb[:, 0:n], in_=hT[:, kt, cb:cb + n], func=AF.Square),
                         reads=[hTb[kt][bi]], writes=[tbb])
                    k.op("pe", lambda kt=kt, tb=tb: nc.tensor.matmul(pM[:, 0:n], lhsT=onesN, rhs=tb[:, 0:n], start=(kt == 0), stop=(kt == 7)),
                         reads=[tbb, cbuf], writes=[pMb])
                rsqrt(rstd[:, 0:n], pM[:, 0:n], [pMb], rstdb)
                lc = lcol(p, cb)
                hb = hnTb[hn_idx(p, cb)]
                for kt in range(8):
                    k.op("dve", lambda kt=kt: nc.vector.scalar_tensor_tensor(
                        out=hnT[:, kt, lc:lc + n], in0=hT[:, kt, cb:cb + n], scalar=csb[:, wbase + wi * 8 + kt: wbase + wi * 8 + kt + 1],
                        in1=rstd[:, 0:n], op0=ALU.mult, op1=ALU.mult), reads=[hTb[kt][bi], rstdb, cbuf], writes=[hb])

        def outproj(p, wsrc, gt, gtb, slot=None, pre=None):
            if pre is not None:
                wt, wb = pre
            else:
                wt, wb = wload(wsrc.rearrange("g p a c -> p g (a c)"), 4096, view3=2048, slot=slot)
            wv = wt[:, 0:4096].rearrange("p (a c) -> p a c", c=1024)
            for m in range(8):
                for (cb, n, bi) in pass_blocks(p):
                    if n == 16:
                        continue
                    lc = lcol(p, cb)
                    pp, ppb = next_ab()
                    def mm(pp=pp, lc=lc, n=n, m=m):
                        for a4 in range(4):
                            ins = nc.tensor.matmul(pp[:, 0:n], lhsT=wv[:, a4, m * 128:(m + 1) * 128], rhs=gt[:, a4, lc:lc + n], start=(a4 == 0), stop=(a4 == 3))
                        return ins
                    k.op("pe", mm, reads=[wb, gtb], writes=[ppb])
                    k.op("dve", lambda pp=pp, n=n, m=m, cb=cb: nc.vector.tensor_tensor(out=hT[:, m, cb:cb + n], in0=hT[:, m, cb:cb + n], in1=pp[:, 0:n], op=ALU.add),
                         reads=[ppb], writes=[hTb[m][bi]])
            if p == 1:
                pp, ppb = next_ab()
                def mms(pp=pp):
                    for m in range(8):
                        for a4 in range(4):
                            ins = nc.tensor.matmul(pp[:, m * 16:(m + 1) * 16], lhsT=wv[:, a4, m * 128:(m + 1) * 128], rhs=gt[:, a4, 1024:1040], start=(a4 == 0), stop=(a4 == 3))
                    return ins
                k.op("pe", mms, reads=[wb, gtb], writes=[ppb])
                k.op("dve", lambda pp=pp: nc.vector.tensor_tensor(out=hT[:, :, 2048:2064], in0=hT[:, :, 2048:2064], in1=pp[:, 0:128].rearrange("p (m x) -> p m x", x=16), op=ALU.add),
                     reads=[ppb], writes=[hTb[m][4] for m in range(8)])

        def ple_pass(p, i):
            norm_pass(p, C_NPLE, i)
            c0 = p * 1024
            ncol = PW if p == 1 else 1024
            k.dma("pool", pTs[:, :, 0:ncol], pTd[i, :, :, c0:c0 + ncol], writes=[pTb])
            wpt, wpb = wload(WP[i].rearrange("p a c -> p (a c)"), 2048)
            wpv = wpt[:, 0:2048].rearrange("p (a c) -> p a c", c=1024)
            wgl = []
            for ch in range(2):
                wgt, wgb = wload(WG[i, ch].rearrange("p a c -> p (a c)"), 4096)
                wgv = wgt[:, 0:4096].rearrange("p (a c) -> p a c", c=512)
                wgl.append((wgv, wgb))
                for mm_ in range(4):
                    m = ch * 4 + mm_
                    for (cb, n, bi) in pass_blocks(p):
                        if n == 16:
                            continue
                        lc = lcol(p, cb)
                        hb = hnTb[hn_idx(p, cb)]
                        rot["n"] += 1
                        r = rot["n"] % 2
                        pg, pgb = (pA, pAb) if r == 0 else (pB, pBb)
                        pq, pqb = pD[r], pDb[r]
                        tg_, tgb_ = (T[2], Tb[2]) if r == 0 else (T[4], Tb[4])
                        tm_, tmb_ = (T[3], Tb[3]) if r == 0 else (T[0], Tb[0])
                        def mg(lc=lc, n=n, mm_=mm_, pg=pg):
                            for kt in range(8):
                                ins = nc.tensor.matmul(pg[:, 0:n], lhsT=wgv[:, kt, mm_ * 128:(mm_ + 1) * 128], rhs=hnT[:, kt, lc:lc + n], start=(kt == 0), stop=(kt == 7))
                            return ins
                        k.op("pe", mg, reads=[wgb, hb], writes=[pgb])
                        def mp(lc=lc, n=n, m=m, pq=pq):
                            nc.tensor.matmul(pq[:, 0:n], lhsT=wpv[:, 0, m * 128:(m + 1) * 128], rhs=pTs[:, 0, lc:lc + n], start=True, stop=False)
                            return nc.tensor.matmul(pq[:, 0:n], lhsT=wpv[:, 1, m * 128:(m + 1) * 128], rhs=pTs[:, 1, lc:lc + n], start=False, stop=True)
                        k.op("pe", mp, reads=[wpb, pTb], writes=[pqb])
                        k.op("act", lambda n=n: nc.scalar.activation(out=tg_[:, 0:n], in_=pg[:, 0:n], func=AF.Tanh, scale=0.5), reads=[pgb], writes=[tgb_])
                        k.op("dve", lambda n=n: nc.vector.scalar_tensor_tensor(out=tm_[:, 0:n], in0=tg_[:, 0:n], scalar=1.0, in1=pq[:, 0:n], op0=ALU.add, op1=ALU.mult), reads=[tgb_, pqb], writes=[tmb_])
                        k.op("dve", lambda n=n, m=m, cb=cb: nc.vector.scalar_tensor_tensor(out=hT[:, m, cb:cb + n], in0=tm_[:, 0:n], scalar=0.5, in1=hT[:, m, cb:cb + n], op0=ALU.mult, op1=ALU.add),
                             reads=[tmb_], writes=[hTb[m][bi]])
            if p == 1:
                hb = hnTb[2]
                def mgs():
                    for m in range(8):
                        wgv_, _ = wgl[m // 4]
                        for kt in range(8):
                            ins = nc.tensor.matmul(pA[:, m * 16:(m + 1) * 16], lhsT=wgv_[:, kt, (m % 4) * 128:(m % 4 + 1) * 128], rhs=hnT[:, kt, 1024:1040], start=(kt == 0), stop=(kt == 7))
                    return ins
                k.op("pe", mgs, reads=[wgl[0][1], wgl[1][1], hb], writes=[pAb])
                def mps():
                    for m in range(8):
                        nc.tensor.matmul(pB[:, m * 16:(m + 1) * 16], lhsT=wpv[:, 0, m * 128:(m + 1) * 128], rhs=pTs[:, 0, 1024:1040], start=True, stop=False)
                        ins = nc.tensor.matmul(pB[:, m * 16:(m + 1) * 16], lhsT=wpv[:, 1, m * 128:(m + 1) * 128], rhs=pTs[:, 1, 1024:1040], start=False, stop=True)
                    return ins
                k.op("pe", mps, reads=[wpb, pTb], writes=[pBb])
                k.op("act", lambda: nc.scalar.activation(out=T[2][:, 0:128], in_=pA[:, 0:128], func=AF.Tanh, scale=0.5), reads=[pAb], writes=[Tb[2]])
                k.op("dve", lambda: nc.vector.scalar_tensor_tensor(out=T[3][:, 0:128], in0=T[2][:, 0:128], scalar=1.0, in1=pB[:, 0:128], op0=ALU.add, op1=ALU.mult), reads=[Tb[2], pBb], writes=[Tb[3]])
                k.op("dve", lambda: nc.vector.scalar_tensor_tensor(out=hT[:, :, 2048:2064], in0=T[3][:, 0:128].rearrange("p (m x) -> p m x", x=16), scalar=0.5, in1=hT[:, :, 2048:2064], op0=ALU.mult, op1=ALU.add),
                     reads=[Tb[3]], writes=[hTb[m][4] for m in range(8)])

        def barrier():
            evs = {}
            for en in ("pe", "act", "dve", "pool", "sp"):
                E = k.E[en]
                if E.cnt > 0:
                    evs[E.sem.num] = Ev(E.sem, E.cnt)
            for ev in k.stores:
                if ev.sem.num not in evs or evs[ev.sem.num].val < ev.val:
                    evs[ev.sem.num] = ev
            for en in ("pe", "act", "dve", "pool", "sp"):
                for ev in evs.values():
                    k.E[en].wait(ev)

        def layer_A(i, j, es2):
            def sb2(name, shape, dt=F32):
                return es2.enter_context(nc.sbuf_tensor(name + "_L%d" % i, shape, dt))
            gTt = [sb2("gT%d" % t, [128, 4, PW], BF16) for t in range(2)]; gTb = [k.buf("gT%d" % t) for t in range(2)]
            TA = [sb2("TA%d" % t, [128, 512]) for t in range(4)]; TAb = [k.buf("TA%d" % t) for t in range(4)]
            qtil = sb2("qtil", [128, 1024], BF16); qtilb = k.buf("qtil")
            qpad = sb2("qpad", [128, 16, 128], BF16); qpadb = k.buf("qpad")
            ktil = sb2("ktil", [128, 1024], BF16); ktilb = k.buf("ktil")
            kdec = sb2("kdec", [128, 1024], BF16); kdecb = k.buf("kdec")
            dch = sb2("dch", [128, 16]); dchb = k.buf("dch")
            vh = sb2("vh", [128, 8, 256], BF16); vhb = [k.buf("vh%d" % t) for t in range(8)]
            szh = sb2("szh", [128, 8, 256], BF16); szhb = [k.buf("szh%d" % t) for t in range(8)]
            vs = sb2("vs", [16, 256], BF16); vsb = k.buf("vs")
            szs = sb2("szs", [16, 256], BF16); szsb = k.buf("szs")
            kT = sb2("kT", [128, 8, 128], BF16); kTb = k.buf("kT")
            attm = [sb2("attm%d" % t, [128, 128], BF16) for t in range(2)]; attmb = [k.buf("attm%d" % t) for t in range(2)]
            Sf = sb2("Sf", [128, 8, 256]); Sfb = [k.buf("Sf%d" % h) for h in range(8)]
            Sx = sb2("Sx", [128, 256]); Sxb = k.buf("Sx")
            NSB = 6
            Sb = sb2("Sb", [128, NSB, 256], BF16); Sbb = [k.buf("Sb%d" % t) for t in range(NSB)]
            junk = sb2("junk", [128, 256], BF16); junkb = k.buf("junk")
            og = sb2("og", [128, 8, 256], BF16); ogs = sb2("ogs", [16, 256], BF16); ogb = k.buf("og"); ogb2 = k.buf("og2")
            ssq = sb2("ssq", [128, 32]); ssqb = k.buf("ssq")
            k.op("dve", lambda: nc.vector.memset(ssq[:, :], 1.0), writes=[ssqb])
            qS2 = sb2("qS", [128, 2, 16]); fS2 = sb2("fS", [128, 2, 16]); kkS2 = sb2("kkS", [128, 2, 16]); smb2 = [k.buf("smp0"), k.buf("smp1")]
            TS = sb2("TS", [128, 4, 16]); TSb = [k.buf("TS%d" % t) for t in range(4)]
            kkST = sb2("kkST", [16, 128]); kkSTb = k.buf("kkST")
            Am3 = sb2("Am3", [16, 3, 128], BF16); Am3b = [k.buf("Am3_%d" % t) for t in range(3)]
            Qp = sb2("Qp", [128, 16, 16], BF16); Qpb = k.buf("Qp")
            qf16 = sb2("qf16", [128, 16]); qf16b = k.buf("qf16")
            qk = sb2("qk", [16, 4]); qkb = k.buf("qk")
            osb = sb2("osb", [16, 256]); osbb = k.buf("osb")
            NSM = 6
            ssm = [sb2("ssm%d" % t, [128, 2, 256]) for t in range(NSM)]; ssmb = [[k.buf("ssm%d_%d" % (t, u)) for u in range(2)] for t in range(NSM)]
            snb = [sb2("snb%d" % t, [128, 256], BF16) for t in range(3)]; snbb = [k.buf("snb%d" % t) for t in range(3)]

            def _z():
                nc.vector.memset(qpad[:], 0.0)
                return nc.vector.memset(Sf[:], 0.0)
            k.op("dve", _z, writes=[qpadb] + Sfb)
            c1_ = lambda h: lbv[:, 1, j, h:h + 1]
            c2_ = lambda h: lbv[:, 2, j, h:h + 1]

            for p in range(2):
                norm_pass(p, C_NMIX, i)
                def make_head(h):
                    W = {}
                    qS, fS, kkS, smb = qS2[:, h % 2, :], fS2[:, h % 2, :], kkS2[:, h % 2, :], smb2[h % 2]
                    gt, gtb = gTt[(h // 2) % 2], gTb[(h // 2) % 2]
                    gk = 2 * (h % 2)
                    blks = pass_blocks(p)

                    def tset(bix):
                        if bix == 2:
                            return TS[:, 0, :], TSb[0], TS[:, 1, :], TSb[1], TS[:, 2, :], TSb[2], TS[:, 3, :], TSb[3]
                        if bix % 2 == 0:
                            return T[2], Tb[2], T[3], Tb[3], T[4], Tb[4], T[5], Tb[5]
                        return TA[0], TAb[0], TA[1], TAb[1], TA[2], TAb[2], TA[3], TAb[3]

                    def qf_s1(bix, part=0):
                        cb, n, bi = blks[bix]
                        lc = lcol(p, cb)
                        hb = hnTb[hn_idx(p, cb)]
                        Tq, Tqb, Tk, Tkb, Tl, Tlb, Te, Teb = tset(bix)
                        pq_, pqb_ = (pTr[:, 0:1024].bitcast(F32), pTrb) if part == 1 else (pA, pAb)
                        def mq(o=0, pp=pq_):
                            for kt in range(8):
                                ins = nc.tensor.matmul(pp[:, 0:n], lhsT=W["wv"][:, kt, o:o + 128], rhs=hnT[:, kt, lc:lc + n], start=(kt == 0), stop=(kt == 7))
                            return ins
                        if part in (0, 1):
                            k.op("pe", mq, reads=[W["wb"], hb], writes=[pqb_])
                            k.op("pe", lambda: mq(128, pB), reads=[W["wb"], hb], writes=[pBb])
                            k.op("act", lambda: nc.scalar.activation(out=Tq[:, 0:n], in_=pq_[:, 0:n], func=AF.Silu), reads=[pqb_], writes=[Tqb])
                            k.op("act", lambda: nc.scalar.activation(out=Tk[:, 0:n], in_=pB[:, 0:n], func=AF.Tanh, scale=0.5), reads=[pBb], writes=[Tkb])
                            k.op("act", lambda: nc.scalar.activation(out=Tl[:, 0:n], in_=Tk[:, 0:n], func=AF.Identity, bias=c2_(h), scale=c1_(h)),
                                 reads=[Tkb, lbb], writes=[Tlb])
                            if n == 512:
                                k.op("pool", lambda: nc.gpsimd.tensor_tensor(out=Te[:, :], in0=Tl[:, :], in1=scanm, op=ALU.mult), reads=[Tlb, cbuf], writes=[Teb])
                        if n == 512:
                            if part in (0, 2):
                                k.op("dve", lambda: nc.vector.tensor_tensor_scan(out=Tk[:, :], data0=Tl[:, :], data1=Te[:, :], initial=0.0, op0=ALU.mult, op1=ALU.max),
                                     reads=[Tlb, Teb], writes=[Tkb])
                                k.op("pool", lambda: nc.gpsimd.tensor_scalar(out=Tl[:, :], in0=Tl[:, :], scalar1=-1.0, scalar2=1.0, op0=ALU.mult, op1=ALU.add), reads=[Tlb], writes=[Tlb])
                            if part in (0, 3):
                                k.op("dve", lambda: nc.vector.reciprocal(out=Te[:, :], in_=Tk[:, :]), reads=[Tkb], writes=[Teb])
                                k.op("pool", lambda: nc.gpsimd.tensor_tensor(out=Tl[:, :], in0=Tl[:, :], in1=Te[:, :], op=ALU.mult), reads=[Tlb, Teb], writes=[Tlb])
                        elif part in (0, 1):
                            def smp_():
                                nc.vector.tensor_copy(out=qS, in_=Tq[:, 0:16])
                                nc.vector.tensor_copy(out=fS, in_=Tl[:, 0:16])
                                return nc.vector.tensor_scalar(out=kkS, in0=Tl[:, 0:16], scalar1=-1.0, scalar2=1.0, op0=ALU.mult, op1=ALU.add)
                            k.op("dve", smp_, reads=[Tqb, Tlb], writes=[smb])

                    def qf_s2(bix):
                        cb, n, bi = blks[bix]
                        if n != 512:
                            return
                        lc = lcol(p, cb)
                        Tq, Tqb, Tk, Tkb, Tl, Tlb, Te, Teb = tset(bix)
                        k.op("pool", lambda: nc.gpsimd.tensor_tensor(out=qtil[:, lc:lc + 512], in0=Tq[:, :], in1=Tk[:, :], op=ALU.mult),
                             reads=[Tqb, Tkb], writes=[qtilb])
                        c0 = lc // 64
                        def qp():
                            src = qtil[:, lc:lc + 512].rearrange("p (t a x) -> p t a x", a=2, x=64)
                            dst = qpad[:, c0:c0 + 8, :].rearrange("p (t a) (b x) -> p t a b x", a=2, b=2)
                            nc.scalar.copy(out=dst[:, :, 0, 0, :], in_=src[:, :, 0, :])
                            return nc.scalar.copy(out=dst[:, :, 1, 1, :], in_=src[:, :, 1, :])
                        k.op("act", qp, reads=[qtilb], writes=[qpadb])
                        k.op("act", lambda: nc.scalar.copy(out=ktil[:, lc:lc + 512], in_=Tl[:, :]), reads=[Tlb], writes=[ktilb])
                        def kd():
                            ebl = Tk[:, :].rearrange("p (c t) -> p c t", t=64)[:, :, 63:64].to_broadcast([128, 8, 64])
                            return nc.vector.tensor_tensor(out=kdec[:, lc:lc + 512].rearrange("p (c t) -> p c t", t=64),
                                                           in0=Tl[:, :].rearrange("p (c t) -> p c t", t=64), in1=ebl, op=ALU.mult)
                        k.op("dve", kd, reads=[Tlb, Tkb], writes=[kdecb])
                        k.op("pool", lambda: nc.gpsimd.tensor_copy(out=dch[:, c0:c0 + 8], in_=Tk[:, :].rearrange("p (c t) -> p c t", t=64)[:, :, 63]),
                             reads=[Tkb], writes=[dchb])

                    ntile = 9 if p == 1 else 8
                    vzbanks = [(pO, pOb), (pM, pMb), (pAtt, pAttb)]

                    def vz(lt):
                        if lt >= ntile or W.get(("vz", lt)):
                            return
                        W[("vz", lt)] = True
                        M = 128 if lt < 8 else 16
                        hb = hnTb[lt // 4]
                        if lt < 3 and not W.get("early"):
                            pp, ppb = vzbanks[lt]
                        else:
                            pp, ppb = pA, pAb
                        def mvz():
                            for kt in range(8):
                                ins = nc.tensor.matmul(pp[0:M, :], lhsT=hnT[:, kt, lt * 128:lt * 128 + M], rhs=W["wv2"][:, kt, :], start=(kt == 0), stop=(kt == 7))
                            return ins
                        k.op("pe", mvz, reads=[W["wb2"], hb], writes=[ppb])
                        if lt < 8:
                            k.op("dve", lambda: nc.vector.tensor_copy(out=vh[:, lt, :], in_=pp[:, 0:256]), reads=[ppb], writes=[vhb[lt]])
                            k.op("act", lambda: nc.scalar.activation(out=szh[:, lt, :], in_=pp[:, 256:512], func=AF.Silu), reads=[ppb], writes=[szhb[lt]])
                        else:
                            k.op("dve", lambda: nc.vector.tensor_copy(out=vs[:, :], in_=pp[0:16, 0:256]), reads=[ppb], writes=[vsb])
                            k.op("act", lambda: nc.scalar.activation(out=szs[:, :], in_=pp[0:16, 256:512], func=AF.Silu), reads=[ppb], writes=[szsb])


                    def preq():
                        wt, W["wb"] = wload(WAQF[j, h].rearrange("p a c -> p (a c)"), 2048, slot=2)
                        W["wv"] = wt[:, 0:2048].rearrange("p (a c) -> p a c", c=256)

                    def pre():
                        wt2, W["wb2"] = wload(WAVZ[j, h].rearrange("p a c -> p (a c)"), 4096, slot=h % 2)
                        W["wv2"] = wt2[:, 0:4096].rearrange("p (a c) -> p a c", c=512)

                    def prew():
                        W["wo"] = wload(WOA[j, h - 1:h + 1].rearrange("g p a c -> p g (a c)"), 4096, view3=2048, slot=h % 2)

                    def stl():
                        if p == 1:
                            for g2 in range(NSM):
                                bi_ = (8 * h + g2) % NSM
                                k.dma("sp", ssm[bi_][:, :, :], st_in[j, h, :, g2 * 2:(g2 + 1) * 2, :], writes=ssmb[bi_])

                    def s1all():
                        for bix in range(len(blks)):
                            qf_s1(bix)

                    def vz_early():
                        W["early"] = True
                        vz(0)
                        vz(1)
                        W["early"] = False

                    def mida():
                        qf_s2(0)
                        qf_s2(1)
                        if p == 1:
                            vz(8)
                        vz(0)
                        vz(1)
                        vz(2)

                    def mid(nxt, nxt2=None):
                        def trs():
                            for lt in range(8):
                                ins = nc.tensor.transpose(out=pTr[:, lt * 128:(lt + 1) * 128], in_=kdec[:, lt * 128:(lt + 1) * 128], identity=identb)
                            return ins
                        k.op("pe", trs, reads=[kdecb, cbfb], writes=[pTrb])
                        k.op("act", lambda: nc.scalar.copy(out=kT[:, :, :], in_=pTr[:, 0:1024].rearrange("p (t c) -> p t c", c=128)), reads=[pTrb], writes=[kTb])
                        k.op("act", lambda: nc.scalar.copy(out=Sb[:, 0, :], in_=Sf[:, h, :]), reads=[Sfb[h]], writes=[Sbb[0]])

                        def stage_a(lt):
                            for a in range(2):
                                c = 2 * lt + a
                                dsl = c % 2
                                pd = pD[dsl][:, 0:256]
                                k.op("pe", lambda lt=lt, a=a, pd=pd: nc.tensor.matmul(pd, lhsT=kT[a * 64:(a + 1) * 64, lt, :], rhs=vh[a * 64:(a + 1) * 64, lt, :], start=True, stop=True),
                                     reads=[kTb, vhb[lt]], writes=[pD3b[dsl]])
                                if c % 2 == 0:
                                    ssrc, ssrcb, sdst, sdstb = Sf[:, h, :], Sfb[h], Sx[:, :], Sxb
                                else:
                                    ssrc, ssrcb, sdst, sdstb = Sx[:, :], Sxb, Sf[:, h, :], Sfb[h]
                                k.op("dve", lambda c=c, pd=pd, ssrc=ssrc, sdst=sdst: nc.vector.scalar_tensor_tensor(out=sdst, in0=ssrc, scalar=dch[:, c:c + 1], in1=pd, op0=ALU.mult, op1=ALU.add),
                                     reads=[pD3b[dsl], dchb, ssrcb], writes=[sdstb])
                                if c < 15:
                                    k.op("act", lambda c=c, sdst=sdst: nc.scalar.copy(out=Sb[:, (c + 1) % NSB, :], in_=sdst), reads=[sdstb], writes=[Sbb[(c + 1) % NSB]])
                            k.op("pe", lambda lt=lt: nc.tensor.matmul(pAtt[:, 0:128], lhsT=ktil[:, lt * 128:(lt + 1) * 128], rhs=qtil[:, lt * 128:(lt + 1) * 128], start=True, stop=True),
                                 reads=[ktilb, qtilb], writes=[pAttb])
                            k.op("dve", lambda lt=lt: nc.vector.tensor_tensor(out=attm[lt % 2][:, :], in0=pAtt[:, 0:128], in1=maskT, op=ALU.mult),
                                 reads=[pAttb, cbuf], writes=[attmb[lt % 2]])

                        def obank(lt):
                            return (pO, pOb) if lt % 2 == 0 else (pM, pMb)

                        def stage_b(lt):
                            sl = lt % 2
                            c = 2 * lt
                            po_t, pob = obank(lt)
                            po = po_t[:, 0:256]
                            def mo():
                                nc.tensor.matmul(po, lhsT=attm[sl][:, :], rhs=vh[:, lt, :], start=True, stop=False)
                                nc.tensor.matmul(po, lhsT=qpad[:, c, :], rhs=Sb[:, c % NSB, :], start=False, stop=False)
                                return nc.tensor.matmul(po, lhsT=qpad[:, c + 1, :], rhs=Sb[:, (c + 1) % NSB, :], start=False, stop=True)
                            k.op("pe", mo, reads=[attmb[sl], vhb[lt], qpadb, Sbb[c % NSB], Sbb[(c + 1) % NSB]], writes=[pob])
                            post_b(po, pob, szh[:, lt, :], szhb[lt], 128, lt)

                        def post_b(po, pob, sz, szb, M, lt):
                            k.op("act", lambda: nc.scalar.activation(out=junk[0:M, :], in_=po[0:M, :], func=AF.Square, scale=1.0 / 16.0, accum_out=ssq[0:M, lt:lt + 1]),
                                 reads=[pob], writes=[junkb, ssqb])
                            ogd = og[0:M, lt, :] if lt < 8 else ogs[0:M, :]
                            k.op("dve", lambda: nc.vector.tensor_tensor(out=ogd, in0=po[0:M, :], in1=sz[0:M, :] if M < 128 else sz, op=ALU.mult),
                                 reads=[pob, szb], writes=[ogb if (lt < 4 or lt == 8) else ogb2])

                        def tail(nt):
                            rsqrt(ssq[:, 16:16 + nt], ssq[:, 0:nt], [ssqb], ssqb)
                            k.op("dve", lambda: nc.vector.tensor_tensor(out=og[:, 0:4, :], in0=og[:, 0:4, :], in1=ssq[:, 16:20].unsqueeze(2).to_broadcast([128, 4, 256]), op=ALU.mult),
                                 reads=[ssqb], writes=[ogb])
                            k.op("pool", lambda: nc.gpsimd.tensor_tensor(out=og[:, 4:8, :], in0=og[:, 4:8, :], in1=ssq[:, 20:24].unsqueeze(2).to_broadcast([128, 4, 256]), op=ALU.mult),
                                 reads=[ssqb], writes=[ogb2])
                            if nt == 9:
                                k.op("dve", lambda: nc.vector.tensor_scalar(out=ogs[:, :], in0=ogs[:, :], scalar1=ssq[0:16, 24:25], scalar2=None, op0=ALU.mult),
                                     reads=[ssqb], writes=[ogb])
                            for q4 in range(2):
                                def tg():
                                    for t4 in range(4):
                                        for half in range(2):
                                            ins = nc.tensor.transpose(out=pTr[:, (t4 * 2 + half) * 128:(t4 * 2 + half + 1) * 128], in_=og[:, q4 * 4 + t4, half * 128:(half + 1) * 128], identity=identb)
                                    return ins
                                k.op("pe", tg, reads=[ogb if q4 == 0 else ogb2, cbfb], writes=[pTrb])
                                def cg():
                                    src = pTr[:, 0:1024].rearrange("p (t a x) -> p t a x", a=2, x=128)
                                    for half in range(2):
                                        ins = nc.scalar.activation(out=gt[:, gk + half, q4 * 512:(q4 + 1) * 512].rearrange("p (t x) -> p t x", x=128), in_=src[:, :, half, :], func=AF.Copy,
                                                                   scale=csb[:, C_GN + j * 2 + half:C_GN + j * 2 + half + 1])
                                    return ins
                                k.op("act", cg, reads=[pTrb, cbuf], writes=[gtb])
                            if nt == 9:
                                def tgs():
                                    nc.tensor.transpose(out=pTr[:, 0:16], in_=ogs[0:16, 0:128], identity=identb[0:16, 0:16])
                                    return nc.tensor.transpose(out=pTr[:, 128:144], in_=ogs[0:16, 128:256], identity=identb[0:16, 0:16])
                                k.op("pe", tgs, reads=[ogb, cbfb], writes=[pTrb])
                                def cgs():
                                    nc.scalar.activation(out=gt[:, gk, 1024:1040], in_=pTr[:, 0:16], func=AF.Copy, scale=csb[:, C_GN + j * 2:C_GN + j * 2 + 1])
                                    return nc.scalar.activation(out=gt[:, gk + 1, 1024:1040], in_=pTr[:, 128:144], func=AF.Copy, scale=csb[:, C_GN + j * 2 + 1:C_GN + j * 2 + 2])
                                k.op("act", cgs, reads=[pTrb, cbuf], writes=[gtb])

                        W["tail"] = tail
                        if p == 1:
                            pOS = pTr[0:16, 0:512].bitcast(F32)
                            k.op("pe", lambda: nc.tensor.transpose(out=pM[0:16, 0:128], in_=kkS, identity=ident_f), reads=[smb, cbuf], writes=[pMb])
                            k.op("dve", lambda: nc.vector.tensor_copy(out=kkST[:, :], in_=pM[0:16, 0:128]), reads=[pMb], writes=[kkSTb])
                            k.op("dve", lambda: nc.vector.tensor_tensor(out=qf16[:, :], in0=qS, in1=fS, op=ALU.mult), reads=[smb], writes=[qf16b])
                            k.op("dve", lambda: nc.vector.tensor_tensor(out=Qp[:, :, :], in0=qf16[:, :].unsqueeze(2).to_broadcast([128, 16, 16]), in1=id16, op=ALU.mult),
                                 reads=[qf16b, cbuf], writes=[Qpb])
                            k.op("dve", lambda: nc.vector.tensor_tensor(out=qf16[:, :], in0=qS, in1=kkS, op=ALU.mult), reads=[smb, Qpb], writes=[qf16b])
                            k.op("pe", lambda: nc.tensor.matmul(pM[0:16, 0:1], lhsT=qf16[:, :], rhs=csb[:, C_ON:C_ON + 1], start=True, stop=True), reads=[qf16b, cbuf], writes=[pMb])
                            k.op("dve", lambda: nc.vector.tensor_scalar(out=qk[:, 0:1], in0=pM[0:16, 0:1], scalar1=1024.0, scalar2=None, op0=ALU.mult), reads=[pMb], writes=[qkb])

                        def mk_am(b):
                            k.op("dve", lambda: nc.vector.tensor_scalar(out=Am3[:, b % 3, :], in0=kkST[:, :], scalar1=csb[0:16, C_ID + b:C_ID + b + 1], scalar2=None, op0=ALU.mult),
                                 reads=[kkSTb, cbuf], writes=[Am3b[b % 3]])

                        def samp(b):
                            g2, bb = b // 2, b % 2
                            sm_, smb_ = ssm[(8 * h + g2) % NSM], ssmb[(8 * h + g2) % NSM]
                            pd = pD[b % 2][:, 0:256]
                            pdb = pDb[b % 2]
                            k.op("act", lambda: nc.scalar.copy(out=snb[b % 3][:, :], in_=sm_[:, bb, :]), reads=[smb_[bb]], writes=[snbb[b % 3]])
                            k.op("pe", lambda: nc.tensor.matmul(pOS, lhsT=Qp[:, b, :], rhs=snb[b % 3][:, :], start=(b == 0), stop=(b == 15)),
                                 reads=[Qpb, snbb[b % 3]], writes=[pTrb])
                            k.op("pe", lambda: nc.tensor.matmul(pd, lhsT=Am3[:, b % 3, :], rhs=vs[:, :], start=True, stop=True), reads=[Am3b[b % 3], vsb], writes=[pdb])
                            if b + 2 < 16:
                                mk_am(b + 2)
                            k.op("dve", lambda: nc.vector.scalar_tensor_tensor(out=sm_[:, bb, :], in0=sm_[:, bb, :], scalar=fS[:, b:b + 1], in1=pd, op0=ALU.mult, op1=ALU.add),
                                 reads=[pdb, smb], writes=[smb_[bb]])
                            if bb == 1:
                                k.dma("sp", ss_out[j, h, :, g2 * 2:(g2 + 1) * 2, :], sm_[:, :, :], reads=smb_)
                                if g2 + NSM < 8:
                                    k.dma("sp", sm_[:, :, :], st_in[j, h, :, (g2 + NSM) * 2:(g2 + NSM + 1) * 2, :], writes=smb_)

                        for step in range(9):
                            if step < 8:
                                stage_a(step)
                            if 0 <= step - 1 < 8:
                                stage_b(step - 1)
                            vz(step + 3)
                            if step == 6 and h % 2 == 1:
                                prew()
                            if step == 7 and nxt2 is not None:
                                nxt2["preq"]()
                            if step == 7 and nxt is not None:
                                nxt["vz_early"]()
                            if step == 0:
                                yield
                            if nxt is not None:
                                if step == 0:
                                    nxt["pre"]()
                                elif step == 1:
                                    nxt["s1"](0, 1)
                                elif step == 2:
                                    nxt["s1"](0, 2)
                                elif step == 3:
                                    nxt["s1"](0, 3)
                                elif step == 4:
                                    nxt["s1"](1, 1)
                                elif step == 5:
                                    nxt["s1"](1, 2)
                                elif step == 6:
                                    nxt["s1"](1, 3)
                                    if p == 1:
                                        nxt["s1"](2)
                        if p == 1:
                            mk_am(0)
                            mk_am(1)
                            for b in range(16):
                                samp(b)
                        if p == 1:
                            k.dma("sp", sp_out[j, h, :, :], Sf[:, h, :], reads=[Sfb[h]])
                            k.op("dve", lambda: nc.vector.scalar_tensor_tensor(out=osb[:, :], in0=vs[:, :], scalar=qk[:, 0:1], in1=pOS, op0=ALU.mult, op1=ALU.add),
                                 reads=[vsb, qkb, pTrb], writes=[osbb])
                            post_b(osb[:, :], osbb, szs, szsb, 16, 8)
                            if nxt is not None:
                                nxt["stl"]()

                    def post():
                        W["tail"](9 if p == 1 else 8)
                        if h % 2 == 1:
                            outproj(p, WOA[j, h - 1:h + 1], gt, gtb, pre=W["wo"])

                    return {"mida": mida, "vz_early": vz_early, "pre": pre, "preq": preq, "stl": stl, "s1": qf_s1, "s1all": s1all, "mid": mid, "post": post}

                heads = [make_head(h) for h in range(8)]
                heads[0]["preq"]()
                heads[0]["pre"]()
                heads[0]["stl"]()
                heads[0]["s1all"]()
                heads[1]["preq"]()
                for h in range(8):
                    heads[h]["mida"]()
                    gen = heads[h]["mid"](heads[h + 1] if h < 7 else None, heads[h + 2] if h < 6 else None)
                    next(gen)
                    if h > 0:
                        heads[h - 1]["post"]()
                    for _ in gen:
                        pass
                heads[7]["post"]()
                ple_pass(p, i)

        def layer_B(i, j, es2):
            def sb2(name, shape, dt=F32):
                return es2.enter_context(nc.sbuf_tensor(name + "_L%d" % i, shape, dt))
            vraw = sb2("vraw", [128, 9, 2048], BF16); vrawb = [k.buf("vraw%d" % t) for t in range(9)]
            vo = [sb2("vo%d" % t, [128, 2048]) for t in range(2)]; vob = [k.buf("vo%d" % t) for t in range(2)]
            st1 = sb2("st1", [128, 9, 4]); st2 = sb2("st2", [128, 9, 4]); stb = [k.buf("st%d" % t) for t in range(9)]
            mr = sb2("mr", [128, 9, 4]); mrb = [k.buf("mr%d" % t) for t in range(9)]
            k.op("dve", lambda: nc.vector.memset(mr[:, :, :], 1.0), writes=mrb)
            gTt = [sb2("gT0", [128, 4, PW], BF16)] * 2; gTb = [k.buf("gT0")] * 2
            Cc = sb2("Cc", [128, 16, 128]); Ccb = k.buf("Cc")
            WmT = sb2("WmT", [128, 8, 128], BF16); WmTb = k.buf("WmT")
            wsf = vo[0][:, 0:1024].rearrange("p (a b) -> p a b", b=128); wsfb = vob[0]
            bsb = vo[0][:, 1024:2048].rearrange("p (a b) -> p a b", b=128); bsbb = vob[0]
            Rr = vo[1][:, 0:1024].rearrange("p (a b) -> p a b", b=128); Rrb = vob[1]
            w00 = sb2("w00", [16, 8]); w00b = k.buf("w00")
            Dg = sb2("Dg", [16, 8, 16], BF16); Dgb = k.buf("Dg")
            junk = sb2("junkB", [128, 512], BF16); junkb = k.buf("junkB")
            k.dma("sp", wsf, wspT[j, :, :, :], writes=[wsfb])
            k.dma("sp", vo[0][:, 1024:2048], bspr[j, :].partition_broadcast(128), writes=[bsbb])
            k.dma("sp", w00[:, :], w00r[j, :].partition_broadcast(16), writes=[w00b])
            def mk():
                return nc.vector.tensor_tensor(out=wsf, in0=wsf, in1=maskB.unsqueeze(1).to_broadcast([128, 8, 128]), op=ALU.mult)
            k.op("dve", mk, reads=[cbuf], writes=[wsfb])
            k.op("dve", lambda: nc.vector.tensor_copy(out=WmT[:, :, :], in_=wsf), reads=[wsfb], writes=[WmTb])
            for half in range(2):
                k.op("pe", lambda half=half: nc.tensor.matmul(pM[:, 0:512], lhsT=onesN, rhs=vo[0][:, half * 512:(half + 1) * 512], start=True, stop=True),
                     reads=[wsfb, cbuf], writes=[pMb])
                k.op("dve", lambda half=half: nc.vector.tensor_scalar(out=vo[1][:, half * 512:(half + 1) * 512], in0=pM[:, 0:512], scalar1=1024.0, scalar2=None, op0=ALU.mult),
                     reads=[pMb], writes=[Rrb])
            def mkC():
                for kt in range(16):
                    ins = nc.vector.scalar_tensor_tensor(out=Cc[:, kt, :], in0=Rr[:, kt // 2, :], scalar=csb[:, C_LNB + j * 16 + kt:C_LNB + j * 16 + kt + 1], in1=bsb[:, kt // 2, :],
                                                         op0=ALU.mult, op1=ALU.add)
                return ins
            k.op("dve", mkC, reads=[Rrb, bsbb, cbuf], writes=[Ccb])
            k.op("dve", lambda: nc.vector.tensor_tensor(out=Dg[:, :, :], in0=w00[:, :].unsqueeze(2).to_broadcast([16, 8, 16]),
                                                       in1=csb[0:16, C_ID:C_ID + 16].unsqueeze(1).to_broadcast([16, 8, 16]),
                                                       op=ALU.mult), reads=[w00b, cbuf], writes=[Dgb])

            for p in range(2):
                norm_pass(p, C_NMIX, i)
                ntile = 9 if p == 1 else 8
                def lnbufs(q4):
                    return (T[0], Tb[0], T[1], Tb[1]) if q4 % 2 == 0 else (T[4], Tb[4], T[5], Tb[5])

                def lnload(q4):
                    tg_, tgb_, tb_, tbb_ = lnbufs(q4)
                    k.dma("sp", tg_[:, :], lnraw[j, 0, q4 * 512:(q4 + 1) * 512].partition_broadcast(128), writes=[tgb_])
                    k.dma("sp", tb_[:, :], lnraw[j, 1, q4 * 512:(q4 + 1) * 512].partition_broadcast(128), writes=[tbb_])
                if p == 1:
                    lnload(0)
                    lnload(1)
                for vc in range(4):
                    wt, wb = wload(WBV[j, vc].rearrange("p a c -> p (a c)"), 4096)
                    wv = wt[:, 0:4096].rearrange("p (a c) -> p a c", c=512)
                    for lt in range(ntile):
                        M = 128 if lt < 8 else 16
                        hb = hnTb[lt // 4]
                        pp, ppb = next_ab()
                        def mv(lt=lt, M=M, pp=pp):
                            for kt in range(8):
                                ins = nc.tensor.matmul(pp[0:M, :], lhsT=hnT[:, kt, lt * 128:lt * 128 + M], rhs=wv[:, kt, :], start=(kt == 0), stop=(kt == 7))
                            return ins
                        k.op("pe", mv, reads=[wb, hb], writes=[ppb])
                        tix = 2 + (lt % 2)
                        k.op("act", lambda M=M, pp=pp, lt=lt, tix=tix: nc.scalar.activation(out=T[tix][0:M, :], in_=pp[0:M, :], func=AF.Gelu_apprx_tanh, accum_out=st1[0:M, lt, vc:vc + 1]),
                             reads=[ppb], writes=[Tb[tix], stb[lt]])
                        k.op("act", lambda M=M, lt=lt, tix=tix: nc.scalar.activation(out=junk[0:M, :], in_=T[tix][0:M, :], func=AF.Square, accum_out=st2[0:M, lt, vc:vc + 1]),
                             reads=[Tb[tix]], writes=[junkb, stb[lt]])
                        k.op("dve", lambda M=M, lt=lt, tix=tix: nc.vector.tensor_copy(out=vraw[0:M, lt, vc * 512:(vc + 1) * 512], in_=T[tix][0:M, :]), reads=[Tb[tix]], writes=[vrawb[lt]])
                        if p == 1 and lt >= 7:
                            oi = lt - 7
                            k.op("dve", lambda M=M, oi=oi, tix=tix: nc.vector.tensor_copy(out=vo[oi][0:M, vc * 512:(vc + 1) * 512], in_=T[tix][0:M, :]), reads=[Tb[tix]], writes=[vob[oi]])
                for lt in range(ntile):
                    M = 128 if lt < 8 else 16
                    def st_a(lt=lt, M=M):
                        nc.vector.tensor_reduce(out=mr[0:M, lt, 0:1], in_=st1[0:M, lt, :], axis=mybir.AxisListType.X, op=ALU.add)
                        return nc.vector.tensor_reduce(out=mr[0:M, lt, 1:2], in_=st2[0:M, lt, :], axis=mybir.AxisListType.X, op=ALU.add)
                    k.op("dve", st_a, reads=[stb[lt]], writes=[mrb[lt]])
                    k.op("dve", lambda lt=lt, M=M: nc.vector.tensor_scalar(out=mr[0:M, lt, 0:2], in0=mr[0:M, lt, 0:2], scalar1=1.0 / 2048.0, scalar2=None, op0=ALU.mult),
                         reads=[mrb[lt]], writes=[mrb[lt]])
                    k.op("dve", lambda lt=lt, M=M: nc.vector.tensor_tensor(out=mr[0:M, lt, 2:3], in0=mr[0:M, lt, 0:1], in1=mr[0:M, lt, 0:1], op=ALU.mult),
                         reads=[mrb[lt]], writes=[mrb[lt]])
                    k.op("dve", lambda lt=lt, M=M: nc.vector.tensor_tensor(out=mr[0:M, lt, 2:3], in0=mr[0:M, lt, 1:2], in1=mr[0:M, lt, 2:3], op=ALU.subtract),
                         reads=[mrb[lt]], writes=[mrb[lt]])
                rsqrt(mr[:, 0:ntile, 3], mr[:, 0:ntile, 2], mrb[0:ntile], mrb[0])
                for lt in range(ntile):
                    M = 128 if lt < 8 else 16
                    k.op("dve", lambda lt=lt, M=M: nc.vector.tensor_scalar(out=vraw[0:M, lt, :], in0=vraw[0:M, lt, :], scalar1=mr[0:M, lt, 0:1], scalar2=mr[0:M, lt, 3:4], op0=ALU.subtract, op1=ALU.mult),
                         reads=[mrb[lt], mrb[0]], writes=[vrawb[lt]])
                    if p == 1 and lt >= 7:
                        oi = lt - 7
                        k.op("dve", lambda lt=lt, M=M, oi=oi: nc.vector.tensor_scalar(out=vo[oi][0:M, :], in0=vo[oi][0:M, :], scalar1=mr[0:M, lt, 0:1], scalar2=mr[0:M, lt, 3:4], op0=ALU.subtract, op1=ALU.mult),
                             reads=[mrb[lt], mrb[0]], writes=[vob[oi]])
                if p == 1:
                    for q4 in range(4):
                        tg_, tgb_, tb_, tbb_ = lnbufs(q4)
                        for oi, M in ((0, 128), (1, 16)):
                            k.op("dve", lambda M=M, oi=oi, q4=q4, tg_=tg_: nc.vector.tensor_tensor(out=vo[oi][0:M, q4 * 512:(q4 + 1) * 512], in0=vo[oi][0:M, q4 * 512:(q4 + 1) * 512], in1=tg_[0:M, :], op=ALU.mult),
                                 reads=[tgb_], writes=[vob[oi]])
                            k.op("dve", lambda M=M, oi=oi, q4=q4, tb_=tb_: nc.vector.tensor_tensor(out=vo[oi][0:M, q4 * 512:(q4 + 1) * 512], in0=vo[oi][0:M, q4 * 512:(q4 + 1) * 512], in1=tb_[0:M, :], op=ALU.add),
                                 reads=[tbb_], writes=[vob[oi]])
                        if q4 + 2 < 4:
                            lnload(q4 + 2)
                    k.dma("sp", cvp[j, :, :], vo[0][:, :], reads=[vob[0]])
                    k.dma("sp", cvs[j, :, :], vo[1][0:16, :], reads=[vob[1]])
                for g in range(8):
                    wt, wb = wload(WBUZ[j, g].rearrange("p a c -> p (a c)"), 4096)
                    wv = wt[:, 0:4096].rearrange("p (a c) -> p a c", c=512)
                    gt, gtb = gTt[(g // 2) % 2], gTb[(g // 2) % 2]
                    gk = 2 * (g % 2)
                    for half in range(2):
                        kt16 = 2 * g + half
                        for (cb, n, bi) in pass_blocks(p):
                            lc = lcol(p, cb)
                            hb = hnTb[hn_idx(p, cb)]
                            rot["n"] += 1
                            r = rot["n"] % 2
                            pu, pub = (pA, pAb) if r == 0 else (pD[0], pDb[0])
                            pz, pzb = (pB, pBb) if r == 0 else (pD[1], pDb[1])
                            tu, tub = (T[4], Tb[4]) if r == 0 else (T[2], Tb[2])
                            tz, tzb = (T[5], Tb[5]) if r == 0 else (T[3], Tb[3])
                            def mu(lc=lc, n=n, o=half * 128, pp=pu):
                                for kt in range(8):
                                    ins = nc.tensor.matmul(pp[:, 0:n], lhsT=wv[:, kt, o:o + 128], rhs=hnT[:, kt, lc:lc + n], start=(kt == 0), stop=(kt == 7))
                                return ins
                            k.op("pe", mu, reads=[wb, hb], writes=[pub])
                            k.op("pe", lambda lc=lc, n=n, half=half: mu(lc, n, 256 + half * 128, pz), reads=[wb, hb], writes=[pzb])
                            k.op("act", lambda n=n: nc.scalar.activation(out=tu[:, 0:n], in_=pu[:, 0:n], func=AF.Gelu_apprx_tanh), reads=[pub], writes=[tub])
                            k.op("act", lambda n=n: nc.scalar.activation(out=tz[:, 0:n], in_=pz[:, 0:n], func=AF.Tanh, scale=0.5), reads=[pzb], writes=[tzb])
                            k.op("dve", lambda n=n: nc.vector.scalar_tensor_tensor(out=tz[:, 0:n], in0=tz[:, 0:n], scalar=1.0, in1=pz[:, 0:n], op0=ALU.add, op1=ALU.mult), reads=[tzb, pzb], writes=[tzb])
                            k.op("dve", lambda n=n, lc=lc, half=half: nc.vector.scalar_tensor_tensor(out=gt[:, gk + half, lc:lc + n], in0=tz[:, 0:n], scalar=0.5, in1=tu[:, 0:n], op0=ALU.mult, op1=ALU.mult),
                                 reads=[tub, tzb], writes=[gtb])
                        for q in range(2):
                            rot["n"] += 1
                            r = rot["n"] % 2
                            psp, pspb = (pAtt, pAttb) if r == 0 else (pO, pOb)
                            tsp, tspb = (T[0], Tb[0]) if r == 0 else (T[1], Tb[1])
                            def msp(q=q, kt16=kt16, psp=psp):
                                for t4 in range(4):
                                    lt = q * 4 + t4
                                    ins = nc.tensor.matmul(psp[:, t4 * 128:(t4 + 1) * 128], lhsT=vraw[:, lt, kt16 * 128:(kt16 + 1) * 128], rhs=WmT[:, g, :], start=True, stop=True)
                                return ins
                            k.op("pe", msp, reads=[vrawb[q * 4 + t] for t in range(4)] + [WmTb], writes=[pspb])
                            lc = q * 512
                            k.op("dve", lambda kt16=kt16, psp=psp, tsp=tsp: nc.vector.scalar_tensor_tensor(out=tsp[:, :].rearrange("p (t x) -> p t x", x=128), in0=psp[:, :].rearrange("p (t x) -> p t x", x=128),
                                                                                       scalar=csb[:, C_LNG + j * 16 + kt16:C_LNG + j * 16 + kt16 + 1],
                                                                                       in1=Cc[:, kt16, :].unsqueeze(1).to_broadcast([128, 4, 128]), op0=ALU.mult, op1=ALU.add),
                                 reads=[pspb, Ccb, cbuf], writes=[tspb])
                            k.op("dve", lambda lc=lc, half=half, tsp=tsp: nc.vector.tensor_tensor(out=gt[:, gk + half, lc:lc + 512], in0=tsp[:, :], in1=gt[:, gk + half, lc:lc + 512], op=ALU.mult),
                                 reads=[tspb], writes=[gtb])
                        if p == 1:
                            k.op("pe", lambda kt16=kt16: nc.tensor.matmul(pM[:, 0:16], lhsT=vraw[0:16, 8, kt16 * 128:(kt16 + 1) * 128], rhs=Dg[:, g, :], start=True, stop=True),
                                 reads=[vrawb[8], Dgb], writes=[pMb])
                            k.op("dve", lambda kt16=kt16: nc.vector.scalar_tensor_tensor(out=T[1][:, 0:16], in0=pM[:, 0:16], scalar=csb[:, C_LNG + j * 16 + kt16:C_LNG + j * 16 + kt16 + 1],
                                                                                       in1=Cc[:, kt16, 0:1].to_broadcast([128, 16]), op0=ALU.mult, op1=ALU.add),
                                 reads=[pMb, Ccb, cbuf], writes=[Tb[1]])
                            k.op("dve", lambda half=half: nc.vector.tensor_tensor(out=gt[:, gk + half, 1024:1040], in0=T[1][:, 0:16], in1=gt[:, gk + half, 1024:1040], op=ALU.mult),
                                 reads=[Tb[1]], writes=[gtb])
                    if g % 2 == 1:
                        outproj(p, WOB[j, g - 1:g + 1], gt, gtb)
                ple_pass(p, i)

        for i in range(depth_run):
            j = i // 2
            with ExitStack() as es2:
                if i % 2 == 0:
                    layer_A(i, j, es2)
                else:
                    layer_B(i, j, es2)
                barrier()

        for (cb, n, bi) in [(0, 512, 0), (512, 512, 1), (1024, 512, 2), (1536, 512, 3), (2048, 16, 4)]:
            for kt in range(8):
                tb, tbb = T[kt % 2], Tb[kt % 2]
                k.op("act", lambda kt=kt, tb=tb, n=n, cb=cb: nc.scalar.activation(out=tb[:, 0:n], in_=hT[:, kt, cb:cb + n], func=AF.Square), reads=[hTb[kt][bi]], writes=[tbb])
                k.op("pe", lambda kt=kt, tb=tb, n=n: nc.tensor.matmul(pM[:, 0:n], lhsT=onesN, rhs=tb[:, 0:n], start=(kt == 0), stop=(kt == 7)), reads=[tbb, cbuf], writes=[pMb])
            rsqrt(rstd[:, 0:n], pM[:, 0:n], [pMb], rstdb)
            for kt in range(8):
                k.op("dve", lambda kt=kt, n=n, cb=cb: nc.vector.scalar_tensor_tensor(out=hT[:, kt, cb:cb + n], in0=hT[:, kt, cb:cb + n], scalar=csb[:, C_NFIN + kt:C_NFIN + kt + 1],
                                                                                  in1=rstd[:, 0:n], op0=ALU.mult, op1=ALU.mult), reads=[rstdb, cbuf], writes=[hTb[kt][bi]])
            k.dma("sp", yT[:, :, cb:cb + n], hT[:, :, cb:cb + n], reads=[hTb[kt][bi] for kt in range(8)])
        k.finish()
    return nc


def _prep(inputs):
    f = lambda a: np.ascontiguousarray(np.asarray(a, dtype=np.float32))
    w_in_a = f(inputs["w_in_a"]).reshape(2, 8, 128, 6144)
    q = w_in_a[..., 0:1024].reshape(2, 8, 128, 8, 128)
    fz = w_in_a[..., 1024:2048].reshape(2, 8, 128, 8, 128)
    v = w_in_a[..., 2048:4096].reshape(2, 8, 128, 8, 256)
    z = w_in_a[..., 4096:6144].reshape(2, 8, 128, 8, 256)
    WAQF = f(np.concatenate([q, fz], axis=-1).transpose(0, 3, 2, 1, 4))
    WAVZ = f(np.concatenate([v, z], axis=-1).transpose(0, 3, 2, 1, 4))
    WOA = f(f(inputs["w_out_a"]).reshape(2, 8, 2, 128, 1024).transpose(0, 1, 3, 2, 4))
    w_in_b = f(inputs["w_in_b"]).reshape(2, 8, 128, 6144)
    WBV = f(w_in_b[..., 2048:4096].reshape(2, 8, 128, 4, 512).transpose(0, 3, 2, 1, 4))
    u = w_in_b[..., 0:2048].reshape(2, 8, 128, 8, 256)
    zb = w_in_b[..., 4096:6144].reshape(2, 8, 128, 8, 256)
    WBUZ = f(np.concatenate([u, zb], axis=-1).transpose(0, 3, 2, 1, 4))
    WOB = f(f(inputs["w_out_b"]).reshape(2, 8, 2, 128, 1024).transpose(0, 1, 3, 2, 4))
    WG = f(f(inputs["w_ple_gate"]).reshape(4, 8, 128, 2, 512).transpose(0, 3, 2, 1, 4))
    WP = f(f(inputs["w_ple_proj"]).reshape(4, 2, 128, 1024).transpose(0, 2, 1, 3))
    cst = np.zeros((128, C_TOT), np.float32)
    cst[:, C_NMIX:C_NMIX + 32] = f(inputs["norm_mix"]).reshape(4, 8, 128).transpose(2, 0, 1).reshape(128, 32)
    cst[:, C_NPLE:C_NPLE + 32] = f(inputs["norm_ple"]).reshape(4, 8, 128).transpose(2, 0, 1).reshape(128, 32)
    cst[:, C_NFIN:C_NFIN + 8] = f(inputs["norm_final"]).reshape(8, 128).T
    cst[:, C_LBL:C_LBL + 16] = f(inputs["lb_logits"]).reshape(2, 8, 128).transpose(2, 0, 1).reshape(128, 16)
    cst[:, C_GN:C_GN + 4] = f(inputs["gnorm_a"]).reshape(2, 2, 128).transpose(2, 0, 1).reshape(128, 4)
    cst[:, C_LNG:C_LNG + 32] = f(inputs["ln_v_g"]).reshape(2, 16, 128).transpose(2, 0, 1).reshape(128, 32)
    cst[:, C_LNB:C_LNB + 32] = f(inputs["ln_v_b"]).reshape(2, 16, 128).transpose(2, 0, 1).reshape(128, 32)
    cst[:, C_ID:C_ID + 128] = np.eye(128, dtype=np.float32)
    s = np.arange(128)[:, None]
    t = np.arange(128)[None, :]
    cst[:, C_MT:C_MT + 128] = ((s <= t) & (s // 64 == t // 64)).astype(np.float32)
    cst[:, C_MB:C_MB + 128] = (s <= t).astype(np.float32)
    sm = np.zeros(512, np.float32)
    sm[::64] = 1.0
    cst[:, C_SM:C_SM + 512] = sm[None, :]
    cst[:, C_ON:C_ON + 128] = 1.0 / 1024.0
    cst[:, C_I16:C_I16 + 256] = np.eye(16, dtype=np.float32).reshape(1, 256)
    lnraw = f(np.stack([f(inputs["ln_v_g"]), f(inputs["ln_v_b"])], axis=1))
    wspT = f(f(inputs["w_spatial"]).transpose(0, 3, 1, 2))
    bspr = f(f(inputs["b_spatial"]).reshape(2, 1024))
    w00r = f(f(inputs["w_spatial"])[:, :, 0, 0])
    shared = dict(cst=cst, lnraw=lnraw, wspT=wspT, bspr=bspr, w00r=w00r, WAQF=WAQF, WAVZ=WAVZ, WOA=WOA, WBV=WBV, WBUZ=WBUZ, WOB=WOB, WG=WG, WP=WP)
    xp = f(inputs["x_prompt"]); xs = f(inputs["x_sample"])
    pp = f(inputs["p_prompt"]); psm = f(inputs["p_sample"])
    st = f(inputs["state_hgrn"])
    maps = []
    for c in range(NCORES):
        xa = np.concatenate([xp[c], xs[16 * c:16 * c + 16, 0]], axis=0)
        xT = f(xa.T.reshape(8, 128, NT).transpose(1, 0, 2))
        pa = np.concatenate([pp[:, c], psm[:, 16 * c:16 * c + 16, 0]], axis=1)
        pT = f(pa.transpose(0, 2, 1).reshape(4, 2, 128, NT).transpose(0, 2, 1, 3))
        sti = f(st[:, 16 * c:16 * c + 16].transpose(0, 2, 3, 1, 4))
        m = dict(shared)
        m.update(xT=xT, pT=pT, st_in=sti)
        maps.append(m)
    return maps


_NC_CACHE = {}


def kernel(**inputs):
    maps = _prep(inputs)
    if "nc" not in _NC_CACHE:
        _NC_CACHE["nc"] = build()
    nc = _NC_CACHE["nc"]
    res = run_bass_kernel_spmd(nc, maps, core_ids=list(range(NCORES)))
    R = res.results
    y_prompt = np.zeros((8, 2048, 1024), np.float32)
    y_sample = np.zeros((128, 1, 1024), np.float32)
    sp = np.zeros((2, 8, 8, 128, 256), np.float32)
    ss = np.zeros((2, 128, 8, 128, 256), np.float32)
    cvp = np.zeros((2, 8, 128, 2048), np.float32)
    cvs = np.zeros((2, 128, 1, 2048), np.float32)
    for c in range(NCORES):
        yT = R[c]["yT"]
        ya = yT.transpose(2, 1, 0).reshape(NT, 1024)
        y_prompt[c] = ya[:2048]
        y_sample[16 * c:16 * c + 16, 0] = ya[2048:]
        sp[:, c] = R[c]["sp_out"]
        ss[:, 16 * c:16 * c + 16] = R[c]["ss_out"].transpose(0, 3, 1, 2, 4)
        cvp[:, c] = R[c]["cvp"]
        cvs[:, 16 * c:16 * c + 16, 0] = R[c]["cvs"]
    return (y_prompt, y_sample, sp, ss, cvp, cvs)
```

```python
from contextlib import ExitStack
import numpy as np
import concourse.bass as bass
import concourse.mybir as mybir
from concourse.bass_utils import run_bass_kernel_spmd

F32 = mybir.dt.float32
BF16 = mybir.dt.bfloat16
AF = mybir.ActivationFunctionType
ALU = mybir.AluOpType

NCORES = 8
NTOK = 2048
NS = 16
NT = NTOK + NS
EPS = 1e-6
PW = 1040

C_NMIX, C_NPLE, C_NFIN, C_LBL, C_GN, C_LNG, C_LNB = 0, 32, 64, 72, 88, 92, 124
C_ID, C_MT, C_MB, C_SM, C_ON, C_I16 = 156, 284, 412, 540, 1052, 1180
C_TOT = 1436


class Ev:
    __slots__ = ("sem", "val")

    def __init__(self, sem, val):
        self.sem, self.val = sem, val


class Buf:
    __slots__ = ("name", "w", "rs", "dsem", "dcnt", "excl")

    def __init__(self, name, excl=False):
        self.name, self.w, self.rs, self.dsem, self.dcnt, self.excl = name, None, {}, None, 0, excl


class Eng:
    def __init__(self, name, h, sem):
        self.name, self.h, self.sem, self.cnt, self.seen = name, h, sem, 0, {}

    def wait(self, ev):
        if ev is None:
            return
        if self.seen.get(ev.sem.num, 0) < ev.val:
            self.h.wait_ge(ev.sem, ev.val)
            self.seen[ev.sem.num] = ev.val


class K:
    def __init__(self, nc, es):
        self.nc, self.es = nc, es
        self.E = {}
        for name, h in (("pe", nc.tensor), ("act", nc.scalar), ("dve", nc.vector),
                        ("pool", nc.gpsimd), ("sp", nc.sync)):
            self.E[name] = Eng(name, h, es.enter_context(nc.semaphore("s_" + name)))
        self.stores = []
        self.nb = 0

    def buf(self, name=None, excl=False):
        self.nb += 1
        return Buf(name or "b%d" % self.nb, excl)

    def _deps(self, E, reads, writes):
        for b in reads:
            if b.w is not None and not (E.name == "pe" and b.w.sem is E.sem):
                E.wait(b.w)
        for b in writes:
            if b.w is not None and not (E.name == "pe" and b.w.sem is E.sem):
                E.wait(b.w)
            for ev in b.rs.values():
                if not (E.name == "pe" and ev.sem is E.sem):
                    E.wait(ev)

    def op(self, eng, fn, reads=(), writes=()):
        E = self.E[eng]
        ex = [b for b in reads if b.excl]
        if ex:
            reads = [b for b in reads if not b.excl]
            writes = list(writes) + [b for b in ex if b not in writes]
        self._deps(E, reads, writes)
        ins = fn()
        E.cnt += 1
        assert E.cnt < 60000
        ins.then_inc(E.sem, 1)
        ev = Ev(E.sem, E.cnt)
        for b in reads:
            b.rs[E.sem.num] = ev
        for b in writes:
            b.w = ev
            b.rs = {}
        return ev

    def dma(self, q, out, in_, reads=(), writes=()):
        E = self.E[q]
        self._deps(E, reads, writes)
        pb = writes[0] if writes else reads[0]
        if pb.dsem is None:
            self.nb += 1
            pb.dsem = self.es.enter_context(self.nc.semaphore("d%d_%s" % (self.nb, pb.name)))
        pb.dcnt += 16
        assert pb.dcnt < 60000
        E.h.dma_start(out=out, in_=in_).then_inc(pb.dsem, 16)
        ev = Ev(pb.dsem, pb.dcnt)
        for b in reads:
            b.rs[pb.dsem.num] = ev
        for b in writes:
            b.w = ev
            b.rs = {}
        if not writes:
            self.stores.append(ev)
        return ev

    def finish(self):
        E = self.E["sp"]
        for ev in self.stores:
            E.wait(ev)


def build(depth_run=4):
    nc = bass.Bass("TRN2", target_bir_lowering=False)
    D = {}

    def din(name, shape):
        D[name] = nc.dram_tensor(name, shape, F32, kind="ExternalInput").ap()
        return D[name]

    def dout(name, shape):
        D[name] = nc.dram_tensor(name, shape, F32, kind="ExternalOutput").ap()
        return D[name]

    xT = din("xT", [128, 8, NT])
    pTd = din("pT", [4, 128, 2, NT])
    st_in = din("st_in", [2, 8, 128, 16, 256])
    cst = din("cst", [128, C_TOT])
    lnraw = din("lnraw", [2, 2, 2048])
    wspT = din("wspT", [2, 128, 8, 128])
    bspr = din("bspr", [2, 1024])
    w00r = din("w00r", [2, 8])
    WAQF = din("WAQF", [2, 8, 128, 8, 256])
    WAVZ = din("WAVZ", [2, 8, 128, 8, 512])
    WOA = din("WOA", [2, 8, 128, 2, 1024])
    WBV = din("WBV", [2, 4, 128, 8, 512])
    WBUZ = din("WBUZ", [2, 8, 128, 8, 512])
    WOB = din("WOB", [2, 8, 128, 2, 1024])
    WG = din("WG", [4, 2, 128, 8, 512])
    WP = din("WP", [4, 128, 2, 1024])
    yT = dout("yT", [128, 8, NT])
    sp_out = dout("sp_out", [2, 8, 128, 256])
    ss_out = dout("ss_out", [2, 8, 128, 16, 256])
    cvp = dout("cvp", [2, 128, 2048])
    cvs = dout("cvs", [2, 16, 2048])

    es = ExitStack()
    with es:
        k = K(nc, es)

        def sb(name, shape, dt=F32):
            return es.enter_context(nc.sbuf_tensor(name, shape, dt))

        def ps(name, shape, dt=F32):
            return es.enter_context(nc.psum_tensor(name, shape, dt))

        hT = sb("hT", [128, 8, NT])
        hTb = [[k.buf("hT%d_%d" % (kt, b)) for b in range(5)] for kt in range(8)]
        hnT = sb("hnT", [128, 8, PW], BF16)
        hnTb = [k.buf("hnT%d" % b) for b in range(3)]
        csb = sb("csb", [128, C_TOT])
        cbuf = k.buf("csb")
        wring = [sb("wr%d" % i, [128, 4096], BF16) for i in range(3)]
        wrb = [k.buf("wr%d" % i) for i in range(3)]
        wstate = {"n": 0}
        T = [sb("T%d" % i, [128, 512]) for i in range(6)]
        Tb = [k.buf("T%d" % i) for i in range(6)]
        rstd = T[5]; rstdb = Tb[5]
        pTs = sb("pTs", [128, 2, PW], BF16); pTb = k.buf("pTs")
        cbf = sb("cbf", [128, 128], BF16)
        cbfb = k.buf("cbf")
        lbv = sb("lbv", [128, 3, 2, 8]); lbb = k.buf("lbv")
        pA = ps("pA", [128, 512]); pB = ps("pB", [128, 512])
        pD = [ps("pD0", [128, 512]), ps("pD1", [128, 512])]
        pAtt = ps("pAtt", [128, 512]); pO = ps("pO", [128, 512])
        pTr = ps("pTr", [128, 1024], BF16); pM = ps("pM", [128, 512])
        pAb, pBb, pAttb, pOb, pTrb, pMb = (k.buf(n, True) for n in ("pA", "pB", "pAtt", "pO", "pTr", "pM"))
        pDb = [k.buf("pD%d" % i, True) for i in range(2)]
        pAttq = [pAttb] * 4
        pD3b = [pDb[0], pDb[1], pTrb]
        pOq = [pOb] * 2
        pAB = [(pA, pAb), (pB, pBb)]
        abst = {"n": 0}
        rot = {"n": 0}

        def next_ab():
            abst["n"] += 1
            return pAB[abst["n"] % 2]

        ident_f = csb[:, C_ID:C_ID + 128]
        maskT = csb[:, C_MT:C_MT + 128]
        maskB = csb[:, C_MB:C_MB + 128]
        scanm = csb[:, C_SM:C_SM + 512]
        onesN = csb[:, C_ON:C_ON + 128]
        id16 = csb[:, C_I16:C_I16 + 256].rearrange("p (a b) -> p a b", b=16)
        identb = cbf[:, 0:128]

        def cvec(base, i, n):
            return csb[:, base + i * n: base + (i + 1) * n]

        k.dma("sp", csb[:], cst[:, :], writes=[cbuf])
        for bi_, (cb_, n_) in enumerate([(0, 512), (512, 512), (1024, 512), (1536, 512), (2048, 16)]):
            k.dma("sp", hT[:, :, cb_:cb_ + n_], xT[:, :, cb_:cb_ + n_], writes=[hTb[kt][bi_] for kt in range(8)])
        k.op("dve", lambda: nc.vector.tensor_copy(out=identb, in_=ident_f), reads=[cbuf], writes=[cbfb])
        def _lb():
            nc.vector.memset(lbv[:, 0, 0, :], 0.0)
            return nc.vector.tensor_tensor(out=lbv[:, 0, 1, :], in0=csb[:, C_LBL + 8:C_LBL + 16], in1=csb[:, C_LBL:C_LBL + 8], op=ALU.subtract)
        k.op("dve", _lb, reads=[cbuf], writes=[lbb])
        k.op("act", lambda: nc.scalar.activation(out=lbv[:, 0, 1, :], in_=lbv[:, 0, 1, :], func=AF.Sigmoid), reads=[lbb], writes=[lbb])
        def _lb2():
            nc.vector.tensor_scalar(out=lbv[:, 1, :, :], in0=lbv[:, 0, :, :], scalar1=-0.5, scalar2=0.5, op0=ALU.mult, op1=ALU.add)
            return nc.vector.tensor_scalar(out=lbv[:, 2, :, :], in0=lbv[:, 0, :, :], scalar1=0.5, scalar2=0.5, op0=ALU.mult, op1=ALU.add)
        k.op("dve", _lb2, reads=[lbb], writes=[lbb])

        def wload(src_ap, ncols_total, view3=None, slot=None):
            if slot is None:
                i = wstate["n"] % 3
                wstate["n"] += 1
            else:
                i = slot
            dst = wring[i][:, 0:ncols_total]
            if view3:
                dst = dst.rearrange("p (g c) -> p g c", c=view3)
            k.dma("pool", dst, src_ap, writes=[wrb[i]])
            return wring[i], wrb[i]

        def pass_blocks(p):
            bl = [(p * 1024, 512, 2 * p), (p * 1024 + 512, 512, 2 * p + 1)]
            if p == 1:
                bl.append((2048, 16, 4))
            return bl

        def lcol(p, cb):
            return cb - p * 1024

        def hn_idx(p, cb):
            return (cb - p * 1024) // 512

        def rsqrt(out, in_, rbufs, wbuf):
            k.op("act", lambda: nc.scalar.activation(out=out, in_=in_, func=AF.Sqrt, bias=EPS, scale=1.0), reads=rbufs, writes=[wbuf])
            k.op("dve", lambda: nc.vector.reciprocal(out=out, in_=out), reads=[wbuf], writes=[wbuf])

        def norm_pass(p, wbase, wi):
            for (cb, n, bi) in pass_blocks(p):
                for kt in range(8):
                    tb, tbb = T[kt % 2], Tb[kt % 2]
                    k.op("act", lambda kt=kt, tb=tb: nc.scalar.activation(out=tb[:, 0:n], in_=hT[:, kt, cb:cb + n], func=AF.Square),
                         reads=[hTb[kt][bi]], writes=[tbb])
                    k.op("pe", lambda kt=kt, tb=tb: nc.tensor.matmul(pM[:, 0:n], lhsT=onesN, rhs=tb[:, 0:n], start=(kt == 0), stop=(kt == 7)),
                         reads=[tbb, cbuf], writes=[pMb])
                rsqrt(rstd[:, 0:n], pM[:, 0:n], [pMb], rstdb)
                lc = lcol(p, cb)
                hb = hnTb[hn_idx(p, cb)]
                for kt in range(8):
                    k.op("dve", lambda kt=kt: nc.vector.scalar_tensor_tensor(
                        out=hnT[:, kt, lc:lc + n], in0=hT[:, kt, cb:cb + n], scalar=csb[:, wbase + wi * 8 + kt: wbase + wi * 8 + kt + 1],
                        in1=rstd[:, 0:n], op0=ALU.mult, op1=ALU.mult), reads=[hTb[kt][bi], rstdb, cbuf], writes=[hb])

        def outproj(p, wsrc, gt, gtb, slot=None, pre=None):
            if pre is not None:
                wt, wb = pre
            else:
                wt, wb = wload(wsrc.rearrange("g p a c -> p g (a c)"), 4096, view3=2048, slot=slot)
            wv = wt[:, 0:4096].rearrange("p (a c) -> p a c", c=1024)
            for m in range(8):
                for (cb, n, bi) in pass_blocks(p):
                    if n == 16:
                        continue
                    lc = lcol(p, cb)
                    pp, ppb = next_ab()
                    def mm(pp=pp, lc=lc, n=n, m=m):
                        for a4 in range(4):
                            ins = nc.tensor.matmul(pp[:, 0:n], lhsT=wv[:, a4, m * 128:(m + 1) * 128], rhs=gt[:, a4, lc:lc + n], start=(a4 == 0), stop=(a4 == 3))
                        return ins
                    k.op("pe", mm, reads=[wb, gtb], writes=[ppb])
                    k.op("dve", lambda pp=pp, n=n, m=m, cb=cb: nc.vector.tensor_tensor(out=hT[:, m, cb:cb + n], in0=hT[:, m, cb:cb + n], in1=pp[:, 0:n], op=ALU.add),
                         reads=[ppb], writes=[hTb[m][bi]])
            if p == 1:
                pp, ppb = next_ab()
                def mms(pp=pp):
                    for m in range(8):
                        for a4 in range(4):
                            ins = nc.tensor.matmul(pp[:, m * 16:(m + 1) * 16], lhsT=wv[:, a4, m * 128:(m + 1) * 128], rhs=gt[:, a4, 1024:1040], start=(a4 == 0), stop=(a4 == 3))
                    return ins
                k.op("pe", mms, reads=[wb, gtb], writes=[ppb])
                k.op("dve", lambda pp=pp: nc.vector.tensor_tensor(out=hT[:, :, 2048:2064], in0=hT[:, :, 2048:2064], in1=pp[:, 0:128].rearrange("p (m x) -> p m x", x=16), op=ALU.add),
                     reads=[ppb], writes=[hTb[m][4] for m in range(8)])

        def ple_pass(p, i):
            norm_pass(p, C_NPLE, i)
            c0 = p * 1024
            ncol = PW if p == 1 else 1024
            k.dma("pool", pTs[:, :, 0:ncol], pTd[i, :, :, c0:c0 + ncol], writes=[pTb])
            wpt, wpb = wload(WP[i].rearrange("p a c -> p (a c)"), 2048)
            wpv = wpt[:, 0:2048].rearrange("p (a c) -> p a c", c=1024)
            wgl = []
            for ch in range(2):
                wgt, wgb = wload(WG[i, ch].rearrange("p a c -> p (a c)"), 4096)
                wgv = wgt[:, 0:4096].rearrange("p (a c) -> p a c", c=512)
                wgl.append((wgv, wgb))
                for mm_ in range(4):
                    m = ch * 4 + mm_
                    for (cb, n, bi) in pass_blocks(p):
                        if n == 16:
                            continue
                        lc = lcol(p, cb)
                        hb = hnTb[hn_idx(p, cb)]
                        rot["n"] += 1
                        r = rot["n"] % 2
                        pg, pgb = (pA, pAb) if r == 0 else (pB, pBb)
                        pq, pqb = pD[r], pDb[r]
                        tg_, tgb_ = (T[2], Tb[2]) if r == 0 else (T[4], Tb[4])
                        tm_, tmb_ = (T[3], Tb[3]) if r == 0 else (T[0], Tb[0])
                        def mg(lc=lc, n=n, mm_=mm_, pg=pg):
                            for kt in range(8):
                                ins = nc.tensor.matmul(pg[:, 0:n], lhsT=wgv[:, kt, mm_ * 128:(mm_ + 1) * 128], rhs=hnT[:, kt, lc:lc + n], start=(kt == 0), stop=(kt == 7))
                            return ins
                        k.op("pe", mg, reads=[wgb, hb], writes=[pgb])
                        def mp(lc=lc, n=n, m=m, pq=pq):
                            nc.tensor.matmul(pq[:, 0:n], lhsT=wpv[:, 0, m * 128:(m + 1) * 128], rhs=pTs[:, 0, lc:lc + n], start=True, stop=False)
                            return nc.tensor.matmul(pq[:, 0:n], lhsT=wpv[:, 1, m * 128:(m + 1) * 128], rhs=pTs[:, 1, lc:lc + n], start=False, stop=True)
                        k.op("pe", mp, reads=[wpb, pTb], writes=[pqb])
                        k.op("act", lambda n=n: nc.scalar.activation(out=tg_[:, 0:n], in_=pg[:, 0:n], func=AF.Tanh, scale=0.5), reads=[pgb], writes=[tgb_])
                        k.op("dve", lambda n=n: nc.vector.scalar_tensor_tensor(out=tm_[:, 0:n], in0=tg_[:, 0:n], scalar=1.0, in1=pq[:, 0:n], op0=ALU.add, op1=ALU.mult), reads=[tgb_, pqb], writes=[tmb_])
                        k.op("dve", lambda n=n, m=m, cb=cb: nc.vector.scalar_tensor_tensor(out=hT[:, m, cb:cb + n], in0=tm_[:, 0:n], scalar=0.5, in1=hT[:, m, cb:cb + n], op0=ALU.mult, op1=ALU.add),
                             reads=[tmb_], writes=[hTb[m][bi]])
            if p == 1:
                hb = hnTb[2]
                def mgs():
                    for m in range(8):
                        wgv_, _ = wgl[m // 4]
                        for kt in range(8):
                            ins = nc.tensor.matmul(pA[:, m * 16:(m + 1) * 16], lhsT=wgv_[:, kt, (m % 4) * 128:(m % 4 + 1) * 128], rhs=hnT[:, kt, 1024:1040], start=(kt == 0), stop=(kt == 7))
                    return ins
                k.op("pe", mgs, reads=[wgl[0][1], wgl[1][1], hb], writes=[pAb])
                def mps():
                    for m in range(8):
                        nc.tensor.matmul(pB[:, m * 16:(m + 1) * 16], lhsT=wpv[:, 0, m * 128:(m + 1) * 128], rhs=pTs[:, 0, 1024:1040], start=True, stop=False)
                        ins = nc.tensor.matmul(pB[:, m * 16:(m + 1) * 16], lhsT=wpv[:, 1, m * 128:(m + 1) * 128], rhs=pTs[:, 1, 1024:1040], start=False, stop=True)
                    return ins
                k.op("pe", mps, reads=[wpb, pTb], writes=[pBb])
                k.op("act", lambda: nc.scalar.activation(out=T[2][:, 0:128], in_=pA[:, 0:128], func=AF.Tanh, scale=0.5), reads=[pAb], writes=[Tb[2]])
                k.op("dve", lambda: nc.vector.scalar_tensor_tensor(out=T[3][:, 0:128], in0=T[2][:, 0:128], scalar=1.0, in1=pB[:, 0:128], op0=ALU.add, op1=ALU.mult), reads=[Tb[2], pBb], writes=[Tb[3]])
                k.op("dve", lambda: nc.vector.scalar_tensor_tensor(out=hT[:, :, 2048:2064], in0=T[3][:, 0:128].rearrange("p (m x) -> p m x", x=16), scalar=0.5, in1=hT[:, :, 2048:2064], op0=ALU.mult, op1=ALU.add),
                     reads=[Tb[3]], writes=[hTb[m][4] for m in range(8)])

        def barrier():
            evs = {}
            for en in ("pe", "act", "dve", "pool", "sp"):
                E = k.E[en]
                if E.cnt > 0:
                    evs[E.sem.num] = Ev(E.sem, E.cnt)
            for ev in k.stores:
                if ev.sem.num not in evs or evs[ev.sem.num].val < ev.val:
                    evs[ev.sem.num] = ev
            for en in ("pe", "act", "dve", "pool", "sp"):
                for ev in evs.values():
                    k.E[en].wait(ev)

        def layer_A(i, j, es2):
            def sb2(name, shape, dt=F32):
                return es2.enter_context(nc.sbuf_tensor(name + "_L%d" % i, shape, dt))
            gTt = [sb2("gT%d" % t, [128, 4, PW], BF16) for t in range(2)]; gTb = [k.buf("gT%d" % t) for t in range(2)]
            TA = [T[0], T[1]] + [sb2("TA%d" % t, [128, 512]) for t in range(2, 4)]; TAb = [Tb[0], Tb[1]] + [k.buf("TA%d" % t) for t in range(2, 4)]
            qtil = sb2("qtil", [128, 1024], BF16); qtilb = k.buf("qtil")
            qpad = sb2("qpad", [128, 16, 128], BF16); qpadb = k.buf("qpad")
            ktil = sb2("ktil", [128, 1024], BF16); ktilb = k.buf("ktil")
            kdec = sb2("kdec", [128, 1024], BF16); kdecb = k.buf("kdec")
            dch = sb2("dch", [128, 16]); dchb = k.buf("dch")
            vh = sb2("vh", [128, 8, 256], BF16); vhb = [k.buf("vh%d" % t) for t in range(8)]
            szh = sb2("szh", [128, 8, 256], BF16); szhb = [k.buf("szh%d" % t) for t in range(8)]
            vs = sb2("vs", [16, 256], BF16); vsb = k.buf("vs")
            szs = sb2("szs", [16, 256], BF16); szsb = k.buf("szs")
            kT = sb2("kT", [128, 8, 128], BF16); kTb = k.buf("kT")
            attm = [sb2("attm%d" % t, [128, 128], BF16) for t in range(2)]; attmb = [k.buf("attm%d" % t) for t in range(2)]
            Sf = sb2("Sf", [128, 8, 256]); Sfb = [k.buf("Sf%d" % h) for h in range(8)]
            Sx = sb2("Sx", [128, 256]); Sxb = k.buf("Sx")
            NSB = 6
            Sb = sb2("Sb", [128, NSB, 256], BF16); Sbb = [k.buf("Sb%d" % t) for t in range(NSB)]
            junk = sb2("junk", [128, 256], BF16); junkb = k.buf("junk")
            og = sb2("og", [128, 8, 256], BF16); ogs = sb2("ogs", [16, 256], BF16); ogb = k.buf("og"); ogb2 = k.buf("og2")
            ssq = sb2("ssq", [128, 32]); ssqb = k.buf("ssq")
            k.op("dve", lambda: nc.vector.memset(ssq[:, :], 1.0), writes=[ssqb])
            qS2 = sb2("qS", [128, 2, 16]); fS2 = sb2("fS", [128, 2, 16]); kkS2 = sb2("kkS", [128, 2, 16]); smb2 = [k.buf("smp0"), k.buf("smp1")]
            TS = sb2("TS", [128, 4, 16]); TSb = [k.buf("TS%d" % t) for t in range(4)]
            kkST = sb2("kkST", [16, 128]); kkSTb = k.buf("kkST")
            Am3 = sb2("Am3", [16, 3, 128], BF16); Am3b = [k.buf("Am3_%d" % t) for t in range(3)]
            Qp = sb2("Qp", [128, 16, 16], BF16); Qpb = k.buf("Qp")
            qf16 = sb2("qf16", [128, 16]); qf16b = k.buf("qf16")
            qk = sb2("qk", [16, 4]); qkb = k.buf("qk")
            osb = sb2("osb", [16, 256]); osbb = k.buf("osb")
            NSM = 8
            ssm = [sb2("ssm%d" % t, [128, 2, 256]) for t in range(NSM)]; ssmb = [[k.buf("ssm%d_%d" % (t, u)) for u in range(2)] for t in range(NSM)]
            snb = [sb2("snb%d" % t, [128, 256], BF16) for t in range(3)]; snbb = [k.buf("snb%d" % t) for t in range(3)]

            def _z():
                nc.vector.memset(qpad[:], 0.0)
                return nc.vector.memset(Sf[:], 0.0)
            k.op("dve", _z, writes=[qpadb] + Sfb)
            c1_ = lambda h: lbv[:, 1, j, h:h + 1]
            c2_ = lambda h: lbv[:, 2, j, h:h + 1]

            for p in range(2):
                norm_pass(p, C_NMIX, i)
                def make_head(h):
                    W = {}
                    qS, fS, kkS, smb = qS2[:, h % 2, :], fS2[:, h % 2, :], kkS2[:, h % 2, :], smb2[h % 2]
                    gt, gtb = gTt[(h // 2) % 2], gTb[(h // 2) % 2]
                    gk = 2 * (h % 2)
                    blks = pass_blocks(p)

                    def tset(bix):
                        if bix == 2:
                            return TS[:, 0, :], TSb[0], TS[:, 1, :], TSb[1], TS[:, 2, :], TSb[2], TS[:, 3, :], TSb[3]
                        if bix % 2 == 0:
                            return T[2], Tb[2], T[3], Tb[3], T[4], Tb[4], T[5], Tb[5]
                        return TA[0], TAb[0], TA[1], TAb[1], TA[2], TAb[2], TA[3], TAb[3]

                    def qf_s1(bix, part=0):
                        cb, n, bi = blks[bix]
                        lc = lcol(p, cb)
                        hb = hnTb[hn_idx(p, cb)]
                        Tq, Tqb, Tk, Tkb, Tl, Tlb, Te, Teb = tset(bix)
                        pq_, pqb_ = (pTr[:, 0:1024].bitcast(F32), pTrb) if part == 1 else (pA, pAb)
                        def mq(o=0, pp=pq_):
                            for kt in range(8):
                                ins = nc.tensor.matmul(pp[:, 0:n], lhsT=W["wv"][:, kt, o:o + 128], rhs=hnT[:, kt, lc:lc + n], start=(kt == 0), stop=(kt == 7))
                            return ins
                        if part in (0, 1):
                            k.op("pe", mq, reads=[W["wb"], hb], writes=[pqb_])
                            k.op("pe", lambda: mq(128, pB), reads=[W["wb"], hb], writes=[pBb])
                            k.op("act", lambda: nc.scalar.activation(out=Tq[:, 0:n], in_=pq_[:, 0:n], func=AF.Silu), reads=[pqb_], writes=[Tqb])
                            k.op("act", lambda: nc.scalar.activation(out=Tk[:, 0:n], in_=pB[:, 0:n], func=AF.Tanh, scale=0.5), reads=[pBb], writes=[Tkb])
                            k.op("act", lambda: nc.scalar.activation(out=Tl[:, 0:n], in_=Tk[:, 0:n], func=AF.Identity, bias=c2_(h), scale=c1_(h)),
                                 reads=[Tkb, lbb], writes=[Tlb])
                            if n == 512:
                                k.op("pool", lambda: nc.gpsimd.tensor_tensor(out=Te[:, :], in0=Tl[:, :], in1=scanm, op=ALU.mult), reads=[Tlb, cbuf], writes=[Teb])
                        if n == 512:
                            if part in (0, 2):
                                k.op("dve", lambda: nc.vector.tensor_tensor_scan(out=Tk[:, :], data0=Tl[:, :], data1=Te[:, :], initial=0.0, op0=ALU.mult, op1=ALU.max),
                                     reads=[Tlb, Teb], writes=[Tkb])
                                k.op("pool", lambda: nc.gpsimd.tensor_scalar(out=Tl[:, :], in0=Tl[:, :], scalar1=-1.0, scalar2=1.0, op0=ALU.mult, op1=ALU.add), reads=[Tlb], writes=[Tlb])
                            if part in (0, 3):
                                k.op("dve", lambda: nc.vector.reciprocal(out=Te[:, :], in_=Tk[:, :]), reads=[Tkb], writes=[Teb])
                                k.op("pool", lambda: nc.gpsimd.tensor_tensor(out=Tl[:, :], in0=Tl[:, :], in1=Te[:, :], op=ALU.mult), reads=[Tlb, Teb], writes=[Tlb])
                        elif part in (0, 1):
                            def smp_():
                                nc.vector.tensor_copy(out=qS, in_=Tq[:, 0:16])
                                nc.vector.tensor_copy(out=fS, in_=Tl[:, 0:16])
                                return nc.vector.tensor_scalar(out=kkS, in0=Tl[:, 0:16], scalar1=-1.0, scalar2=1.0, op0=ALU.mult, op1=ALU.add)
                            k.op("dve", smp_, reads=[Tqb, Tlb], writes=[smb])

                    def qf_s2(bix):
                        cb, n, bi = blks[bix]
                        if n != 512:
                            return
                        lc = lcol(p, cb)
                        Tq, Tqb, Tk, Tkb, Tl, Tlb, Te, Teb = tset(bix)
                        k.op("pool", lambda: nc.gpsimd.tensor_tensor(out=qtil[:, lc:lc + 512], in0=Tq[:, :], in1=Tk[:, :], op=ALU.mult),
                             reads=[Tqb, Tkb], writes=[qtilb])
                        c0 = lc // 64
                        def qp():
                            src = qtil[:, lc:lc + 512].rearrange("p (t a x) -> p t a x", a=2, x=64)
                            dst = qpad[:, c0:c0 + 8, :].rearrange("p (t a) (b x) -> p t a b x", a=2, b=2)
                            nc.scalar.copy(out=dst[:, :, 0, 0, :], in_=src[:, :, 0, :])
                            return nc.scalar.copy(out=dst[:, :, 1, 1, :], in_=src[:, :, 1, :])
                        k.op("act", qp, reads=[qtilb], writes=[qpadb])
                        k.op("act", lambda: nc.scalar.copy(out=ktil[:, lc:lc + 512], in_=Tl[:, :]), reads=[Tlb], writes=[ktilb])
                        def kd():
                            ebl = Tk[:, :].rearrange("p (c t) -> p c t", t=64)[:, :, 63:64].to_broadcast([128, 8, 64])
                            return nc.vector.tensor_tensor(out=kdec[:, lc:lc + 512].rearrange("p (c t) -> p c t", t=64),
                                                           in0=Tl[:, :].rearrange("p (c t) -> p c t", t=64), in1=ebl, op=ALU.mult)
                        k.op("dve", kd, reads=[Tlb, Tkb], writes=[kdecb])
                        k.op("pool", lambda: nc.gpsimd.tensor_copy(out=dch[:, c0:c0 + 8], in_=Tk[:, :].rearrange("p (c t) -> p c t", t=64)[:, :, 63]),
                             reads=[Tkb], writes=[dchb])

                    ntile = 9 if p == 1 else 8
                    vzbanks = [(pO, pOb), (pM, pMb), (pAtt, pAttb)]

                    def vz(lt):
                        if lt >= ntile or W.get(("vz", lt)):
                            return
                        W[("vz", lt)] = True
                        M = 128 if lt < 8 else 16
                        hb = hnTb[lt // 4]
                        if lt < 3 and not W.get("early"):
                            pp, ppb = vzbanks[lt]
                        else:
                            pp, ppb = pA, pAb
                        def mvz():
                            for kt in range(8):
                                ins = nc.tensor.matmul(pp[0:M, :], lhsT=hnT[:, kt, lt * 128:lt * 128 + M], rhs=W["wv2"][:, kt, :], start=(kt == 0), stop=(kt == 7))
                            return ins
                        k.op("pe", mvz, reads=[W["wb2"], hb], writes=[ppb])
                        if lt < 8:
                            k.op("dve", lambda: nc.vector.tensor_copy(out=vh[:, lt, :], in_=pp[:, 0:256]), reads=[ppb], writes=[vhb[lt]])
                            k.op("act", lambda: nc.scalar.activation(out=szh[:, lt, :], in_=pp[:, 256:512], func=AF.Silu), reads=[ppb], writes=[szhb[lt]])
                        else:
                            k.op("dve", lambda: nc.vector.tensor_copy(out=vs[:, :], in_=pp[0:16, 0:256]), reads=[ppb], writes=[vsb])
                            k.op("act", lambda: nc.scalar.activation(out=szs[:, :], in_=pp[0:16, 256:512], func=AF.Silu), reads=[ppb], writes=[szsb])


                    def preq():
                        wt, W["wb"] = wload(WAQF[j, h].rearrange("p a c -> p (a c)"), 2048, slot=2)
                        W["wv"] = wt[:, 0:2048].rearrange("p (a c) -> p a c", c=256)

                    def pre():
                        wt2, W["wb2"] = wload(WAVZ[j, h].rearrange("p a c -> p (a c)"), 4096, slot=h % 2)
                        W["wv2"] = wt2[:, 0:4096].rearrange("p (a c) -> p a c", c=512)

                    def prew():
                        W["wo"] = wload(WOA[j, h - 1:h + 1].rearrange("g p a c -> p g (a c)"), 4096, view3=2048, slot=h % 2)

                    def stl():
                        if p == 1:
                            for g2 in range(NSM):
                                bi_ = (8 * h + g2) % NSM
                                k.dma("sp", ssm[bi_][:, :, :], st_in[j, h, :, g2 * 2:(g2 + 1) * 2, :], writes=ssmb[bi_])

                    def s1all():
                        for bix in range(len(blks)):
                            qf_s1(bix)

                    def vz_early():
                        W["early"] = True
                        vz(0)
                        vz(1)
                        W["early"] = False

                    def mida():
                        qf_s2(0)
                        qf_s2(1)
                        if p == 1:
                            vz(8)
                        vz(0)
                        vz(1)
                        vz(2)

                    def mid(nxt, nxt2=None):
                        def trs():
                            for lt in range(8):
                                ins = nc.tensor.transpose(out=pTr[:, lt * 128:(lt + 1) * 128], in_=kdec[:, lt * 128:(lt + 1) * 128], identity=identb)
                            return ins
                        k.op("pe", trs, reads=[kdecb, cbfb], writes=[pTrb])
                        k.op("act", lambda: nc.scalar.copy(out=kT[:, :, :], in_=pTr[:, 0:1024].rearrange("p (t c) -> p t c", c=128)), reads=[pTrb], writes=[kTb])
                        k.op("act", lambda: nc.scalar.copy(out=Sb[:, 0, :], in_=Sf[:, h, :]), reads=[Sfb[h]], writes=[Sbb[0]])

                        def stage_a(lt):
                            for a in range(2):
                                c = 2 * lt + a
                                dsl = c % 2
                                pd = pD[dsl][:, 0:256]
                                k.op("pe", lambda lt=lt, a=a, pd=pd: nc.tensor.matmul(pd, lhsT=kT[a * 64:(a + 1) * 64, lt, :], rhs=vh[a * 64:(a + 1) * 64, lt, :], start=True, stop=True),
                                     reads=[kTb, vhb[lt]], writes=[pD3b[dsl]])
                                if c % 2 == 0:
                                    ssrc, ssrcb, sdst, sdstb = Sf[:, h, :], Sfb[h], Sx[:, :], Sxb
                                else:
                                    ssrc, ssrcb, sdst, sdstb = Sx[:, :], Sxb, Sf[:, h, :], Sfb[h]
                                k.op("dve", lambda c=c, pd=pd, ssrc=ssrc, sdst=sdst: nc.vector.scalar_tensor_tensor(out=sdst, in0=ssrc, scalar=dch[:, c:c + 1], in1=pd, op0=ALU.mult, op1=ALU.add),
                                     reads=[pD3b[dsl], dchb, ssrcb], writes=[sdstb])
                                if c < 15:
                                    k.op("act", lambda c=c, sdst=sdst: nc.scalar.copy(out=Sb[:, (c + 1) % NSB, :], in_=sdst), reads=[sdstb], writes=[Sbb[(c + 1) % NSB]])
                            k.op("pe", lambda lt=lt: nc.tensor.matmul(pAtt[:, 0:128], lhsT=ktil[:, lt * 128:(lt + 1) * 128], rhs=qtil[:, lt * 128:(lt + 1) * 128], start=True, stop=True),
                                 reads=[ktilb, qtilb], writes=[pAttb])
                            k.op("dve", lambda lt=lt: nc.vector.tensor_tensor(out=attm[lt % 2][:, :], in0=pAtt[:, 0:128], in1=maskT, op=ALU.mult),
                                 reads=[pAttb, cbuf], writes=[attmb[lt % 2]])

                        def obank(lt):
                            return (pO, pOb) if lt % 2 == 0 else (pM, pMb)

                        def stage_b(lt):
                            sl = lt % 2
                            c = 2 * lt
                            po_t, pob = obank(lt)
                            po = po_t[:, 0:256]
                            def mo():
                                nc.tensor.matmul(po, lhsT=attm[sl][:, :], rhs=vh[:, lt, :], start=True, stop=False)
                                nc.tensor.matmul(po, lhsT=qpad[:, c, :], rhs=Sb[:, c % NSB, :], start=False, stop=False)
                                return nc.tensor.matmul(po, lhsT=qpad[:, c + 1, :], rhs=Sb[:, (c + 1) % NSB, :], start=False, stop=True)
                            k.op("pe", mo, reads=[attmb[sl], vhb[lt], qpadb, Sbb[c % NSB], Sbb[(c + 1) % NSB]], writes=[pob])
                            post_b(po, pob, szh[:, lt, :], szhb[lt], 128, lt)

                        def post_b(po, pob, sz, szb, M, lt):
                            k.op("act", lambda: nc.scalar.activation(out=junk[0:M, :], in_=po[0:M, :], func=AF.Square, scale=1.0 / 16.0, accum_out=ssq[0:M, lt:lt + 1]),
                                 reads=[pob], writes=[junkb, ssqb])
                            ogd = og[0:M, lt, :] if lt < 8 else ogs[0:M, :]
                            k.op("dve", lambda: nc.vector.tensor_tensor(out=ogd, in0=po[0:M, :], in1=sz[0:M, :] if M < 128 else sz, op=ALU.mult),
                                 reads=[pob, szb], writes=[ogb if (lt < 4 or lt == 8) else ogb2])

                        def tail(nt):
                            rsqrt(ssq[:, 16:16 + nt], ssq[:, 0:nt], [ssqb], ssqb)
                            k.op("dve", lambda: nc.vector.tensor_tensor(out=og[:, 0:4, :], in0=og[:, 0:4, :], in1=ssq[:, 16:20].unsqueeze(2).to_broadcast([128, 4, 256]), op=ALU.mult),
                                 reads=[ssqb], writes=[ogb])
                            k.op("pool", lambda: nc.gpsimd.tensor_tensor(out=og[:, 4:8, :], in0=og[:, 4:8, :], in1=ssq[:, 20:24].unsqueeze(2).to_broadcast([128, 4, 256]), op=ALU.mult),
                                 reads=[ssqb], writes=[ogb2])
                            if nt == 9:
                                k.op("dve", lambda: nc.vector.tensor_scalar(out=ogs[:, :], in0=ogs[:, :], scalar1=ssq[0:16, 24:25], scalar2=None, op0=ALU.mult),
                                     reads=[ssqb], writes=[ogb])
                            for q4 in range(2):
                                def tg():
                                    for t4 in range(4):
                                        for half in range(2):
                                            ins = nc.tensor.transpose(out=pTr[:, (t4 * 2 + half) * 128:(t4 * 2 + half + 1) * 128], in_=og[:, q4 * 4 + t4, half * 128:(half + 1) * 128], identity=identb)
                                    return ins
                                k.op("pe", tg, reads=[ogb if q4 == 0 else ogb2, cbfb], writes=[pTrb])
                                def cg():
                                    src = pTr[:, 0:1024].rearrange("p (t a x) -> p t a x", a=2, x=128)
                                    for half in range(2):
                                        ins = nc.scalar.activation(out=gt[:, gk + half, q4 * 512:(q4 + 1) * 512].rearrange("p (t x) -> p t x", x=128), in_=src[:, :, half, :], func=AF.Copy,
                                                                   scale=csb[:, C_GN + j * 2 + half:C_GN + j * 2 + half + 1])
                                    return ins
                                k.op("act", cg, reads=[pTrb, cbuf], writes=[gtb])
                            if nt == 9:
                                def tgs():
                                    nc.tensor.transpose(out=pTr[:, 0:16], in_=ogs[0:16, 0:128], identity=identb[0:16, 0:16])
                                    return nc.tensor.transpose(out=pTr[:, 128:144], in_=ogs[0:16, 128:256], identity=identb[0:16, 0:16])
                                k.op("pe", tgs, reads=[ogb, cbfb], writes=[pTrb])
                                def cgs():
                                    nc.scalar.activation(out=gt[:, gk, 1024:1040], in_=pTr[:, 0:16], func=AF.Copy, scale=csb[:, C_GN + j * 2:C_GN + j * 2 + 1])
                                    return nc.scalar.activation(out=gt[:, gk + 1, 1024:1040], in_=pTr[:, 128:144], func=AF.Copy, scale=csb[:, C_GN + j * 2 + 1:C_GN + j * 2 + 2])
                                k.op("act", cgs, reads=[pTrb, cbuf], writes=[gtb])

                        W["tail"] = tail
                        if p == 1:
                            pOS = pTr[0:16, 0:512].bitcast(F32)
                            k.op("pe", lambda: nc.tensor.transpose(out=pM[0:16, 0:128], in_=kkS, identity=ident_f), reads=[smb, cbuf], writes=[pMb])
                            k.op("dve", lambda: nc.vector.tensor_copy(out=kkST[:, :], in_=pM[0:16, 0:128]), reads=[pMb], writes=[kkSTb])
                            k.op("dve", lambda: nc.vector.tensor_tensor(out=qf16[:, :], in0=qS, in1=fS, op=ALU.mult), reads=[smb], writes=[qf16b])
                            k.op("dve", lambda: nc.vector.tensor_tensor(out=Qp[:, :, :], in0=qf16[:, :].unsqueeze(2).to_broadcast([128, 16, 16]), in1=id16, op=ALU.mult),
                                 reads=[qf16b, cbuf], writes=[Qpb])
                            k.op("dve", lambda: nc.vector.tensor_tensor(out=qf16[:, :], in0=qS, in1=kkS, op=ALU.mult), reads=[smb, Qpb], writes=[qf16b])
                            k.op("pe", lambda: nc.tensor.matmul(pM[0:16, 0:1], lhsT=qf16[:, :], rhs=csb[:, C_ON:C_ON + 1], start=True, stop=True), reads=[qf16b, cbuf], writes=[pMb])
                            k.op("dve", lambda: nc.vector.tensor_scalar(out=qk[:, 0:1], in0=pM[0:16, 0:1], scalar1=1024.0, scalar2=None, op0=ALU.mult), reads=[pMb], writes=[qkb])

                        def mk_am(b):
                            k.op("dve", lambda: nc.vector.tensor_scalar(out=Am3[:, b % 3, :], in0=kkST[:, :], scalar1=csb[0:16, C_ID + b:C_ID + b + 1], scalar2=None, op0=ALU.mult),
                                 reads=[kkSTb, cbuf], writes=[Am3b[b % 3]])

                        def samp(b):
                            g2, bb = b // 2, b % 2
                            sm_, smb_ = ssm[(8 * h + g2) % NSM], ssmb[(8 * h + g2) % NSM]
                            pd = pD[b % 2][:, 0:256]
                            pdb = pDb[b % 2]
                            k.op("act", lambda: nc.scalar.copy(out=snb[b % 3][:, :], in_=sm_[:, bb, :]), reads=[smb_[bb]], writes=[snbb[b % 3]])
                            k.op("pe", lambda: nc.tensor.matmul(pOS, lhsT=Qp[:, b, :], rhs=snb[b % 3][:, :], start=(b == 0), stop=(b == 15)),
                                 reads=[Qpb, snbb[b % 3]], writes=[pTrb])
                            k.op("pe", lambda: nc.tensor.matmul(pd, lhsT=Am3[:, b % 3, :], rhs=vs[:, :], start=True, stop=True), reads=[Am3b[b % 3], vsb], writes=[pdb])
                            if b + 2 < 16:
                                mk_am(b + 2)
                            k.op("dve", lambda: nc.vector.scalar_tensor_tensor(out=sm_[:, bb, :], in0=sm_[:, bb, :], scalar=fS[:, b:b + 1], in1=pd, op0=ALU.mult, op1=ALU.add),
                                 reads=[pdb, smb], writes=[smb_[bb]])
                            if bb == 1:
                                k.dma("sp", ss_out[j, h, :, g2 * 2:(g2 + 1) * 2, :], sm_[:, :, :], reads=smb_)
                                if g2 + NSM < 8:
                                    k.dma("sp", sm_[:, :, :], st_in[j, h, :, (g2 + NSM) * 2:(g2 + NSM + 1) * 2, :], writes=smb_)

                        for step in range(9):
                            if step < 8:
                                stage_a(step)
                            if 0 <= step - 1 < 8:
                                stage_b(step - 1)
                            vz(step + 3)
                            if step == 6 and h % 2 == 1:
                                prew()
                            if step == 7 and nxt2 is not None:
                                nxt2["preq"]()
                            if step == 7 and nxt is not None:
                                nxt["vz_early"]()
                            if step == 0:
                                yield
                            if nxt is not None:
                                if step == 0:
                                    nxt["pre"]()
                                elif step == 1:
                                    nxt["s1"](0, 1)
                                elif step == 2:
                                    nxt["s1"](0, 2)
                                elif step == 3:
                                    nxt["s1"](0, 3)
                                elif step == 4:
                                    nxt["s1"](1, 1)
                                elif step == 5:
                                    nxt["s1"](1, 2)
                                elif step == 6:
                                    nxt["s1"](1, 3)
                                    if p == 1:
                                        nxt["s1"](2)
                        if p == 1:
                            mk_am(0)
                            mk_am(1)
                            for b in range(16):
                                samp(b)
                        if p == 1:
                            k.dma("sp", sp_out[j, h, :, :], Sf[:, h, :], reads=[Sfb[h]])
                            k.op("dve", lambda: nc.vector.scalar_tensor_tensor(out=osb[:, :], in0=vs[:, :], scalar=qk[:, 0:1], in1=pOS, op0=ALU.mult, op1=ALU.add),
                                 reads=[vsb, qkb, pTrb], writes=[osbb])
                            post_b(osb[:, :], osbb, szs, szsb, 16, 8)
                            if nxt is not None:
                                nxt["stl"]()

                    def post():
                        W["tail"](9 if p == 1 else 8)
                        if h % 2 == 1:
                            outproj(p, WOA[j, h - 1:h + 1], gt, gtb, pre=W["wo"])

                    return {"mida": mida, "vz_early": vz_early, "pre": pre, "preq": preq, "stl": stl, "s1": qf_s1, "s1all": s1all, "mid": mid, "post": post}

                heads = [make_head(h) for h in range(8)]
                heads[0]["preq"]()
                heads[0]["pre"]()
                heads[0]["stl"]()
                heads[0]["s1all"]()
                heads[1]["preq"]()
                for h in range(8):
                    heads[h]["mida"]()
                    gen = heads[h]["mid"](heads[h + 1] if h < 7 else None, heads[h + 2] if h < 6 else None)
                    next(gen)
                    if h > 0:
                        heads[h - 1]["post"]()
                    for _ in gen:
                        pass
                heads[7]["post"]()
                ple_pass(p, i)

        def layer_B(i, j, es2):
            def sb2(name, shape, dt=F32):
                return es2.enter_context(nc.sbuf_tensor(name + "_L%d" % i, shape, dt))
            vraw = sb2("vraw", [128, 9, 2048], BF16); vrawb = [k.buf("vraw%d" % t) for t in range(9)]
            vo = [sb2("vo%d" % t, [128, 2048]) for t in range(2)]; vob = [k.buf("vo%d" % t) for t in range(2)]
            st1 = sb2("st1", [128, 9, 4]); st2 = sb2("st2", [128, 9, 4]); stb = [k.buf("st%d" % t) for t in range(9)]
            mr = sb2("mr", [128, 9, 4]); mrb = [k.buf("mr%d" % t) for t in range(9)]
            k.op("dve", lambda: nc.vector.memset(mr[:, :, :], 1.0), writes=mrb)
            gTt = [sb2("gT0", [128, 4, PW], BF16)] * 2; gTb = [k.buf("gT0")] * 2
            Cc = sb2("Cc", [128, 16, 128]); Ccb = k.buf("Cc")
            WmT = sb2("WmT", [128, 8, 128], BF16); WmTb = k.buf("WmT")
            wsf = vo[0][:, 0:1024].rearrange("p (a b) -> p a b", b=128); wsfb = vob[0]
            bsb = vo[0][:, 1024:2048].rearrange("p (a b) -> p a b", b=128); bsbb = vob[0]
            Rr = vo[1][:, 0:1024].rearrange("p (a b) -> p a b", b=128); Rrb = vob[1]
            w00 = sb2("w00", [16, 8]); w00b = k.buf("w00")
            Dg = sb2("Dg", [16, 8, 16], BF16); Dgb = k.buf("Dg")
            junk = sb2("junkB", [128, 512], BF16); junkb = k.buf("junkB")
            k.dma("sp", wsf, wspT[j, :, :, :], writes=[wsfb])
            k.dma("sp", vo[0][:, 1024:2048], bspr[j, :].partition_broadcast(128), writes=[bsbb])
            k.dma("sp", w00[:, :], w00r[j, :].partition_broadcast(16), writes=[w00b])
            def mk():
                return nc.vector.tensor_tensor(out=wsf, in0=wsf, in1=maskB.unsqueeze(1).to_broadcast([128, 8, 128]), op=ALU.mult)
            k.op("dve", mk, reads=[cbuf], writes=[wsfb])
            k.op("dve", lambda: nc.vector.tensor_copy(out=WmT[:, :, :], in_=wsf), reads=[wsfb], writes=[WmTb])
            for half in range(2):
                k.op("pe", lambda half=half: nc.tensor.matmul(pM[:, 0:512], lhsT=onesN, rhs=vo[0][:, half * 512:(half + 1) * 512], start=True, stop=True),
                     reads=[wsfb, cbuf], writes=[pMb])
                k.op("dve", lambda half=half: nc.vector.tensor_scalar(out=vo[1][:, half * 512:(half + 1) * 512], in0=pM[:, 0:512], scalar1=1024.0, scalar2=None, op0=ALU.mult),
                     reads=[pMb], writes=[Rrb])
            def mkC():
                for kt in range(16):
                    ins = nc.vector.scalar_tensor_tensor(out=Cc[:, kt, :], in0=Rr[:, kt // 2, :], scalar=csb[:, C_LNB + j * 16 + kt:C_LNB + j * 16 + kt + 1], in1=bsb[:, kt // 2, :],
                                                         op0=ALU.mult, op1=ALU.add)
                return ins
            k.op("dve", mkC, reads=[Rrb, bsbb, cbuf], writes=[Ccb])
            k.op("dve", lambda: nc.vector.tensor_tensor(out=Dg[:, :, :], in0=w00[:, :].unsqueeze(2).to_broadcast([16, 8, 16]),
                                                       in1=csb[0:16, C_ID:C_ID + 16].unsqueeze(1).to_broadcast([16, 8, 16]),
                                                       op=ALU.mult), reads=[w00b, cbuf], writes=[Dgb])

            for p in range(2):
                norm_pass(p, C_NMIX, i)
                ntile = 9 if p == 1 else 8
                def lnbufs(q4):
                    return (T[0], Tb[0], T[1], Tb[1]) if q4 % 2 == 0 else (T[4], Tb[4], T[5], Tb[5])

                def lnload(q4):
                    tg_, tgb_, tb_, tbb_ = lnbufs(q4)
                    k.dma("sp", tg_[:, :], lnraw[j, 0, q4 * 512:(q4 + 1) * 512].partition_broadcast(128), writes=[tgb_])
                    k.dma("sp", tb_[:, :], lnraw[j, 1, q4 * 512:(q4 + 1) * 512].partition_broadcast(128), writes=[tbb_])
                if p == 1:
                    lnload(0)
                    lnload(1)
                for vc in range(4):
                    wt, wb = wload(WBV[j, vc].rearrange("p a c -> p (a c)"), 4096)
                    wv = wt[:, 0:4096].rearrange("p (a c) -> p a c", c=512)
                    for lt in range(ntile):
                        M = 128 if lt < 8 else 16
                        hb = hnTb[lt // 4]
                        pp, ppb = next_ab()
                        def mv(lt=lt, M=M, pp=pp):
                            for kt in range(8):
                                ins = nc.tensor.matmul(pp[0:M, :], lhsT=hnT[:, kt, lt * 128:lt * 128 + M], rhs=wv[:, kt, :], start=(kt == 0), stop=(kt == 7))
                            return ins
                        k.op("pe", mv, reads=[wb, hb], writes=[ppb])
                        tix = 2 + (lt % 2)
                        k.op("act", lambda M=M, pp=pp, lt=lt, tix=tix: nc.scalar.activation(out=T[tix][0:M, :], in_=pp[0:M, :], func=AF.Gelu_apprx_tanh, accum_out=st1[0:M, lt, vc:vc + 1]),
                             reads=[ppb], writes=[Tb[tix], stb[lt]])
                        k.op("act", lambda M=M, lt=lt, tix=tix: nc.scalar.activation(out=junk[0:M, :], in_=T[tix][0:M, :], func=AF.Square, accum_out=st2[0:M, lt, vc:vc + 1]),
                             reads=[Tb[tix]], writes=[junkb, stb[lt]])
                        k.op("dve", lambda M=M, lt=lt, tix=tix: nc.vector.tensor_copy(out=vraw[0:M, lt, vc * 512:(vc + 1) * 512], in_=T[tix][0:M, :]), reads=[Tb[tix]], writes=[vrawb[lt]])
                        if p == 1 and lt >= 7:
                            oi = lt - 7
                            k.op("dve", lambda M=M, oi=oi, tix=tix: nc.vector.tensor_copy(out=vo[oi][0:M, vc * 512:(vc + 1) * 512], in_=T[tix][0:M, :]), reads=[Tb[tix]], writes=[vob[oi]])
                for lt in range(ntile):
                    M = 128 if lt < 8 else 16
                    def st_a(lt=lt, M=M):
                        nc.vector.tensor_reduce(out=mr[0:M, lt, 0:1], in_=st1[0:M, lt, :], axis=mybir.AxisListType.X, op=ALU.add)
                        return nc.vector.tensor_reduce(out=mr[0:M, lt, 1:2], in_=st2[0:M, lt, :], axis=mybir.AxisListType.X, op=ALU.add)
                    k.op("dve", st_a, reads=[stb[lt]], writes=[mrb[lt]])
                    k.op("dve", lambda lt=lt, M=M: nc.vector.tensor_scalar(out=mr[0:M, lt, 0:2], in0=mr[0:M, lt, 0:2], scalar1=1.0 / 2048.0, scalar2=None, op0=ALU.mult),
                         reads=[mrb[lt]], writes=[mrb[lt]])
                    k.op("dve", lambda lt=lt, M=M: nc.vector.tensor_tensor(out=mr[0:M, lt, 2:3], in0=mr[0:M, lt, 0:1], in1=mr[0:M, lt, 0:1], op=ALU.mult),
                         reads=[mrb[lt]], writes=[mrb[lt]])
                    k.op("dve", lambda lt=lt, M=M: nc.vector.tensor_tensor(out=mr[0:M, lt, 2:3], in0=mr[0:M, lt, 1:2], in1=mr[0:M, lt, 2:3], op=ALU.subtract),
                         reads=[mrb[lt]], writes=[mrb[lt]])
                rsqrt(mr[:, 0:ntile, 3], mr[:, 0:ntile, 2], mrb[0:ntile], mrb[0])
                for lt in range(ntile):
                    M = 128 if lt < 8 else 16
                    k.op("dve", lambda lt=lt, M=M: nc.vector.tensor_scalar(out=vraw[0:M, lt, :], in0=vraw[0:M, lt, :], scalar1=mr[0:M, lt, 0:1], scalar2=mr[0:M, lt, 3:4], op0=ALU.subtract, op1=ALU.mult),
                         reads=[mrb[lt], mrb[0]], writes=[vrawb[lt]])
                    if p == 1 and lt >= 7:
                        oi = lt - 7
                        k.op("dve", lambda lt=lt, M=M, oi=oi: nc.vector.tensor_scalar(out=vo[oi][0:M, :], in0=vo[oi][0:M, :], scalar1=mr[0:M, lt, 0:1], scalar2=mr[0:M, lt, 3:4], op0=ALU.subtract, op1=ALU.mult),
                             reads=[mrb[lt], mrb[0]], writes=[vob[oi]])
                if p == 1:
                    for q4 in range(4):
                        tg_, tgb_, tb_, tbb_ = lnbufs(q4)
                        for oi, M in ((0, 128), (1, 16)):
                            k.op("dve", lambda M=M, oi=oi, q4=q4, tg_=tg_: nc.vector.tensor_tensor(out=vo[oi][0:M, q4 * 512:(q4 + 1) * 512], in0=vo[oi][0:M, q4 * 512:(q4 + 1) * 512], in1=tg_[0:M, :], op=ALU.mult),
                                 reads=[tgb_], writes=[vob[oi]])
                            k.op("dve", lambda M=M, oi=oi, q4=q4, tb_=tb_: nc.vector.tensor_tensor(out=vo[oi][0:M, q4 * 512:(q4 + 1) * 512], in0=vo[oi][0:M, q4 * 512:(q4 + 1) * 512], in1=tb_[0:M, :], op=ALU.add),
                                 reads=[tbb_], writes=[vob[oi]])
                        if q4 + 2 < 4:
                            lnload(q4 + 2)
                    k.dma("sp", cvp[j, :, :], vo[0][:, :], reads=[vob[0]])
                    k.dma("sp", cvs[j, :, :], vo[1][0:16, :], reads=[vob[1]])
                for g in range(8):
                    wt, wb = wload(WBUZ[j, g].rearrange("p a c -> p (a c)"), 4096)
                    wv = wt[:, 0:4096].rearrange("p (a c) -> p a c", c=512)
                    gt, gtb = gTt[(g // 2) % 2], gTb[(g // 2) % 2]
                    gk = 2 * (g % 2)
                    for half in range(2):
                        kt16 = 2 * g + half
                        for (cb, n, bi) in pass_blocks(p):
                            lc = lcol(p, cb)
                            hb = hnTb[hn_idx(p, cb)]
                            rot["n"] += 1
                            r = rot["n"] % 2
                            pu, pub = (pA, pAb) if r == 0 else (pD[0], pDb[0])
                            pz, pzb = (pB, pBb) if r == 0 else (pD[1], pDb[1])
                            tu, tub = (T[4], Tb[4]) if r == 0 else (T[2], Tb[2])
                            tz, tzb = (T[5], Tb[5]) if r == 0 else (T[3], Tb[3])
                            def mu(lc=lc, n=n, o=half * 128, pp=pu):
                                for kt in range(8):
                                    ins = nc.tensor.matmul(pp[:, 0:n], lhsT=wv[:, kt, o:o + 128], rhs=hnT[:, kt, lc:lc + n], start=(kt == 0), stop=(kt == 7))
                                return ins
                            k.op("pe", mu, reads=[wb, hb], writes=[pub])
                            k.op("pe", lambda lc=lc, n=n, half=half: mu(lc, n, 256 + half * 128, pz), reads=[wb, hb], writes=[pzb])
                            k.op("act", lambda n=n: nc.scalar.activation(out=tu[:, 0:n], in_=pu[:, 0:n], func=AF.Gelu_apprx_tanh), reads=[pub], writes=[tub])
                            k.op("act", lambda n=n: nc.scalar.activation(out=tz[:, 0:n], in_=pz[:, 0:n], func=AF.Tanh, scale=0.5), reads=[pzb], writes=[tzb])
                            k.op("dve", lambda n=n: nc.vector.scalar_tensor_tensor(out=tz[:, 0:n], in0=tz[:, 0:n], scalar=1.0, in1=pz[:, 0:n], op0=ALU.add, op1=ALU.mult), reads=[tzb, pzb], writes=[tzb])
                            k.op("dve", lambda n=n, lc=lc, half=half: nc.vector.scalar_tensor_tensor(out=gt[:, gk + half, lc:lc + n], in0=tz[:, 0:n], scalar=0.5, in1=tu[:, 0:n], op0=ALU.mult, op1=ALU.mult),
                                 reads=[tub, tzb], writes=[gtb])
                        for q in range(2):
                            rot["n"] += 1
                            r = rot["n"] % 2
                            psp, pspb = (pAtt, pAttb) if r == 0 else (pO, pOb)
                            tsp, tspb = (T[0], Tb[0]) if r == 0 else (T[1], Tb[1])
                            def msp(q=q, kt16=kt16, psp=psp):
                                for t4 in range(4):
                                    lt = q * 4 + t4
                                    ins = nc.tensor.matmul(psp[:, t4 * 128:(t4 + 1) * 128], lhsT=vraw[:, lt, kt16 * 128:(kt16 + 1) * 128], rhs=WmT[:, g, :], start=True, stop=True)
                                return ins
                            k.op("pe", msp, reads=[vrawb[q * 4 + t] for t in range(4)] + [WmTb], writes=[pspb])
                            lc = q * 512
                            k.op("dve", lambda kt16=kt16, psp=psp, tsp=tsp: nc.vector.scalar_tensor_tensor(out=tsp[:, :].rearrange("p (t x) -> p t x", x=128), in0=psp[:, :].rearrange("p (t x) -> p t x", x=128),
                                                                                       scalar=csb[:, C_LNG + j * 16 + kt16:C_LNG + j * 16 + kt16 + 1],
                                                                                       in1=Cc[:, kt16, :].unsqueeze(1).to_broadcast([128, 4, 128]), op0=ALU.mult, op1=ALU.add),
                                 reads=[pspb, Ccb, cbuf], writes=[tspb])
                            k.op("dve", lambda lc=lc, half=half, tsp=tsp: nc.vector.tensor_tensor(out=gt[:, gk + half, lc:lc + 512], in0=tsp[:, :], in1=gt[:, gk + half, lc:lc + 512], op=ALU.mult),
                                 reads=[tspb], writes=[gtb])
                        if p == 1:
                            k.op("pe", lambda kt16=kt16: nc.tensor.matmul(pM[:, 0:16], lhsT=vraw[0:16, 8, kt16 * 128:(kt16 + 1) * 128], rhs=Dg[:, g, :], start=True, stop=True),
                                 reads=[vrawb[8], Dgb], writes=[pMb])
                            k.op("dve", lambda kt16=kt16: nc.vector.scalar_tensor_tensor(out=T[1][:, 0:16], in0=pM[:, 0:16], scalar=csb[:, C_LNG + j * 16 + kt16:C_LNG + j * 16 + kt16 + 1],
                                                                                       in1=Cc[:, kt16, 0:1].to_broadcast([128, 16]), op0=ALU.mult, op1=ALU.add),
                                 reads=[pMb, Ccb, cbuf], writes=[Tb[1]])
                            k.op("dve", lambda half=half: nc.vector.tensor_tensor(out=gt[:, gk + half, 1024:1040], in0=T[1][:, 0:16], in1=gt[:, gk + half, 1024:1040], op=ALU.mult),
                                 reads=[Tb[1]], writes=[gtb])
                    if g % 2 == 1:
                        outproj(p, WOB[j, g - 1:g + 1], gt, gtb)
                ple_pass(p, i)

        for i in range(depth_run):
            j = i // 2
            with ExitStack() as es2:
                if i % 2 == 0:
                    layer_A(i, j, es2)
                else:
                    layer_B(i, j, es2)
                barrier()

        for (cb, n, bi) in [(0, 512, 0), (512, 512, 1), (1024, 512, 2), (1536, 512, 3), (2048, 16, 4)]:
            for kt in range(8):
                tb, tbb = T[kt % 2], Tb[kt % 2]
                k.op("act", lambda kt=kt, tb=tb, n=n, cb=cb: nc.scalar.activation(out=tb[:, 0:n], in_=hT[:, kt, cb:cb + n], func=AF.Square), reads=[hTb[kt][bi]], writes=[tbb])
                k.op("pe", lambda kt=kt, tb=tb, n=n: nc.tensor.matmul(pM[:, 0:n], lhsT=onesN, rhs=tb[:, 0:n], start=(kt == 0), stop=(kt == 7)), reads=[tbb, cbuf], writes=[pMb])
            rsqrt(rstd[:, 0:n], pM[:, 0:n], [pMb], rstdb)
            for kt in range(8):
                k.op("dve", lambda kt=kt, n=n, cb=cb: nc.vector.scalar_tensor_tensor(out=hT[:, kt, cb:cb + n], in0=hT[:, kt, cb:cb + n], scalar=csb[:, C_NFIN + kt:C_NFIN + kt + 1],
                                                                                  in1=rstd[:, 0:n], op0=ALU.mult, op1=ALU.mult), reads=[rstdb, cbuf], writes=[hTb[kt][bi]])
            k.dma("sp", yT[:, :, cb:cb + n], hT[:, :, cb:cb + n], reads=[hTb[kt][bi] for kt in range(8)])
        k.finish()
    return nc


def _prep(inputs):
    f = lambda a: np.ascontiguousarray(np.asarray(a, dtype=np.float32))
    w_in_a = f(inputs["w_in_a"]).reshape(2, 8, 128, 6144)
    q = w_in_a[..., 0:1024].reshape(2, 8, 128, 8, 128)
    fz = w_in_a[..., 1024:2048].reshape(2, 8, 128, 8, 128)
    v = w_in_a[..., 2048:4096].reshape(2, 8, 128, 8, 256)
    z = w_in_a[..., 4096:6144].reshape(2, 8, 128, 8, 256)
    WAQF = f(np.concatenate([q, fz], axis=-1).transpose(0, 3, 2, 1, 4))
    WAVZ = f(np.concatenate([v, z], axis=-1).transpose(0, 3, 2, 1, 4))
    WOA = f(f(inputs["w_out_a"]).reshape(2, 8, 2, 128, 1024).transpose(0, 1, 3, 2, 4))
    w_in_b = f(inputs["w_in_b"]).reshape(2, 8, 128, 6144)
    WBV = f(w_in_b[..., 2048:4096].reshape(2, 8, 128, 4, 512).transpose(0, 3, 2, 1, 4))
    u = w_in_b[..., 0:2048].reshape(2, 8, 128, 8, 256)
    zb = w_in_b[..., 4096:6144].reshape(2, 8, 128, 8, 256)
    WBUZ = f(np.concatenate([u, zb], axis=-1).transpose(0, 3, 2, 1, 4))
    WOB = f(f(inputs["w_out_b"]).reshape(2, 8, 2, 128, 1024).transpose(0, 1, 3, 2, 4))
    WG = f(f(inputs["w_ple_gate"]).reshape(4, 8, 128, 2, 512).transpose(0, 3, 2, 1, 4))
    WP = f(f(inputs["w_ple_proj"]).reshape(4, 2, 128, 1024).transpose(0, 2, 1, 3))
    cst = np.zeros((128, C_TOT), np.float32)
    cst[:, C_NMIX:C_NMIX + 32] = f(inputs["norm_mix"]).reshape(4, 8, 128).transpose(2, 0, 1).reshape(128, 32)
    cst[:, C_NPLE:C_NPLE + 32] = f(inputs["norm_ple"]).reshape(4, 8, 128).transpose(2, 0, 1).reshape(128, 32)
    cst[:, C_NFIN:C_NFIN + 8] = f(inputs["norm_final"]).reshape(8, 128).T
    cst[:, C_LBL:C_LBL + 16] = f(inputs["lb_logits"]).reshape(2, 8, 128).transpose(2, 0, 1).reshape(128, 16)
    cst[:, C_GN:C_GN + 4] = f(inputs["gnorm_a"]).reshape(2, 2, 128).transpose(2, 0, 1).reshape(128, 4)
    cst[:, C_LNG:C_LNG + 32] = f(inputs["ln_v_g"]).reshape(2, 16, 128).transpose(2, 0, 1).reshape(128, 32)
    cst[:, C_LNB:C_LNB + 32] = f(inputs["ln_v_b"]).reshape(2, 16, 128).transpose(2, 0, 1).reshape(128, 32)
    cst[:, C_ID:C_ID + 128] = np.eye(128, dtype=np.float32)
    s = np.arange(128)[:, None]
    t = np.arange(128)[None, :]
    cst[:, C_MT:C_MT + 128] = ((s <= t) & (s // 64 == t // 64)).astype(np.float32)
    cst[:, C_MB:C_MB + 128] = (s <= t).astype(np.float32)
    sm = np.zeros(512, np.float32)
    sm[::64] = 1.0
    cst[:, C_SM:C_SM + 512] = sm[None, :]
    cst[:, C_ON:C_ON + 128] = 1.0 / 1024.0
    cst[:, C_I16:C_I16 + 256] = np.eye(16, dtype=np.float32).reshape(1, 256)
    lnraw = f(np.stack([f(inputs["ln_v_g"]), f(inputs["ln_v_b"])], axis=1))
    wspT = f(f(inputs["w_spatial"]).transpose(0, 3, 1, 2))
    bspr = f(f(inputs["b_spatial"]).reshape(2, 1024))
    w00r = f(f(inputs["w_spatial"])[:, :, 0, 0])
    shared = dict(cst=cst, lnraw=lnraw, wspT=wspT, bspr=bspr, w00r=w00r, WAQF=WAQF, WAVZ=WAVZ, WOA=WOA, WBV=WBV, WBUZ=WBUZ, WOB=WOB, WG=WG, WP=WP)
    xp = f(inputs["x_prompt"]); xs = f(inputs["x_sample"])
    pp = f(inputs["p_prompt"]); psm = f(inputs["p_sample"])
    st = f(inputs["state_hgrn"])
    maps = []
    for c in range(NCORES):
        xa = np.concatenate([xp[c], xs[16 * c:16 * c + 16, 0]], axis=0)
        xT = f(xa.T.reshape(8, 128, NT).transpose(1, 0, 2))
        pa = np.concatenate([pp[:, c], psm[:, 16 * c:16 * c + 16, 0]], axis=1)
        pT = f(pa.transpose(0, 2, 1).reshape(4, 2, 128, NT).transpose(0, 2, 1, 3))
        sti = f(st[:, 16 * c:16 * c + 16].transpose(0, 2, 3, 1, 4))
        m = dict(shared)
        m.update(xT=xT, pT=pT, st_in=sti)
        maps.append(m)
    return maps


_NC_CACHE = {}


def kernel(**inputs):
    maps = _prep(inputs)
    if "nc" not in _NC_CACHE:
        _NC_CACHE["nc"] = build()
    nc = _NC_CACHE["nc"]
    res = run_bass_kernel_spmd(nc, maps, core_ids=list(range(NCORES)))
    R = res.results
    y_prompt = np.zeros((8, 2048, 1024), np.float32)
    y_sample = np.zeros((128, 1, 1024), np.float32)
    sp = np.zeros((2, 8, 8, 128, 256), np.float32)
    ss = np.zeros((2, 128, 8, 128, 256), np.float32)
    cvp = np.zeros((2, 8, 128, 2048), np.float32)
    cvs = np.zeros((2, 128, 1, 2048), np.float32)
    for c in range(NCORES):
        yT = R[c]["yT"]
        ya = yT.transpose(2, 1, 0).reshape(NT, 1024)
        y_prompt[c] = ya[:2048]
        y_sample[16 * c:16 * c + 16, 0] = ya[2048:]
        sp[:, c] = R[c]["sp_out"]
        ss[:, 16 * c:16 * c + 16] = R[c]["ss_out"].transpose(0, 3, 1, 2, 4)
        cvp[:, c] = R[c]["cvp"]
        cvs[:, 16 * c:16 * c + 16, 0] = R[c]["cvs"]
    return (y_prompt, y_sample, sp, ss, cvp, cvs)
```
